# Optimizing a Trainium2 kernel written in Bass

```python
import math
import jax, jax.numpy as jnp
from jax import lax
import numpy as np

D_MODEL = 2048
BATCH = 32
SEQ = 256
DEPTH = 4
DEC_BATCH = 4
DEC_SEQ = 4096
PAST_LEN = 512

GRID_W = 64
N_MIXERS = 4
GROUP_W = D_MODEL // N_MIXERS
N_IN_COLS = 8 * GROUP_W
CHUNK = 128
A_HEAD_DIM = 128
A_HEADS = GROUP_W // A_HEAD_DIM
CONV_W = 31
C_HEADS = 4
C_DK = GROUP_W // (2 * C_HEADS)
C_DV = 2 * C_DK
ROPE_NF = C_DK // 4
ROPE_THETA = 10000.0
D_GROUPS = 4
D_GW = GROUP_W // D_GROUPS
D_FF = 5632
N_MOD = 9
Q_BLOCK = 128
EPS = 1e-6

kernel_name = 'hybrid_diffusion_prefix_step'


def rms_norm(x, g):
    xf = x.astype(jnp.float32)
    y = xf * lax.rsqrt(jnp.mean(xf * xf, axis=-1, keepdims=True) + EPS)
    return (y * g.astype(jnp.float32)).astype(x.dtype)


def layer_norm(x, g, b):
    xf = x.astype(jnp.float32)
    mu = jnp.mean(xf, axis=-1, keepdims=True)
    d = xf - mu
    y = d * lax.rsqrt(jnp.mean(d * d, axis=-1, keepdims=True) + EPS)
    return (y * g.astype(jnp.float32) + b.astype(jnp.float32)).astype(x.dtype)


def swiglu(h, w_up, w_down):
    a, g = jnp.split(h @ w_up, 2, axis=-1)
    return (jax.nn.silu(g) * a) @ w_down


def modulation(cond, w_mod, b_mod):
    return (jax.nn.silu(cond) @ w_mod + b_mod).reshape(cond.shape[0], N_MOD, D_MODEL)


def axial_rope_tables(n_tokens):
    rows = n_tokens // GRID_W
    row = jnp.repeat(jnp.arange(rows, dtype=jnp.float32), GRID_W)
    col = jnp.tile(jnp.arange(GRID_W, dtype=jnp.float32), rows)
    freqs = ROPE_THETA ** (-jnp.arange(ROPE_NF, dtype=jnp.float32) / ROPE_NF)
    ang_r = row[:, None] * freqs
    ang_c = col[:, None] * freqs
    return (jnp.cos(ang_r), jnp.sin(ang_r), jnp.cos(ang_c), jnp.sin(ang_c))


def _rot(xp, cos, sin):
    x1, x2 = xp[..., :ROPE_NF], xp[..., ROPE_NF:]
    return jnp.concatenate([x1 * cos - x2 * sin, x2 * cos + x1 * sin], axis=-1)


def apply_rope(x, tabs):
    cr, sr, cc, sc = [t.reshape(1, t.shape[0], 1, 1, ROPE_NF).astype(x.dtype) for t in tabs]
    half = C_DK // 2
    return jnp.concatenate([_rot(x[..., :half], cr, sr), _rot(x[..., half:], cc, sc)], axis=-1)


def diff_attention(q, k, v, lam):
    bn, lq = q.shape[0], q.shape[1]
    nb = lq // Q_BLOCK
    qb = q.reshape(bn, nb, Q_BLOCK, C_HEADS, 2, C_DK).transpose(1, 0, 2, 3, 4, 5)
    scale = C_DK ** -0.5

    def block(qi):
        s = jnp.einsum('bqhmd,bkhmd->bhmqk', qi, k).astype(jnp.float32) * scale
        p = jax.nn.softmax(s, axis=-1)
        a = p[:, :, 0] - lam * p[:, :, 1]
        return jnp.einsum('bhqk,bkhd->bqhd', a.astype(v.dtype), v)

    o = lax.map(block, qb)
    return o.transpose(1, 0, 2, 3, 4).reshape(bn, lq, C_HEADS, C_DV)


def token_mixers(h, lw, lambda_init, rope, ctx_k, ctx_v):
    bn, L, _ = h.shape
    proj = h @ lw['w_in']
    a_u, a_v, b_a, b_g, c_q, c_k, c_v, d_x = jnp.split(proj, 8, axis=-1)

    u = jax.nn.gelu(a_u)
    vv = rms_norm(jax.nn.gelu(a_v), lw['a_norm_g'])
    vv = vv.reshape(bn, L // CHUNK, CHUNK, A_HEADS, A_HEAD_DIM)
    sp = jnp.einsum('hpq,bnqhc->bnphc', lw['a_ws'], vv) + jnp.swapaxes(lw['a_bs'], 0, 1)[None, None, :, :, None]
    out_a = u * sp.reshape(bn, L, GROUP_W)

    glu = b_a * jax.nn.sigmoid(b_g)
    conv = lax.conv_general_dilated(
        glu, lw['b_conv_w'][:, None, :], window_strides=(1,),
        padding=[(CONV_W // 2, CONV_W // 2)], dimension_numbers=('NWC', 'WIO', 'NWC'),
        feature_group_count=GROUP_W) + lw['b_conv_b']
    out_b = jax.nn.silu(layer_norm(conv, lw['b_ln_g'], lw['b_ln_b'])) @ lw['b_pw']

    q = rms_norm(c_q.reshape(bn, L, C_HEADS, 2, C_DK), lw['c_qnorm_g'])
    k = rms_norm(c_k.reshape(bn, L, C_HEADS, 2, C_DK), lw['c_knorm_g'])
    v = c_v.reshape(bn, L, C_HEADS, C_DV)
    lp = lw['c_lambda'].astype(jnp.float32)
    lam = jnp.exp(jnp.sum(lp[0] * lp[1])) - jnp.exp(jnp.sum(lp[2] * lp[3])) + lambda_init
    if rope is None:
        keys, vals = k, v
        kv = (k.reshape(bn, L, C_HEADS, 2 * C_DK), v)
    else:
        q = apply_rope(q, rope)
        k = apply_rope(k, rope)
        lc = ctx_k.shape[1]
        keys = jnp.concatenate([ctx_k.reshape(bn, lc, C_HEADS, 2, C_DK).astype(k.dtype), k], axis=1)
        vals = jnp.concatenate([ctx_v.astype(v.dtype), v], axis=1)
        kv = None
    o = diff_attention(q, keys, vals, lam)
    o = rms_norm(o, lw['c_subln_g']) * (1.0 - lambda_init)
    out_c = o.reshape(bn, L, GROUP_W)

    dx = d_x.reshape(bn, L, D_GROUPS, D_GW).astype(jnp.float32)
    f = jnp.fft.fft2(dx, axes=(1, 3), norm='ortho').real.astype(h.dtype)
    out_d = f.reshape(bn, L, GROUP_W) @ lw['d_lin']

    y = jnp.concatenate([out_a, out_b, out_c, out_d], axis=-1) @ lw['w_out']
    return y, kv


def trunk_layer(x, mod, lw, lambda_init, rope, ctx_k, ctx_v):
    sh1, sc1, g1, sh2, sc2, g2, sh3, sc3, g3 = [mod[:, None, i, :] for i in range(N_MOD)]
    h = rms_norm(x, lw['norm_g'][0]) * (1 + sc1) + sh1
    x = x + 0.5 * g1 * swiglu(h, lw['w_ff1_in'], lw['w_ff1_down'])
    h = rms_norm(x, lw['norm_g'][1]) * (1 + sc2) + sh2
    y, kv = token_mixers(h, lw, lambda_init, rope, ctx_k, ctx_v)
    x = x + g2 * y
    h = rms_norm(x, lw['norm_g'][2]) * (1 + sc3) + sh3
    x = x + 0.5 * g3 * swiglu(h, lw['w_ff2_in'], lw['w_ff2_down'])
    return x, kv


def setup_inputs(seed: int = 0) -> dict:
    key = jax.random.key(seed)
    ks = jax.random.split(key, 32)
    f32 = jnp.float32

    def nrm(k, shape, s):
        return jax.random.normal(k, shape, f32) * s

    return {
        'x_prompt': nrm(ks[0], (BATCH, SEQ, D_MODEL), 1.0),
        'x_sample': nrm(ks[1], (DEC_BATCH, DEC_SEQ, D_MODEL), 1.0),
        'c': nrm(ks[2], (DEC_BATCH, D_MODEL), 1.0),
        'cache_k': nrm(ks[3], (DEC_BATCH, DEPTH, PAST_LEN, C_HEADS, 2 * C_DK), 1.0),
        'cache_v': nrm(ks[4], (DEC_BATCH, DEPTH, PAST_LEN, C_HEADS, C_DV), 1.0),
        'c_ctx': nrm(ks[5], (D_MODEL,), 1.0),
        'w_mod': nrm(ks[6], (DEPTH, D_MODEL, N_MOD * D_MODEL), 0.5 * D_MODEL ** -0.5),
        'b_mod': nrm(ks[7], (DEPTH, N_MOD * D_MODEL), 0.02),
        'norm_g': 1.0 + nrm(ks[8], (DEPTH, 3, D_MODEL), 0.02),
        'w_ff1_in': nrm(ks[9], (DEPTH, D_MODEL, 2 * D_FF), D_MODEL ** -0.5),
        'w_ff1_down': nrm(ks[10], (DEPTH, D_FF, D_MODEL), D_FF ** -0.5),
        'w_ff2_in': nrm(ks[11], (DEPTH, D_MODEL, 2 * D_FF), D_MODEL ** -0.5),
        'w_ff2_down': nrm(ks[12], (DEPTH, D_FF, D_MODEL), D_FF ** -0.5),
        'w_in': nrm(ks[13], (DEPTH, D_MODEL, N_IN_COLS), D_MODEL ** -0.5),
        'w_out': nrm(ks[14], (DEPTH, N_MIXERS * GROUP_W, D_MODEL), (N_MIXERS * GROUP_W) ** -0.5),
        'a_norm_g': 1.0 + nrm(ks[15], (DEPTH, GROUP_W), 0.02),
        'a_ws': nrm(ks[16], (DEPTH, A_HEADS, CHUNK, CHUNK), CHUNK ** -0.5),
        'a_bs': nrm(ks[17], (DEPTH, A_HEADS, CHUNK), 0.02),
        'b_conv_w': nrm(ks[18], (DEPTH, CONV_W, GROUP_W), CONV_W ** -0.5),
        'b_conv_b': nrm(ks[19], (DEPTH, GROUP_W), 0.02),
        'b_ln_g': 1.0 + nrm(ks[20], (DEPTH, GROUP_W), 0.02),
        'b_ln_b': nrm(ks[21], (DEPTH, GROUP_W), 0.02),
        'b_pw': nrm(ks[22], (DEPTH, GROUP_W, GROUP_W), GROUP_W ** -0.5),
        'c_qnorm_g': 1.0 + nrm(ks[23], (DEPTH, C_DK), 0.02),
        'c_knorm_g': 1.0 + nrm(ks[24], (DEPTH, C_DK), 0.02),
        'c_lambda': nrm(ks[25], (DEPTH, 4, C_DK), 0.1),
        'c_subln_g': 1.0 + nrm(ks[26], (DEPTH, C_DV), 0.02),
        'd_lin': nrm(ks[27], (DEPTH, GROUP_W, GROUP_W), GROUP_W ** -0.5),
    }


def reference(x_prompt, x_sample, c, cache_k, cache_v, c_ctx, w_mod, b_mod, norm_g,
              w_ff1_in, w_ff1_down, w_ff2_in, w_ff2_down, w_in, w_out,
              a_norm_g, a_ws, a_bs, b_conv_w, b_conv_b, b_ln_g, b_ln_b, b_pw,
              c_qnorm_g, c_knorm_g, c_lambda, c_subln_g, d_lin):
    rope = axial_rope_tables(x_sample.shape[1])
    xp, xs = x_prompt, x_sample
    ks_out, vs_out = [], []
    for l in range(DEPTH):
        lw = {
            'norm_g': norm_g[l], 'w_ff1_in': w_ff1_in[l], 'w_ff1_down': w_ff1_down[l],
            'w_ff2_in': w_ff2_in[l], 'w_ff2_down': w_ff2_down[l],
            'w_in': w_in[l], 'w_out': w_out[l],
            'a_norm_g': a_norm_g[l], 'a_ws': a_ws[l], 'a_bs': a_bs[l],
            'b_conv_w': b_conv_w[l], 'b_conv_b': b_conv_b[l], 'b_ln_g': b_ln_g[l],
            'b_ln_b': b_ln_b[l], 'b_pw': b_pw[l],
            'c_qnorm_g': c_qnorm_g[l], 'c_knorm_g': c_knorm_g[l], 'c_lambda': c_lambda[l],
            'c_subln_g': c_subln_g[l], 'd_lin': d_lin[l],
        }
        lambda_init = 0.8 - 0.6 * math.exp(-0.3 * l)
        ctx_mod = modulation(c_ctx[None, :], w_mod[l], b_mod[l])
        xp, (k_l, v_l) = trunk_layer(xp, ctx_mod, lw, lambda_init, None, None, None)
        ks_out.append(k_l)
        vs_out.append(v_l)
        lat_mod = modulation(c, w_mod[l], b_mod[l])
        xs, _ = trunk_layer(xs, lat_mod, lw, lambda_init, rope, cache_k[:, l], cache_v[:, l])
    new_k = jnp.stack(ks_out, axis=1)
    new_v = jnp.stack(vs_out, axis=1)
    return (xp, xs, new_k, new_v)
```

```python
import math
from contextlib import ExitStack
import numpy as np
import ml_dtypes
import concourse.bass as bass
import concourse.mybir as mybir
from concourse.bass_utils import run_bass_kernel_spmd

F32 = mybir.dt.float32
BF16 = mybir.dt.bfloat16
AF = mybir.ActivationFunctionType
ALU = mybir.AluOpType
AX = mybir.AxisListType

D = 2048
NCH = 16
DFF = 5632
NFC = 44
GRID_W = 64
EPS = 1e-6
TT = 1024
PSEQ = 256
NPS = 4
CONV_W = 31
HALO = 15


class Tk:
    __slots__ = ("name", "w", "rs", "excl")

    def __init__(self, name="", excl=False):
        self.name = name
        self.w = None
        self.rs = []
        self.excl = excl


class FW:
    ENG = ("pe", "act", "dve", "pool", "sp")

    def __init__(self, nc, stack, n_dma_sems=24, same_engine_sync=True):
        self.nc = nc
        self.e = {"pe": nc.tensor, "act": nc.scalar, "dve": nc.vector, "pool": nc.gpsimd, "sp": nc.sync}
        self.sem = {}
        self.cnt = {}
        for k in self.ENG:
            self.sem[k] = stack.enter_context(nc.semaphore("s_" + k))
            self.cnt[k] = 0
        self.dq = {}
        for q in ("sp", "pool"):
            sems = [stack.enter_context(nc.semaphore("d_%s%d" % (q, i))) for i in range(n_dma_sems)]
            self.dq[q] = {"sems": sems, "n": 0}
            for i, s in enumerate(sems):
                self.sem[("d", q, i)] = s
        self.seen = {k: {} for k in self.ENG}
        self.same = same_engine_sync
        self.nwait = 0
        self.nins = 0
        self.stopped = False

    def _deps(self, reads, writes):
        d = {}
        for t in reads:
            if t.w is not None:
                k, v = t.w
                if d.get(k, 0) < v:
                    d[k] = v
            if t.excl:
                for (k, v) in t.rs:
                    if d.get(k, 0) < v:
                        d[k] = v
        for t in writes:
            if t.w is not None:
                k, v = t.w
                if d.get(k, 0) < v:
                    d[k] = v
            for (k, v) in t.rs:
                if d.get(k, 0) < v:
                    d[k] = v
        return d

    def _wait(self, eng, deps):
        seen = self.seen[eng]
        for k, v in deps.items():
            if k == eng and (eng == "pe" or not self.same):
                continue
            if seen.get(k, 0) >= v:
                continue
            self.e[eng].wait_ge(self.sem[k], v)
            self.nwait += 1
            seen[k] = v

    def _commit(self, key, val, reads, writes):
        for t in writes:
            t.w = (key, val)
            t.rs = []
        for t in reads:
            if t.excl:
                t.w = (key, val)
                t.rs = []
                continue
            t.rs.append((key, val))
            if len(t.rs) > 48:
                m = {}
                for (k, v) in t.rs:
                    if m.get(k, 0) < v:
                        m[k] = v
                t.rs = list(m.items())

    def op(self, eng, fn, reads=(), writes=()):
        if self.stopped:
            return
        self._wait(eng, self._deps(reads, writes))
        ins = fn(self.e[eng])
        self.cnt[eng] += 1
        ins.then_inc(self.sem[eng], 1)
        self.nins += 1
        self._commit(eng, self.cnt[eng], reads, writes)

    def pe_group(self, fns, reads=(), writes=()):
        if self.stopped:
            return
        self._wait("pe", self._deps(reads, writes))
        ins = None
        for fn in fns:
            ins = fn(self.e["pe"])
            self.nins += 1
        self.cnt["pe"] += 1
        ins.then_inc(self.sem["pe"], 1)
        self._commit("pe", self.cnt["pe"], reads, writes)

    def dma(self, q, out, in_, reads=(), writes=()):
        if self.stopped:
            return
        dq = self.dq[q]
        n = dq["n"]
        ns = len(dq["sems"])
        slot = n % ns
        rnd = n // ns
        key = ("d", q, slot)
        deps = self._deps(reads, writes)
        if rnd > 0 and deps.get(key, 0) < 16 * rnd:
            deps[key] = 16 * rnd
        self._wait(q, deps)
        ins = self.e[q].dma_start(out=out, in_=in_)
        ins.then_inc(self.sem[key], 16)
        dq["n"] = n + 1
        self.nins += 1
        self._commit(key, 16 * (rnd + 1), reads, writes)

    def barrier(self):
        if self.stopped:
            return
        deps = {}
        for k in self.ENG:
            if self.cnt[k] > 0:
                deps[k] = self.cnt[k]
        for q, dq in self.dq.items():
            ns = len(dq["sems"])
            for slot in range(ns):
                uses = (dq["n"] - slot + ns - 1) // ns if dq["n"] > slot else 0
                if uses > 0:
                    deps[("d", q, slot)] = 16 * uses
        for k in self.ENG:
            self._wait(k, dict(deps))

    def drain_all(self):
        if self.stopped:
            return
        deps = {}
        for k in self.ENG:
            if k != "sp" and self.cnt[k] > 0:
                deps[k] = self.cnt[k]
        for q, dq in self.dq.items():
            ns = len(dq["sems"])
            for slot in range(ns):
                uses = (dq["n"] - slot + ns - 1) // ns if dq["n"] > slot else 0
                if uses > 0:
                    deps[("d", q, slot)] = 16 * uses
        self._wait("sp", deps)


class Ring:
    def __init__(self, bufs):
        self.bufs = bufs
        self.i = 0

    def next(self):
        b = self.bufs[self.i % len(self.bufs)]
        self.i += 1
        return b


class _Stop(Exception):
    pass


class Cfg:
    def __init__(self, L=4, SL=4096, PAST=512, stop=None):
        self.stop = stop
        self.L = L
        self.SL = SL
        self.PAST = PAST
        self.TP = NPS * PSEQ
        self.T = self.TP + SL
        self.NT = self.T // TT
        self.NKC = (PAST + SL) // 128
        self.FPB = 256
        self.NPB = SL // self.FPB
        self.NQC = SL // 128


def lambda_init(l):
    return 0.8 - 0.6 * math.exp(-0.3 * l)


def build_program(cfg):
    L, SL, PAST, T, NT = cfg.L, cfg.SL, cfg.PAST, cfg.T, cfg.NT
    nc = bass.Bass("TRN2", target_bir_lowering=False)

    def din(name, shape, dt=F32):
        return nc.dram_tensor(name, list(shape), dt, kind="ExternalInput").ap()

    def dout(name, shape, dt=F32):
        return nc.dram_tensor(name, list(shape), dt, kind="ExternalOutput").ap()

    def dscr(name, shape, dt):
        kind = "ExternalOutput" if getattr(cfg, "debug", False) else "Internal"
        return nc.dram_tensor(name, list(shape), dt, kind=kind).ap()

    xT_in = din("xT_in", [NCH, 128, T])
    condT = din("condT", [128, NCH, 2])
    w_mod = din("w_mod", [L, 36, 128, 4, NCH, 128])
    b_modT = din("b_modT", [L, 128, 144])
    norm_gT = din("norm_gT", [L, 128, 3, NCH])
    w_up = [din("w_up%d" % i, [L, NFC, 128, 2, NCH, 128]) for i in range(2)]
    w_dn = [din("w_dn%d" % i, [L, NCH, 128, NFC, 128]) for i in range(2)]
    w_inf = din("w_inf", [L, 24, 128, NCH, 128])
    w_int = din("w_int", [L, 2, 128, NCH, 512])
    w_out = din("w_out", [L, NCH, 128, NCH, 128])
    b_pw = din("b_pw", [L, 4, 128, 4, 128])
    d_lin = din("d_lin", [L, 4, 128, 4, 128])
    a_ng = din("a_ng", [L, 128, 512])
    a_wsT = din("a_wsT", [L, 128, 4, 128])
    a_bs = din("a_bs", [L, 128, 512])
    b_cw = din("b_cw", [L, 128, 4, CONV_W])
    b_vec = din("b_vec", [L, 128, 3, 4])
    c_g = din("c_g", [L, 128, 2])
    c_lam = din("c_lam", [L, 128, 256])
    c_sub = din("c_sub", [L, 128, 1])
    cache_kT = din("cache_kT", [L, 4, 128, PAST])
    cache_v = din("cache_v", [L, PAST // 128, 128, 512])
    ropeT = din("ropeT", [128, 2, SL])
    permM = din("permM", [128, 128], BF16)
    onesM = din("onesM", [128, 128], BF16)
    blkM = din("blkM", [128, 128], BF16)
    csC = din("csC", [128, 256], BF16)
    dftP = din("dftP", [128, 2, 2, PSEQ], BF16)
    dftS = din("dftS", [cfg.NPB, 128, 2, cfg.NQC, cfg.FPB], BF16)
    yT = dout("yT", [NCH, 128, T])
    newkT = dout("newkT", [L, 4, 128, cfg.TP])
    newv = dout("newv", [L, cfg.TP // 128, 128, 512])
    XS = [dscr("xs%d" % i, [NCH, 128, T], F32) for i in range(3)]
    catA = dscr("catA", [4, 128, T], BF16)
    gluS2 = [dscr("gluS%d" % i, [4, 128, T], BF16) for i in range(2)]
    qS = dscr("qS", [4, 128, T], BF16)
    kS = dscr("kS", [4, 128, T], BF16)
    vS = dscr("vS", [T // 128, 128, 512], BF16)
    dxS = dscr("dxS", [4, 128, T], BF16)
    catC = dscr("catC", [4, 128, T], BF16)
    fS = dscr("fS", [4, 128, T], BF16)

    _uid = [0]

    def uname(name):
        _uid[0] += 1
        return "%s_%d" % (name, _uid[0])

    def chk(name):
        if cfg.stop == name:
            fw.drain_all()
            fw.stopped = True

    dbg_st2 = dout("dbg_st2", [T // 128, 128, 2]) if getattr(cfg, "debug", False) else None
    dbg_cat = dout("dbg_cat", [NT, 128, NCH, TT], BF16) if getattr(cfg, "debug", False) else None
    dbg_acc = dout("dbg_acc", [NT, 128, 4, TT]) if getattr(cfg, "debug", False) else None
    dbg_ln = dout("dbg_ln", [NT, 2, 128, TT]) if getattr(cfg, "debug", False) else None
    with ExitStack() as st:
      fw = FW(nc, st)
      if True:

        def sb(name, shape, dt):
            return st.enter_context(nc.sbuf_tensor(name, list(shape), dt))

        def tile_tks(prefix, n):
            return [Tk("%s%d" % (prefix, i)) for i in range(n)]

        tk_xin = tile_tks("xin", NT)
        tk_XS = [tile_tks("xs%d_" % i, NT) for i in range(3)]
        tk_yT = tile_tks("y", NT)
        tk_catA = tile_tks("catA", NT)
        tk_glu2 = [tile_tks("glu%d_" % i, NT) for i in range(2)]
        tk_q = tile_tks("q", NT)
        tk_k = tile_tks("k", NT)
        tk_v = tile_tks("v", NT)
        tk_dx = tile_tks("dx", NT)
        tk_catC = tile_tks("catC", NT)
        tk_f = tile_tks("f", NT)
        tk_out_misc = Tk("outmisc")

        ps_bufs = []
        for i in range(4):
            p = st.enter_context(nc.psum_tensor("ps%d" % i, [128, 1024], F32))
            ps_bufs.append((p, Tk("ps%d" % i, excl=True)))
        PS = Ring(ps_bufs)

        ones = sb("ones", [128, 128], BF16); tk_c = Tk("consts")
        blk = sb("blk", [128, 128], BF16)
        perm = sb("perm", [128, 128], BF16)
        csc = sb("csc", [128, 256], BF16)
        dftp = sb("dftp", [128, 2, 2, PSEQ], BF16)
        fw.dma("sp", ones[:], onesM, writes=[tk_c])
        fw.dma("sp", blk[:], blkM, writes=[tk_c])
        fw.dma("sp", perm[:], permM, writes=[tk_c])
        fw.dma("sp", csc[:], csC, writes=[tk_c])
        fw.dma("sp", dftp[:], dftP, writes=[tk_c])

        chk("c0")
        modT = sb("modT", [128, L, 144, 2], F32); tk_mod = Tk("mod")
        gm = sb("gm", [128, L, 3, NCH, 2], F32)
        gate = sb("gate", [128, L, 3, NCH, 2], F32)
        cwt = sb("cwt", [128, L, 4, CONV_W], F32)
        bvt = sb("bvt", [128, L, 3, 4], F32)
        cgt = sb("cgt", [128, L, 2], F32)
        cst = sb("cst", [128, L, 1], F32)
        nlam = sb("nlam", [128, L], F32)
        tk_par = Tk("params")
        for l in range(L):
            fw.dma("sp", cwt[:, l], b_cw[l], writes=[tk_par])
            fw.dma("sp", bvt[:, l], b_vec[l], writes=[tk_par])
            fw.dma("sp", cgt[:, l], c_g[l], writes=[tk_par])
            fw.dma("sp", cst[:, l], c_sub[l], writes=[tk_par])

        with ExitStack() as ps_:
            def sbp(name, shape, dt):
                return ps_.enter_context(nc.sbuf_tensor(uname(name), list(shape), dt))
            ngt = sbp("ngt", [128, L, 3, NCH], F32)
            bmt = sbp("bmt", [128, L, 144], F32)
            clt = sbp("clt", [128, L, 256], F32)
            for l in range(L):
                fw.dma("sp", ngt[:, l], norm_gT[l], writes=[tk_par])
                fw.dma("sp", bmt[:, l], b_modT[l], writes=[tk_par])
                fw.dma("sp", clt[:, l], c_lam[l], writes=[tk_par])
            cnd = sbp("cnd", [128, NCH, 2], F32); tk_cnd = Tk()
            scb = sbp("scb", [128, NCH, 2], BF16); tk_scb = Tk()
            fw.dma("sp", cnd[:], condT, writes=[tk_cnd])
            fw.op("act", lambda e: e.activation(out=scb[:], in_=cnd[:], func=AF.Silu), reads=[tk_cnd], writes=[tk_scb])
            wm_ring = Ring([(sbp("wm%d" % i, [128, 4, NCH, 128], BF16), Tk()) for i in range(3)])
            for l in range(L):
                mp, tk_mp = PS.next()
                mpv = mp[:, 0:288].rearrange("p (n c) -> p n c", c=2)
                for g in range(36):
                    wm, tk_wm = wm_ring.next()
                    fw.dma("pool", wm[:], w_mod[l, g], writes=[tk_wm])
                    for j in range(4):
                        n = g * 4 + j
                        fw.pe_group([
                            (lambda e, kc=kc, j=j, n=n: e.matmul(mpv[:, n, :], lhsT=wm[:, j, kc, :], rhs=scb[:, kc, :],
                                                                  start=(kc == 0), stop=(kc == NCH - 1)))
                            for kc in range(NCH)], reads=[tk_wm, tk_scb], writes=[tk_mp])
                for cidx in range(2):
                    fw.op("dve", lambda e, cidx=cidx: e.tensor_tensor(out=modT[:, l, :, cidx], in0=mpv[:, :, cidx], in1=bmt[:, l, :], op=ALU.add),
                          reads=[tk_mp, tk_par], writes=[tk_mod])
                for i in range(3):
                    for cidx in range(2):
                        fw.op("dve", lambda e, i=i, cidx=cidx: e.scalar_tensor_tensor(
                            out=gm[:, l, i, :, cidx], in0=modT[:, l, (3 * i + 1) * 16:(3 * i + 2) * 16, cidx], scalar=1.0,
                            in1=ngt[:, l, i, :], op0=ALU.add, op1=ALU.mult), reads=[tk_mod, tk_par], writes=[tk_mod])
                        gs = 1.0 if i == 1 else 0.5
                        fw.op("dve", lambda e, i=i, cidx=cidx, gs=gs: e.tensor_scalar(
                            out=gate[:, l, i, :, cidx], in0=modT[:, l, (3 * i + 2) * 16:(3 * i + 3) * 16, cidx], scalar1=gs, scalar2=None,
                            op0=ALU.mult), reads=[tk_mod], writes=[tk_mod])
                lt = sbp("lt%d" % l, [128, 2, 64], F32); tk_lt = Tk()
                ls = sbp("ls%d" % l, [128, 2], F32)
                for r in range(2):
                    fw.op("dve", lambda e, r=r: e.tensor_tensor(out=lt[:, r, :], in0=clt[:, l, (2 * r) * 64:(2 * r + 1) * 64],
                                                                 in1=clt[:, l, (2 * r + 1) * 64:(2 * r + 2) * 64], op=ALU.mult),
                          reads=[tk_par], writes=[tk_lt])
                    fw.op("dve", lambda e, r=r: e.reduce_sum(out=ls[:, r:r + 1], in_=lt[:, r, :], axis=AX.X), reads=[tk_lt], writes=[tk_lt])
                fw.op("act", lambda e: e.activation(out=ls[:], in_=ls[:], func=AF.Exp), reads=[tk_lt], writes=[tk_lt])
                fw.op("dve", lambda e, l=l: e.scalar_tensor_tensor(out=nlam[:, l:l + 1], in0=ls[:, 1:2], scalar=-lambda_init(l),
                                                                    in1=ls[:, 0:1], op0=ALU.add, op1=ALU.subtract),
                      reads=[tk_lt], writes=[tk_mod])

        chk("pro")
        fw.barrier()
        def cond_of(t):
            return 0 if t == 0 else 1

        def x_src(buf_idx):
            if buf_idx == "in":
                return xT_in, tk_xin
            if buf_idx == "out":
                return yT, tk_yT
            return XS[buf_idx], tk_XS[buf_idx]

        def norm_mod(env, src, t, l, i):
            sap, stk = x_src(src)
            cidx = cond_of(t)
            tok = slice(t * TT, (t + 1) * TT)
            hT, tk_h = env["hT"]
            ssq, tk_ssq = PS.next()
            for c in range(NCH):
                xc, tk_xc = env["xring"].next()
                fw.dma("sp", xc[:], sap[c, :, tok], reads=[stk[t]], writes=[tk_xc])
                sq, tk_sq = env["sqring"].next()
                fw.op("act", lambda e: e.activation(out=sq[:], in_=xc[:], func=AF.Square), reads=[tk_xc], writes=[tk_sq])
                fw.pe_group([(lambda e, h=h: e.matmul(ssq[:, h * 512:(h + 1) * 512], lhsT=ones[:], rhs=sq[:, h * 512:(h + 1) * 512],
                                                        start=(c == 0), stop=(c == NCH - 1))) for h in range(2)],
                            reads=[tk_sq, tk_c], writes=[tk_ssq])
            rstd, tk_rstd = env["rstd"]
            fw.op("act", lambda e: e.activation(out=rstd[:], in_=ssq[:], func=AF.Sqrt, bias=EPS, scale=1.0 / D), reads=[tk_ssq], writes=[tk_rstd])
            fw.op("dve", lambda e: e.reciprocal(out=rstd[:], in_=rstd[:]), reads=[tk_rstd], writes=[tk_rstd])
            for c in range(NCH):
                xc, tk_xc = env["xring"].next()
                fw.dma("sp", xc[:], sap[c, :, tok], reads=[stk[t]], writes=[tk_xc])
                tb, tk_tb = env["tring"].next()
                fw.op("dve", lambda e: e.scalar_tensor_tensor(out=tb[:], in0=xc[:], scalar=gm[:, l, i, c, cidx:cidx + 1], in1=rstd[:],
                                                              op0=ALU.mult, op1=ALU.mult), reads=[tk_xc, tk_rstd, tk_mod], writes=[tk_tb])
                sh = modT[:, l, (3 * i) * 16 + c, cidx:cidx + 1]
                fw.op("act", lambda e: e.activation(out=hT[:, c, :], in_=tb[:], func=AF.Identity, bias=sh, scale=1.0),
                      reads=[tk_tb, tk_mod], writes=[tk_h])

        def ffn(env, src, dst, t, l, which):
            cidx = cond_of(t)
            tok = slice(t * TT, (t + 1) * TT)
            hT, tk_h = env["hT"]
            mid, tk_mid = env["mid"]
            gi = 0 if which == 0 else 2
            for j in range(NFC):
                wu, tk_wu = env["wu"].next()
                fw.dma("pool", wu[:], w_up[which][l, j], writes=[tk_wu])
                pa, tk_pa = PS.next()
                pg, tk_pg = PS.next()
                for (ag, pp, tkp) in ((0, pa, tk_pa), (1, pg, tk_pg)):
                    for h in range(2):
                        fw.pe_group([(lambda e, kc=kc, ag=ag, pp=pp, h=h: e.matmul(
                            pp[:, h * 512:(h + 1) * 512], lhsT=wu[:, ag, kc, :], rhs=hT[:, kc, h * 512:(h + 1) * 512],
                            start=(kc == 0), stop=(kc == NCH - 1))) for kc in range(NCH)],
                            reads=[tk_wu, tk_h], writes=[tkp])
                sg, tk_sg = env["sgring"].next()
                fw.op("act", lambda e: e.activation(out=sg[:], in_=pg[:], func=AF.Silu), reads=[tk_pg], writes=[tk_sg])
                fw.op("dve", lambda e: e.tensor_tensor(out=mid[:, j, :], in0=pa[:], in1=sg[:], op=ALU.mult),
                      reads=[tk_pa, tk_sg], writes=[tk_mid])
            return residual_proj(env, src, dst, t, l, gi, w_dn[which], mid, tk_mid, NFC, env["wd"], xb=hT, tk_xb=tk_h)

        def residual_proj(env, src, dst, t, l, gi, wdram, act, tk_act, nk, wring, xb=None, tk_xb=None):
            cidx = cond_of(t)
            tok = slice(t * TT, (t + 1) * TT)
            sap, stk = x_src(src)
            dap, dtk = x_src(dst)
            carry = xb is not None
            ring3 = Ring(ps_bufs[0:3]) if carry else PS
            ssq, tk_ssq = ps_bufs[3]
            for c in range(NCH):
                wd, tk_wd = wring.next()
                fw.dma("pool", wd[:, 0:nk, :], wdram[l, c], writes=[tk_wd])
                pp, tk_pp = ring3.next()
                for h in range(2):
                    fw.pe_group([(lambda e, fc=fc, h=h: e.matmul(pp[:, h * 512:(h + 1) * 512], lhsT=wd[:, fc, :],
                                                                  rhs=act[:, fc, h * 512:(h + 1) * 512],
                                                                  start=(fc == 0), stop=(fc == nk - 1))) for fc in range(nk)],
                                reads=[tk_wd, tk_act], writes=[tk_pp])
                xc, tk_xc = env["xring"].next()
                fw.dma("sp", xc[:], sap[c, :, tok], reads=[stk[t]], writes=[tk_xc])
                xo, tk_xo = env["tring"].next()
                fw.op("dve", lambda e: e.scalar_tensor_tensor(out=xo[:], in0=pp[:], scalar=gate[:, l, gi, c, cidx:cidx + 1], in1=xc[:],
                                                              op0=ALU.mult, op1=ALU.add), reads=[tk_pp, tk_xc, tk_mod], writes=[tk_xo])
                fw.dma("sp", dap[c, :, tok], xo[:], reads=[tk_xo], writes=[dtk[t]])
                if carry:
                    sq, tk_sq = env["sqring"].next()
                    fw.op("act", lambda e: e.activation(out=sq[:], in_=xo[:], func=AF.Square), reads=[tk_xo], writes=[tk_sq])
                    fw.op("act", lambda e: e.activation(out=xb[:, c, :], in_=xo[:], func=AF.Copy), reads=[tk_xo], writes=[tk_xb])
                    fw.pe_group([(lambda e, h=h: e.matmul(ssq[:, h * 512:(h + 1) * 512], lhsT=ones[:], rhs=sq[:, h * 512:(h + 1) * 512],
                                                            start=(c == 0), stop=(c == NCH - 1))) for h in range(2)],
                                reads=[tk_sq, tk_c], writes=[tk_ssq])
            return (ssq, tk_ssq) if carry else None

        def norm_sb(env, xb, tk_xb, ssq_pair, t, l, i):
            ssq, tk_ssq = ssq_pair
            cidx = cond_of(t)
            hT, tk_h = env["hT"]
            rstd, tk_rstd = env["rstd"]
            fw.op("act", lambda e: e.activation(out=rstd[:], in_=ssq[:], func=AF.Sqrt, bias=EPS, scale=1.0 / D), reads=[tk_ssq], writes=[tk_rstd])
            fw.op("dve", lambda e: e.reciprocal(out=rstd[:], in_=rstd[:]), reads=[tk_rstd], writes=[tk_rstd])
            for c in range(NCH):
                tb, tk_tb = env["tring"].next()
                fw.op("dve", lambda e: e.scalar_tensor_tensor(out=tb[:], in0=xb[:, c, :], scalar=gm[:, l, i, c, cidx:cidx + 1], in1=rstd[:],
                                                              op0=ALU.mult, op1=ALU.mult), reads=[tk_xb, tk_rstd, tk_mod], writes=[tk_tb])
                sh = modT[:, l, (3 * i) * 16 + c, cidx:cidx + 1]
                wr = [tk_h] if tk_h is not tk_xb else [tk_h]
                fw.op("act", lambda e: e.activation(out=hT[:, c, :], in_=tb[:], func=AF.Identity, bias=sh, scale=1.0),
                      reads=[tk_tb, tk_mod], writes=wr)

        def in_proj(env, t, l):
            tok = slice(t * TT, (t + 1) * TT)
            is_prompt = (t == 0)
            hT, tk_h = env["hT"]
            uT, tk_u = env["uT"]
            stg, tk_stg = env["stage"]

            def proj_chunk(n):
                wt, tk_wt = env["wi"].next()
                fw.dma("pool", wt[:], w_inf[l, n], writes=[tk_wt])
                pp, tk_pp = PS.next()
                for h in range(2):
                    fw.pe_group([(lambda e, kc=kc, h=h: e.matmul(pp[:, h * 512:(h + 1) * 512], lhsT=wt[:, kc, :],
                                                                  rhs=hT[:, kc, h * 512:(h + 1) * 512],
                                                                  start=(kc == 0), stop=(kc == NCH - 1))) for kc in range(NCH)],
                                reads=[tk_wt, tk_h], writes=[tk_pp])
                return pp, tk_pp

            for j in range(4):
                pp, tk_pp = proj_chunk(j)
                fw.op("act", lambda e: e.activation(out=uT[:, j, :], in_=pp[:], func=AF.Gelu_apprx_tanh), reads=[tk_pp], writes=[tk_u])
            chk("ipA1")
            wav, tk_wav = env["wtok"].next()
            fw.dma("pool", wav[:], w_int[l, 0], writes=[tk_wav])
            ang, tk_ang = env["ang"]
            fw.dma("sp", ang[:], a_ng[l], writes=[tk_ang])
            wsT, tk_ws = env["wsT"]
            fw.dma("pool", wsT[:], a_wsT[l], writes=[tk_ws])
            bsr, tk_bs = env["bsr"]
            fw.dma("pool", bsr[:], a_bs[l], writes=[tk_bs])
            for tb in range(TT // 128):
                tsl = slice(tb * 128, (tb + 1) * 128)
                pp, tk_pp = PS.next()
                fw.pe_group([(lambda e, kc=kc: e.matmul(pp[:, 0:512], lhsT=hT[:, kc, tsl], rhs=wav[:, kc, :],
                                                         start=(kc == 0), stop=(kc == NCH - 1))) for kc in range(NCH)],
                            reads=[tk_wav, tk_h], writes=[tk_pp])
                gv, tk_gv = env["gv"].next()
                fw.op("act", lambda e: e.activation(out=gv[:], in_=pp[:, 0:512], func=AF.Gelu_apprx_tanh), reads=[tk_pp], writes=[tk_gv])
                jk, tk_jk = env["junk"]
                st2, tk_st2 = env["st2"].next()
                fw.op("act", lambda e: e.activation(out=jk[:, 0:512], in_=gv[:], func=AF.Square, accum_out=st2[:, 0:1]),
                      reads=[tk_gv], writes=[tk_jk, tk_st2])
                fw.op("act", lambda e: e.activation(out=st2[:, 1:2], in_=st2[:, 0:1], func=AF.Sqrt, bias=EPS, scale=1.0 / 512),
                      reads=[tk_st2], writes=[tk_st2])
                fw.op("dve", lambda e: e.reciprocal(out=st2[:, 1:2], in_=st2[:, 1:2]), reads=[tk_st2], writes=[tk_st2])
                if dbg_st2 is not None:
                    fw.dma("sp", dbg_st2[t * 8 + tb], st2[:], reads=[tk_st2], writes=[tk_out_misc])
                vn, tk_vn = env["vn"].next()
                fw.op("dve", lambda e: e.scalar_tensor_tensor(out=vn[:], in0=gv[:], scalar=st2[:, 1:2], in1=ang[:],
                                                              op0=ALU.mult, op1=ALU.mult), reads=[tk_gv, tk_st2, tk_ang], writes=[tk_vn])
                p2 = pp[:, 512:1024]
                for hh in range(4):
                    fw.pe_group([
                        (lambda e, hh=hh: e.matmul(p2[:, hh * 128:(hh + 1) * 128], lhsT=vn[:, hh * 128:(hh + 1) * 128], rhs=wsT[:, hh, :],
                                                   start=True, stop=False)),
                        (lambda e, hh=hh: e.matmul(p2[:, hh * 128:(hh + 1) * 128], lhsT=ones[:, :], rhs=bsr[:, hh * 128:(hh + 1) * 128],
                                                   start=False, stop=True)),
                    ], reads=[tk_vn, tk_ws, tk_bs, tk_c], writes=[tk_pp])
                fw.op("dve", lambda e: e.tensor_tensor(out=stg[:, :, tsl], in0=uT[:, :, tsl],
                                                       in1=p2.rearrange("p (h q) -> p h q", h=4), op=ALU.mult),
                      reads=[tk_pp, tk_u], writes=[tk_stg])
            for j in range(4):
                fw.dma("sp", catA[j, :, tok], stg[:, j, :], reads=[tk_stg], writes=[tk_catA[t]])
            chk("ipA2")
            for j in range(4):
                pa, tk_pa = proj_chunk(4 + j)
                pg, tk_pg = proj_chunk(8 + j)
                sg, tk_sg = env["sgring"].next()
                fw.op("act", lambda e: e.activation(out=sg[:], in_=pg[:], func=AF.Sigmoid), reads=[tk_pg], writes=[tk_sg])
                ob, tk_ob = env["obring"].next()
                fw.op("dve", lambda e: e.tensor_tensor(out=ob[:], in0=pa[:], in1=sg[:], op=ALU.mult), reads=[tk_pa, tk_sg], writes=[tk_ob])
                fw.dma("sp", gluS2[l % 2][j, :, tok], ob[:], reads=[tk_ob], writes=[tk_glu2[l % 2][t]])
            chk("ipB")
            if not is_prompt:
                rp, tk_rp = env["rope"]
                s0 = t * TT - cfg.TP
                fw.dma("sp", rp[:], ropeT[:, :, s0:s0 + TT], writes=[tk_rp])
            for qk in range(2):
                for hh in range(4):
                    pp, tk_pp = proj_chunk(12 + qk * 4 + hh)
                    sq, tk_sq = env["sqring"].next()
                    fw.op("act", lambda e: e.activation(out=sq[:], in_=pp[:], func=AF.Square), reads=[tk_pp], writes=[tk_sq])
                    p2, tk_p2 = PS.next()
                    fw.pe_group([(lambda e, h=h: e.matmul(p2[:, h * 512:(h + 1) * 512], lhsT=blk[:], rhs=sq[:, h * 512:(h + 1) * 512],
                                                            start=True, stop=True)) for h in range(2)], reads=[tk_sq, tk_c], writes=[tk_p2])
                    rs, tk_rs = env["tring"].next()
                    fw.op("act", lambda e: e.activation(out=rs[:], in_=p2[:], func=AF.Sqrt, bias=EPS, scale=1.0 / 64), reads=[tk_p2], writes=[tk_rs])
                    fw.op("dve", lambda e: e.reciprocal(out=rs[:], in_=rs[:]), reads=[tk_rs], writes=[tk_rs])
                    qn, tk_qn = env["tring"].next()
                    fw.op("dve", lambda e: e.scalar_tensor_tensor(out=qn[:], in0=pp[:], scalar=cgt[:, l, qk:qk + 1], in1=rs[:],
                                                                  op0=ALU.mult, op1=ALU.mult), reads=[tk_pp, tk_rs, tk_par], writes=[tk_qn])
                    ob, tk_ob = env["obring"].next()
                    dstS, dtk = (qS, tk_q) if qk == 0 else (kS, tk_k)
                    if is_prompt:
                        fw.op("act", lambda e: e.activation(out=ob[:], in_=qn[:], func=AF.Copy), reads=[tk_qn], writes=[tk_ob])
                        if qk == 1:
                            fw.dma("sp", newkT[l, hh], qn[:], reads=[tk_qn], writes=[tk_out_misc])
                    else:
                        qb_, tk_qb = env["obring"].next()
                        fw.op("act", lambda e: e.activation(out=qb_[:], in_=qn[:], func=AF.Copy), reads=[tk_qn], writes=[tk_qb])
                        p3, tk_p3 = PS.next()
                        fw.pe_group([(lambda e, h=h: e.matmul(p3[:, h * 512:(h + 1) * 512], lhsT=perm[:], rhs=qb_[:, h * 512:(h + 1) * 512],
                                                                start=True, stop=True)) for h in range(2)], reads=[tk_qb, tk_c], writes=[tk_p3])
                        t1, tk_t1 = env["tring"].next()
                        fw.op("dve", lambda e: e.tensor_tensor(out=t1[:], in0=qn[:], in1=rp[:, 0, :], op=ALU.mult), reads=[tk_qn, tk_rp], writes=[tk_t1])
                        t2, tk_t2 = env["tring"].next()
                        fw.op("dve", lambda e: e.tensor_tensor(out=t2[:], in0=p3[:], in1=rp[:, 1, :], op=ALU.mult), reads=[tk_p3, tk_rp], writes=[tk_t2])
                        fw.op("dve", lambda e: e.tensor_tensor(out=ob[:], in0=t1[:], in1=t2[:], op=ALU.add), reads=[tk_t1, tk_t2], writes=[tk_ob])
                    fw.dma("sp", dstS[hh, :, tok], ob[:], reads=[tk_ob], writes=[dtk[t]])
            chk("ipC")
            wcv, tk_wcv = env["wtok"].next()
            fw.dma("pool", wcv[:], w_int[l, 1], writes=[tk_wcv])
            for tb in range(TT // 128):
                tsl = slice(tb * 128, (tb + 1) * 128)
                pp, tk_pp = PS.next()
                fw.pe_group([(lambda e, kc=kc: e.matmul(pp[:, 0:512], lhsT=hT[:, kc, tsl], rhs=wcv[:, kc, :],
                                                         start=(kc == 0), stop=(kc == NCH - 1))) for kc in range(NCH)],
                            reads=[tk_wcv, tk_h], writes=[tk_pp])
                vn, tk_vn = env["vn"].next()
                fw.op("act", lambda e: e.activation(out=vn[:], in_=pp[:, 0:512], func=AF.Copy), reads=[tk_pp], writes=[tk_vn])
                fw.dma("sp", vS[t * (TT // 128) + tb], vn[:], reads=[tk_vn], writes=[tk_v[t]])
                if is_prompt:
                    gv, tk_gv = env["gv"].next()
                    fw.op("dve", lambda e: e.tensor_copy(out=gv[:], in_=pp[:, 0:512]), reads=[tk_pp], writes=[tk_gv])
                    fw.dma("sp", newv[l, tb], gv[:], reads=[tk_gv], writes=[tk_out_misc])
            chk("ipV")
            for j in range(4):
                pp, tk_pp = proj_chunk(20 + j)
                ob, tk_ob = env["obring"].next()
                fw.op("act", lambda e: e.activation(out=ob[:], in_=pp[:], func=AF.Copy), reads=[tk_pp], writes=[tk_ob])
                fw.dma("sp", dxS[j, :, tok], ob[:], reads=[tk_ob], writes=[tk_dx[t]])

        def out_mix(env, t, l):
            tok = slice(t * TT, (t + 1) * TT)
            is_prompt = (t == 0)
            catT, tk_cat = env["hT"]
            gluS = gluS2[l % 2]
            tk_glu = tk_glu2[l % 2]
            for j in range(4):
                fw.dma("sp", catT[:, j, :], catA[j, :, tok], reads=[tk_catA[t]], writes=[tk_cat])
                fw.dma("sp", catT[:, 8 + j, :], catC[j, :, tok], reads=[tk_catC[t]], writes=[tk_cat])
            fT, tk_fT = env["stage"]
            for j in range(4):
                fw.dma("sp", fT[:, j, :], fS[j, :, tok], reads=[tk_f[t]], writes=[tk_fT])
            for n in range(4):
                wt, tk_wt = env["wsm"].next()
                fw.dma("pool", wt[:], d_lin[l, n], writes=[tk_wt])
                pp, tk_pp = PS.next()
                for h in range(2):
                    fw.pe_group([(lambda e, kc=kc, h=h: e.matmul(pp[:, h * 512:(h + 1) * 512], lhsT=wt[:, kc, :],
                                                                  rhs=fT[:, kc, h * 512:(h + 1) * 512],
                                                                  start=(kc == 0), stop=(kc == 3))) for kc in range(4)],
                                reads=[tk_wt, tk_fT], writes=[tk_pp])
                fw.op("act", lambda e: e.activation(out=catT[:, 12 + n, :], in_=pp[:], func=AF.Copy), reads=[tk_pp], writes=[tk_cat])
            nseg, seglen = (NPS, PSEQ) if is_prompt else (1, TT)
            G, tk_G = env["G"]
            acc, tk_acc = env["acc"]
            Gv = G[:, 0:nseg * (seglen + 2 * HALO)].rearrange("p (s w) -> p s w", s=nseg)
            s1, tk_s1 = PS.next()
            s2, tk_s2 = PS.next()
            for j in range(4):
                fw.op("dve", lambda e: e.memset(G[:], 0.0), writes=[tk_G])
                if is_prompt:
                    for s in range(NPS):
                        fw.dma("sp", Gv[:, s, HALO:HALO + PSEQ], gluS[j, :, s * PSEQ:(s + 1) * PSEQ], reads=[tk_glu[t]], writes=[tk_G])
                else:
                    lo = t * TT - HALO
                    hi = (t + 1) * TT + HALO
                    lo_c = max(lo, cfg.TP)
                    hi_c = min(hi, T)
                    rd = [tk_glu[t]]
                    if t - 1 >= 1:
                        rd.append(tk_glu[t - 1])
                    if t + 1 < NT:
                        rd.append(tk_glu[t + 1])
                    fw.dma("sp", Gv[:, 0, lo_c - lo:hi_c - lo], gluS[j, :, lo_c:hi_c], reads=rd, writes=[tk_G])
                av = acc[:, j, :].rearrange("p (s w) -> p s w", s=nseg)
                fw.op("dve", lambda e: e.tensor_scalar(out=av, in0=Gv[:, :, 0:seglen], scalar1=cwt[:, l, j, 0:1], scalar2=bvt[:, l, 0, j:j + 1],
                                                       op0=ALU.mult, op1=ALU.add), reads=[tk_G, tk_par], writes=[tk_acc])
                for k in range(1, CONV_W):
                    fw.op("dve", lambda e, k=k: e.scalar_tensor_tensor(out=av, in0=Gv[:, :, k:k + seglen], scalar=cwt[:, l, j, k:k + 1], in1=av,
                                                                        op0=ALU.mult, op1=ALU.add), reads=[tk_G, tk_par], writes=[tk_acc])
                cb, tk_cb = env["obring"].next()
                fw.op("act", lambda e: e.activation(out=cb[:], in_=acc[:, j, :], func=AF.Copy), reads=[tk_acc], writes=[tk_cb])
                sq, tk_sq = env["sqring"].next()
                fw.op("act", lambda e: e.activation(out=sq[:], in_=acc[:, j, :], func=AF.Square), reads=[tk_acc], writes=[tk_sq])
                fw.pe_group([(lambda e, h=h: e.matmul(s1[:, h * 512:(h + 1) * 512], lhsT=ones[:], rhs=cb[:, h * 512:(h + 1) * 512],
                                                        start=(j == 0), stop=(j == 3))) for h in range(2)], reads=[tk_cb, tk_c], writes=[tk_s1])
                fw.pe_group([(lambda e, h=h: e.matmul(s2[:, h * 512:(h + 1) * 512], lhsT=ones[:], rhs=sq[:, h * 512:(h + 1) * 512],
                                                        start=(j == 0), stop=(j == 3))) for h in range(2)], reads=[tk_sq, tk_c], writes=[tk_s2])
            mean, tk_mean = env["lnmean"]
            fw.op("act", lambda e: e.activation(out=mean[:], in_=s1[:], func=AF.Copy, scale=1.0 / 512), reads=[tk_s1], writes=[tk_mean])
            msq, tk_msq = env["lnrstd"]
            fw.op("dve", lambda e: e.tensor_tensor(out=msq[:], in0=mean[:], in1=mean[:], op=ALU.mult), reads=[tk_mean], writes=[tk_msq])
            fw.op("dve", lambda e: e.scalar_tensor_tensor(out=msq[:], in0=s2[:], scalar=1.0 / 512, in1=msq[:], op0=ALU.mult, op1=ALU.subtract),
                  reads=[tk_s2, tk_msq], writes=[tk_msq])
            fw.op("act", lambda e: e.activation(out=msq[:], in_=msq[:], func=AF.Sqrt, bias=EPS, scale=1.0), reads=[tk_msq], writes=[tk_msq])
            fw.op("dve", lambda e: e.reciprocal(out=msq[:], in_=msq[:]), reads=[tk_msq], writes=[tk_msq])
            if dbg_acc is not None:
                fw.dma("sp", dbg_acc[t], acc[:], reads=[tk_acc], writes=[tk_out_misc])
                fw.dma("sp", dbg_ln[t, 0], mean[:], reads=[tk_mean], writes=[tk_out_misc])
                fw.dma("sp", dbg_ln[t, 1], msq[:], reads=[tk_msq], writes=[tk_out_misc])
            yb, tk_yb = env["stage"]
            for j in range(4):
                dd, tk_dd = env["tring"].next()
                fw.op("dve", lambda e: e.tensor_tensor(out=dd[:], in0=acc[:, j, :], in1=mean[:], op=ALU.subtract), reads=[tk_acc, tk_mean], writes=[tk_dd])
                fw.op("dve", lambda e: e.tensor_tensor(out=dd[:], in0=dd[:], in1=msq[:], op=ALU.mult), reads=[tk_dd, tk_msq], writes=[tk_dd])
                fw.op("act", lambda e: e.activation(out=yb[:, j, :], in_=dd[:], func=AF.Silu, bias=bvt[:, l, 2, j:j + 1], scale=bvt[:, l, 1, j:j + 1]),
                      reads=[tk_dd, tk_par], writes=[tk_yb])
            for n in range(4):
                wt, tk_wt = env["wsm"].next()
                fw.dma("pool", wt[:], b_pw[l, n], writes=[tk_wt])
                pp, tk_pp = PS.next()
                for h in range(2):
                    fw.pe_group([(lambda e, kc=kc, h=h: e.matmul(pp[:, h * 512:(h + 1) * 512], lhsT=wt[:, kc, :],
                                                                  rhs=yb[:, kc, h * 512:(h + 1) * 512],
                                                                  start=(kc == 0), stop=(kc == 3))) for kc in range(4)],
                                reads=[tk_wt, tk_yb], writes=[tk_pp])
                fw.op("act", lambda e: e.activation(out=catT[:, 4 + n, :], in_=pp[:], func=AF.Copy), reads=[tk_pp], writes=[tk_cat])

        def attention(l):
            with ExitStack() as es:
                def sbm(name, shape, dt):
                    return es.enter_context(nc.sbuf_tensor(uname(name), list(shape), dt))
                nkc_max = cfg.NKC
                Vt = sbm("Vt", [128, nkc_max, 512], BF16); tk_V = Tk()
                KT = sbm("KT", [128, nkc_max * 128], BF16); tk_K = Tk()
                QT = sbm("QT", [128, max(SL, PSEQ)], BF16); tk_Q = Tk()
                pring = Ring([(sbm("pT%d" % i, [128, 2, 512], BF16), Tk()) for i in range(3)])
                rr = sbm("rr", [128, 2, 512], F32); tk_rr = Tk()
                oo = sbm("oo", [128, 2, 512], F32); tk_oo = Tk()
                od = sbm("od", [128, 512], F32); tk_od = Tk()
                sqb = sbm("sqb", [128, 512], BF16); tk_sqb = Tk()
                rs = sbm("rs", [128, 512], F32); tk_rs = Tk()
                ocr = Ring([(sbm("oc%d" % i, [128, 512], BF16), Tk()) for i in range(2)])
                csc_l = sbm("cscl", [128, 1], F32); tk_cs = Tk()
                fw.op("dve", lambda e: e.tensor_scalar(out=csc_l[:], in0=cst[:, l, :], scalar1=(1.0 - lambda_init(l)), scalar2=None, op0=ALU.mult),
                      reads=[tk_par], writes=[tk_cs])
                seqs = [(s * PSEQ, PSEQ, False, [0]) for s in range(NPS)]
                seqs.append((cfg.TP, SL, True, list(range(1, NT))))
                for (tok0, Lq, has_ctx, tl) in seqs:
                    nctx = PAST if has_ctx else 0
                    nk = nctx + Lq
                    nkc = nk // 128
                    QB = min(512, Lq)
                    rd_v = [tk_v[t] for t in tl]
                    rd_q = [tk_q[t] for t in tl]
                    rd_k = [tk_k[t] for t in tl]
                    if has_ctx:
                        for kc in range(nctx // 128):
                            fw.dma("pool", Vt[:, kc, :], cache_v[l, kc], writes=[tk_V])
                    for kc in range(Lq // 128):
                        fw.dma("sp", Vt[:, nctx // 128 + kc, :], vS[tok0 // 128 + kc], reads=rd_v, writes=[tk_V])
                    for hh in range(4):
                        if has_ctx:
                            fw.dma("pool", KT[:, 0:nctx], cache_kT[l, hh], writes=[tk_K])
                        fw.dma("sp", KT[:, nctx:nk], kS[hh, :, tok0:tok0 + Lq], reads=rd_k, writes=[tk_K])
                        fw.dma("sp", QT[:, 0:Lq], qS[hh, :, tok0:tok0 + Lq], reads=rd_q, writes=[tk_Q])
                        for qb in range(Lq // QB):
                            qsl = slice(qb * QB, (qb + 1) * QB)
                            Op, tk_O = ps_bufs[0]
                            Sp, tk_S = ps_bufs[1]
                            for kc in range(nkc):
                                ksl = slice(kc * 128, (kc + 1) * 128)
                                stp, tk_st = ps_bufs[2 + (kc % 2)]
                                fw.pe_group([(lambda e, m=m: e.matmul(stp[:, m * 512:m * 512 + QB], lhsT=KT[64 * m:64 * m + 64, ksl],
                                                                      rhs=QT[64 * m:64 * m + 64, qsl], start=True, stop=True)) for m in range(2)],
                                            reads=[tk_K, tk_Q], writes=[tk_st])
                                pT, tk_pT = pring.next()
                                stv = stp[:].rearrange("p (m q) -> p m q", m=2)[:, :, 0:QB]
                                fw.op("act", lambda e, pT=pT: e.activation(out=pT[:, :, 0:QB], in_=stv, func=AF.Exp, scale=0.125),
                                      reads=[tk_st], writes=[tk_pT])
                                fns = []
                                for m in range(2):
                                    fns.append(lambda e, m=m, pT=pT: e.matmul(Op[:, m * 512:m * 512 + QB], lhsT=Vt[:, kc, hh * 128:(hh + 1) * 128],
                                                                              rhs=pT[:, m, 0:QB], start=(kc == 0), stop=(kc == nkc - 1)))
                                    fns.append(lambda e, m=m, pT=pT: e.matmul(Sp[:, m * 512:m * 512 + QB], lhsT=ones[:], rhs=pT[:, m, 0:QB],
                                                                              start=(kc == 0), stop=(kc == nkc - 1)))
                                fw.pe_group(fns, reads=[tk_V, tk_pT, tk_c], writes=[tk_O, tk_S])
                            Ov = Op[:].rearrange("p (m q) -> p m q", m=2)[:, :, 0:QB]
                            Sv = Sp[:].rearrange("p (m q) -> p m q", m=2)[:, :, 0:QB]
                            fw.op("dve", lambda e: e.reciprocal(out=rr[:, :, 0:QB], in_=Sv), reads=[tk_S], writes=[tk_rr])
                            fw.op("dve", lambda e: e.tensor_tensor(out=oo[:, :, 0:QB], in0=Ov, in1=rr[:, :, 0:QB], op=ALU.mult), reads=[tk_O, tk_rr], writes=[tk_oo])
                            fw.op("dve", lambda e: e.scalar_tensor_tensor(out=od[:, 0:QB], in0=oo[:, 1, 0:QB], scalar=nlam[:, l:l + 1], in1=oo[:, 0, 0:QB],
                                                                          op0=ALU.mult, op1=ALU.add), reads=[tk_oo, tk_mod], writes=[tk_od])
                            fw.op("act", lambda e: e.activation(out=sqb[:, 0:QB], in_=od[:, 0:QB], func=AF.Square), reads=[tk_od], writes=[tk_sqb])
                            p2, tk_p2 = ps_bufs[2]
                            fw.pe_group([lambda e: e.matmul(p2[:, 0:QB], lhsT=ones[:], rhs=sqb[:, 0:QB], start=True, stop=True)],
                                        reads=[tk_sqb, tk_c], writes=[tk_p2])
                            fw.op("act", lambda e: e.activation(out=rs[:, 0:QB], in_=p2[:, 0:QB], func=AF.Sqrt, bias=EPS, scale=1.0 / 128), reads=[tk_p2], writes=[tk_rs])
                            fw.op("dve", lambda e: e.reciprocal(out=rs[:, 0:QB], in_=rs[:, 0:QB]), reads=[tk_rs], writes=[tk_rs])
                            oc, tk_oc = ocr.next()
                            fw.op("dve", lambda e: e.scalar_tensor_tensor(out=oc[:, 0:QB], in0=od[:, 0:QB], scalar=csc_l[:, 0:1], in1=rs[:, 0:QB],
                                                                          op0=ALU.mult, op1=ALU.mult), reads=[tk_od, tk_rs, tk_cs], writes=[tk_oc])
                            g0 = tok0 + qb * QB
                            tt = g0 // TT
                            fw.dma("sp", catC[hh, :, g0:g0 + QB], oc[:, 0:QB], reads=[tk_oc], writes=[tk_catC[tt]])

        def fnet(l):
            with ExitStack() as es:
                def sbm(name, shape, dt):
                    return es.enter_context(nc.sbuf_tensor(uname(name), list(shape), dt))
                nqc_max = max(cfg.NQC, PSEQ // 128)
                YZ = sbm("YZ", [128, 4, nqc_max, 256], BF16); tk_YZ = Tk()
                dxr = Ring([(sbm("dxr%d" % i, [128, 512], BF16), Tk()) for i in range(3)])
                slab = Ring([(sbm("slab%d" % i, [128, 2, cfg.NQC, cfg.FPB], BF16), Tk()) for i in range(2)])
                fo = Ring([(sbm("fo%d" % i, [128, 512], BF16), Tk()) for i in range(3)])
                seqs = [(s * PSEQ, PSEQ, False, [0]) for s in range(NPS)]
                seqs.append((cfg.TP, SL, True, list(range(1, NT))))
                for (tok0, Ls, is_s, tl) in seqs:
                    nqc = Ls // 128
                    rd = [tk_dx[t] for t in tl]
                    for g in range(4):
                        for q4 in range(0, nqc, 4):
                            nq = min(4, nqc - q4)
                            dx, tk_dxb = dxr.next()
                            fw.dma("sp", dx[:, 0:nq * 128], dxS[g, :, tok0 + q4 * 128: tok0 + (q4 + nq) * 128], reads=rd, writes=[tk_dxb])
                            pp, tk_pp = PS.next()
                            for i in range(nq):
                                fw.pe_group([lambda e, i=i: e.matmul(pp[:, i * 256:(i + 1) * 256], lhsT=dx[:, i * 128:(i + 1) * 128], rhs=csc[:],
                                                                     start=True, stop=True)], reads=[tk_dxb, tk_c], writes=[tk_pp])
                            fw.op("act", lambda e: e.activation(out=YZ[:, g, q4:q4 + nq, :], in_=pp[:, 0:nq * 256].rearrange("p (a b) -> p a b", b=256),
                                                                func=AF.Copy), reads=[tk_pp], writes=[tk_YZ])
                    PB = cfg.FPB if is_s else PSEQ
                    for pb in range(Ls // PB):
                        if is_s:
                            sl, tk_sl = slab.next()
                            fw.dma("sp", sl[:], dftS[pb], writes=[tk_sl])
                            Cm = lambda qc: sl[:, 0, qc, :]
                            Sm = lambda qc: sl[:, 1, qc, :]
                            rdm = [tk_sl]
                        else:
                            Cm = lambda qc: dftp[:, 0, qc, :]
                            Sm = lambda qc: dftp[:, 1, qc, :]
                            rdm = [tk_c]
                        for g2 in range(0, 4, 2):
                            pp, tk_pp = PS.next()
                            for gi in range(2):
                                g = g2 + gi
                                fns = []
                                for qc in range(nqc):
                                    fns.append(lambda e, qc=qc, g=g, gi=gi: e.matmul(pp[:, gi * 512:gi * 512 + PB], lhsT=YZ[:, g, qc, 0:128], rhs=Cm(qc),
                                                                                      start=(qc == 0), stop=False))
                                    fns.append(lambda e, qc=qc, g=g, gi=gi: e.matmul(pp[:, gi * 512:gi * 512 + PB], lhsT=YZ[:, g, qc, 128:256], rhs=Sm(qc),
                                                                                      start=False, stop=(qc == nqc - 1)))
                                fw.pe_group(fns, reads=[tk_YZ] + rdm, writes=[tk_pp])
                            ob, tk_ob = fo.next()
                            obv = ob[:].rearrange("p (m q) -> p m q", m=2)[:, :, 0:PB]
                            ppv = pp[:].rearrange("p (m q) -> p m q", m=2)[:, :, 0:PB]
                            fw.op("act", lambda e: e.activation(out=obv, in_=ppv, func=AF.Copy), reads=[tk_pp], writes=[tk_ob])
                            g0 = tok0 + pb * PB
                            tt = g0 // TT
                            for gi in range(2):
                                fw.dma("sp", fS[g2 + gi, :, g0:g0 + PB], ob[:, gi * 256:gi * 256 + PB], reads=[tk_ob], writes=[tk_f[tt]])

        class Region:
            def __init__(self):
                self.tks = []
                self.carry = {}

            def new_gen(self):
                m = self.carry
                for t in self.tks:
                    if t.w is not None and m.get(t.w[0], 0) < t.w[1]:
                        m[t.w[0]] = t.w[1]
                    for (k, v) in t.rs:
                        if m.get(k, 0) < v:
                            m[k] = v
                self.tks = []

            def tk(self, name=""):
                t = Tk(name)
                t.rs = list(self.carry.items())
                self.tks.append(t)
                return t

        def common_env(sbd):
            env = {}
            env["hT"] = (sbd("hT", [128, NCH, TT], BF16), Tk())
            env["xring"] = Ring([(sbd("xr%d" % i, [128, TT], F32), Tk()) for i in range(2)])
            env["tring"] = Ring([(sbd("tr%d" % i, [128, TT], F32), Tk()) for i in range(3)])
            env["sqring"] = Ring([(sbd("sq%d" % i, [128, TT], BF16), Tk()) for i in range(2)])
            env["rstd"] = (sbd("rstd", [128, TT], F32), Tk())
            env["wd"] = Ring([(sbd("wd%d" % i, [128, NFC, 128], BF16), Tk()) for i in range(2)])
            return env

        def ffn_env(env, sbe, reg):
            reg.new_gen()
            env["mid"] = (sbe("mid", [128, NFC, TT], BF16), reg.tk())
            env["wu"] = Ring([(sbe("wu%d" % i, [128, 2, NCH, 128], BF16), reg.tk()) for i in range(2)])
            env["sgring"] = Ring([(sbe("sg%d" % i, [128, TT], BF16), reg.tk()) for i in range(2)])

        def inproj_env(env, sbe, reg):
            reg.new_gen()
            env["uT"] = (sbe("uT", [128, 4, TT], BF16), reg.tk())
            env["stage"] = (sbe("stage", [128, 4, TT], BF16), reg.tk())
            env["wtok"] = Ring([(sbe("wtok", [128, NCH, 512], BF16), reg.tk())])
            env["ang"] = (sbe("ang", [128, 512], F32), reg.tk())
            env["wsT"] = (sbe("wsT", [128, 4, 128], BF16), reg.tk())
            env["bsr"] = (sbe("bsr", [128, 512], BF16), reg.tk())
            env["gv"] = Ring([(sbe("gv%d" % i, [128, 512], F32), reg.tk()) for i in range(2)])
            env["vn"] = Ring([(sbe("vn%d" % i, [128, 512], BF16), reg.tk()) for i in range(2)])
            env["st2"] = Ring([(sbe("st2%d" % i, [128, 2], F32), reg.tk()) for i in range(4)])
            env["junk"] = (sbe("junk", [128, 512], BF16), reg.tk())
            env["rope"] = (sbe("rope", [128, 2, TT], F32), reg.tk())
            env["obring"] = Ring([(sbe("ob%d" % i, [128, TT], BF16), reg.tk()) for i in range(2)])
            env["wi"] = Ring([(sbe("wi%d" % i, [128, NCH, 128], BF16), reg.tk()) for i in range(2)])
            env["sgring"] = Ring([(sbe("sg%d" % i, [128, TT], BF16), reg.tk()) for i in range(2)])

        def outmix_env(env, sbe, reg):
            reg.new_gen()
            env["stage"] = (sbe("stage", [128, 4, TT], BF16), reg.tk())
            env["G"] = (sbe("G", [128, TT + 8 * HALO], BF16), reg.tk())
            env["acc"] = (sbe("acc", [128, 4, TT], F32), reg.tk())
            env["obring"] = Ring([(sbe("ob%d" % i, [128, TT], BF16), reg.tk()) for i in range(2)])
            env["wsm"] = Ring([(sbe("wsm%d" % i, [128, 4, 128], BF16), reg.tk()) for i in range(2)])
            env["lnmean"] = (sbe("lnmean", [128, TT], F32), reg.tk())
            env["lnrstd"] = (sbe("lnrstd", [128, TT], F32), reg.tk())

        def sub(es):
            return lambda name, shape, dt: es.enter_context(nc.sbuf_tensor(uname(name), list(shape), dt))

        def xc_env(env, sbe, reg):
            reg.new_gen()
            env["xbC"] = (sbe("xbC", [128, NCH, TT], BF16), reg.tk())

        NSTEP = 3 * L

        def bufs_of(k):
            src = "in" if k == 0 else (k - 1) % 3
            dst = "out" if k == NSTEP - 1 else k % 3
            return src, dst

        def in_tile(env, reg, es, t, l, ssq_pair):
            src, dst = bufs_of(3 * l)
            if ssq_pair is None:
                norm_mod(env, src, t, l, 0)
            else:
                norm_sb(env, env["hT"][0], env["hT"][1], ssq_pair, t, l, 0)
            with ExitStack() as es2:
                ffn_env(env, sub(es2), reg)
                sp2 = ffn(env, src, dst, t, l, 0)
            norm_sb(env, env["hT"][0], env["hT"][1], sp2, t, l, 1)
            with ExitStack() as es2:
                inproj_env(env, sub(es2), reg)
                in_proj(env, t, l)

        with ExitStack() as es:
            env = common_env(sub(es))
            reg = Region()
            for t in range(NT):
                in_tile(env, reg, es, t, 0, None)
                chk("ip")
        fw.barrier()
        chk("in")
        for l in range(L):
            attention(l)
            fw.barrier()
            chk("att")
            fnet(l)
            fw.barrier()
            chk("fn")
            with ExitStack() as es:
                env = common_env(sub(es))
                reg = Region()
                for t in range(NT):
                    with ExitStack() as es2:
                        outmix_env(env, sub(es2), reg)
                        out_mix(env, t, l)
                    chk("om")
                    s1_, d1_ = bufs_of(3 * l + 1)
                    with ExitStack() as es2:
                        xc_env(env, sub(es2), reg)
                        xbC, tk_xbC = env["xbC"]
                        sp1 = residual_proj(env, s1_, d1_, t, l, 1, w_out, env["hT"][0], env["hT"][1], NCH, env["wd"], xb=xbC, tk_xb=tk_xbC)
                        chk("wo")
                        norm_sb(env, xbC, tk_xbC, sp1, t, l, 2)
                    s2_, d2_ = bufs_of(3 * l + 2)
                    with ExitStack() as es2:
                        ffn_env(env, sub(es2), reg)
                        sp2 = ffn(env, s2_, d2_, t, l, 1)
                    if l + 1 < L:
                        in_tile(env, reg, es, t, l + 1, sp2)
            fw.barrier()
      fw.drain_all()
    nc._fw_stats = (fw.nins, fw.nwait)
    return nc


def _bf16(a):
    return np.asarray(a, dtype=np.float32).astype(ml_dtypes.bfloat16)


def make_consts(cfg):
    SL = cfg.SL
    c = {}
    t = np.arange(SL)
    row = (t // GRID_W).astype(np.float32)
    col = (t % GRID_W).astype(np.float32)
    freqs = (np.float32(10000.0) ** (-np.arange(16, dtype=np.float32) / np.float32(16))).astype(np.float32)
    rope = np.zeros((128, 2, SL), np.float32)
    for p in range(128):
        d = p % 64
        pos = row if d < 32 else col
        ang = (pos * freqs[d % 16]).astype(np.float32)
        rope[p, 0] = np.cos(ang)
        sgn = -1.0 if (d % 32) < 16 else 1.0
        rope[p, 1] = sgn * np.sin(ang)
    c["ropeT"] = rope
    perm = np.zeros((128, 128), np.float32)
    for m in range(128):
        k = m + 16 if (m % 32) < 16 else m - 16
        perm[k, m] = 1.0
    c["permM"] = _bf16(perm)
    c["onesM"] = _bf16(np.ones((128, 128), np.float32))
    blk = np.zeros((128, 128), np.float32)
    blk[:64, :64] = 1.0
    blk[64:, 64:] = 1.0
    c["blkM"] = _bf16(blk)
    i = np.arange(128)
    a = 2.0 * np.pi * ((i[:, None] * i[None, :]) % 128) / 128.0
    c["csC"] = _bf16(np.concatenate([np.cos(a), np.sin(a)], axis=1) / np.sqrt(128.0))

    def pos_dft(n):
        q = np.arange(n, dtype=np.int64)
        a = 2.0 * np.pi * ((q[:, None] * q[None, :]) % n).astype(np.float64) / n
        return (np.cos(a) / np.sqrt(n)).astype(np.float32), (-np.sin(a) / np.sqrt(n)).astype(np.float32)

    C, S = pos_dft(PSEQ)
    cs = np.stack([C, S], 0).reshape(2, PSEQ // 128, 128, PSEQ)
    c["dftP"] = _bf16(cs.transpose(2, 0, 1, 3))
    C, S = pos_dft(SL)
    cs = np.stack([C, S], 0).reshape(2, cfg.NQC, 128, cfg.NPB, cfg.FPB)
    c["dftS"] = _bf16(np.ascontiguousarray(cs.transpose(3, 2, 0, 1, 4)))
    return c


def layout_weights(inp, cfg):
    L = cfg.L
    f = lambda k: np.asarray(inp[k], dtype=np.float32)
    w = {}
    w["w_mod"] = np.ascontiguousarray(f("w_mod").reshape(L, 16, 128, 36, 4, 128).transpose(0, 3, 2, 4, 1, 5))
    w["b_modT"] = np.ascontiguousarray(f("b_mod").reshape(L, 144, 128).transpose(0, 2, 1))
    w["norm_gT"] = np.ascontiguousarray(f("norm_g").reshape(L, 3, 16, 128).transpose(0, 3, 1, 2))
    for i, (a, b) in enumerate((("w_ff1_in", "w_ff1_down"), ("w_ff2_in", "w_ff2_down"))):
        w["w_up%d" % i] = np.ascontiguousarray(f(a).reshape(L, 16, 128, 2, NFC, 128).transpose(0, 4, 2, 3, 1, 5))
        w["w_dn%d" % i] = np.ascontiguousarray(f(b).reshape(L, NFC, 128, 16, 128).transpose(0, 3, 2, 1, 4))
    win = f("w_in").reshape(L, 16, 128, 32, 128)
    sel = [0, 1, 2, 3, 8, 9, 10, 11, 12, 13, 14, 15, 16, 17, 18, 19, 20, 21, 22, 23, 28, 29, 30, 31]
    w["w_inf"] = np.ascontiguousarray(win[:, :, :, sel, :].transpose(0, 3, 2, 1, 4))
    win2 = f("w_in").reshape(L, 16, 128, 8, 512)
    w["w_int"] = np.ascontiguousarray(win2[:, :, :, [1, 6], :].transpose(0, 3, 2, 1, 4))
    w["w_out"] = np.ascontiguousarray(f("w_out").reshape(L, 16, 128, 16, 128).transpose(0, 3, 2, 1, 4))
    w["b_pw"] = np.ascontiguousarray(f("b_pw").reshape(L, 4, 128, 4, 128).transpose(0, 3, 2, 1, 4))
    w["d_lin"] = np.ascontiguousarray(f("d_lin").reshape(L, 4, 128, 4, 128).transpose(0, 3, 2, 1, 4))
    w["a_ng"] = np.ascontiguousarray(np.broadcast_to(f("a_norm_g")[:, None, :], (L, 128, 512)))
    w["a_wsT"] = np.ascontiguousarray(f("a_ws").transpose(0, 3, 1, 2))
    abs_pad = np.zeros((L, 128, 512), np.float32)
    abs_pad[:, 0, :] = f("a_bs").reshape(L, 512)
    w["a_bs"] = abs_pad
    w["b_cw"] = np.ascontiguousarray(f("b_conv_w").reshape(L, CONV_W, 4, 128).transpose(0, 3, 2, 1))
    bv = np.stack([f("b_conv_b"), f("b_ln_g"), f("b_ln_b")], axis=1).reshape(L, 3, 4, 128)
    w["b_vec"] = np.ascontiguousarray(bv.transpose(0, 3, 1, 2))
    qg = np.concatenate([f("c_qnorm_g"), f("c_qnorm_g")], axis=1)
    kg = np.concatenate([f("c_knorm_g"), f("c_knorm_g")], axis=1)
    w["c_g"] = np.ascontiguousarray(np.stack([qg, kg], axis=2))
    w["c_lam"] = np.ascontiguousarray(np.broadcast_to(f("c_lambda").reshape(L, 1, 256), (L, 128, 256)))
    w["c_sub"] = np.ascontiguousarray(f("c_subln_g").reshape(L, 128, 1))
    return w


def per_core_inputs(inp, cfg, core, shared):
    L, SL = cfg.L, cfg.SL
    s = core % inp["x_sample"].shape[0]
    xp = np.asarray(inp["x_prompt"], np.float32)[NPS * core:NPS * (core + 1)].reshape(cfg.TP, D)
    xs = np.asarray(inp["x_sample"], np.float32)[s]
    x = np.concatenate([xp, xs], axis=0)
    m = dict(shared)
    m["xT_in"] = np.ascontiguousarray(x.T).reshape(NCH, 128, cfg.T)
    cond = np.stack([np.asarray(inp["c_ctx"], np.float32), np.asarray(inp["c"], np.float32)[s]], axis=1)
    m["condT"] = np.ascontiguousarray(cond.reshape(NCH, 128, 2).transpose(1, 0, 2))
    ck = np.asarray(inp["cache_k"], np.float32)[s]
    m["cache_kT"] = np.ascontiguousarray(ck.transpose(0, 2, 3, 1))
    cv = np.asarray(inp["cache_v"], np.float32)[s]
    m["cache_v"] = np.ascontiguousarray(cv.reshape(L, cfg.PAST // 128, 128, 512))
    return m


def run(inp, cfg, n_cores):
    nc = build_program(cfg)
    shared = layout_weights(inp, cfg)
    shared.update(make_consts(cfg))
    in_maps = [per_core_inputs(inp, cfg, c, shared) for c in range(n_cores)]
    res = run_bass_kernel_spmd(nc, in_maps, core_ids=list(range(n_cores)))
    return res.results


def assemble(results, cfg, n_cores, n_samples):
    L = cfg.L
    yp, ys, nk, nv = [], [None] * n_samples, [], []
    for c in range(n_cores):
        r = results[c]
        y = np.asarray(r["yT"]).reshape(D, cfg.T).T
        yp.append(y[:cfg.TP].reshape(NPS, PSEQ, D))
        if c < n_samples:
            ys[c] = y[cfg.TP:]
        k = np.asarray(r["newkT"]).transpose(3, 0, 1, 2).reshape(NPS, PSEQ, L, 4, 128).transpose(0, 2, 1, 3, 4)
        nk.append(k)
        v = np.asarray(r["newv"]).reshape(L, NPS, PSEQ, 4, 128).transpose(1, 0, 2, 3, 4)
        nv.append(v)
    return (np.ascontiguousarray(np.concatenate(yp, 0), dtype=np.float32),
            np.ascontiguousarray(np.stack(ys, 0), dtype=np.float32),
            np.ascontiguousarray(np.concatenate(nk, 0), dtype=np.float32),
            np.ascontiguousarray(np.concatenate(nv, 0), dtype=np.float32))


def kernel(**inputs):
    cfg = Cfg(L=4, SL=4096, PAST=512)
    results = run(inputs, cfg, 8)
    return assemble(results, cfg, 8, 4)
```

```python
import math
from contextlib import ExitStack
import numpy as np
import ml_dtypes
import concourse.bass as bass
import concourse.mybir as mybir
from concourse.bass_utils import run_bass_kernel_spmd

F32 = mybir.dt.float32
BF16 = mybir.dt.bfloat16
AF = mybir.ActivationFunctionType
ALU = mybir.AluOpType
AX = mybir.AxisListType

D = 2048
NCH = 16
DFF = 5632
NFC = 44
GRID_W = 64
EPS = 1e-6
TT = 1024
PSEQ = 256
NPS = 4
CONV_W = 31
HALO = 15


class Tk:
    __slots__ = ("name", "w", "rs", "excl")

    def __init__(self, name="", excl=False):
        self.name = name
        self.w = None
        self.rs = []
        self.excl = excl


class FW:
    ENG = ("pe", "act", "dve", "pool", "sp")

    def __init__(self, nc, stack, n_dma_sems=24, same_engine_sync=True):
        self.nc = nc
        self.e = {"pe": nc.tensor, "act": nc.scalar, "dve": nc.vector, "pool": nc.gpsimd, "sp": nc.sync}
        self.sem = {}
        self.cnt = {}
        for k in self.ENG:
            self.sem[k] = stack.enter_context(nc.semaphore("s_" + k))
            self.cnt[k] = 0
        self.dq = {}
        for q in ("sp", "pool"):
            sems = [stack.enter_context(nc.semaphore("d_%s%d" % (q, i))) for i in range(n_dma_sems)]
            self.dq[q] = {"sems": sems, "n": 0}
            for i, s in enumerate(sems):
                self.sem[("d", q, i)] = s
        self.seen = {k: {} for k in self.ENG}
        self.same = same_engine_sync
        self.nwait = 0
        self.nins = 0
        self.stopped = False

    def _deps(self, reads, writes):
        d = {}
        for t in reads:
            if t.w is not None:
                k, v = t.w
                if d.get(k, 0) < v:
                    d[k] = v
            if t.excl:
                for (k, v) in t.rs:
                    if d.get(k, 0) < v:
                        d[k] = v
        for t in writes:
            if t.w is not None:
                k, v = t.w
                if d.get(k, 0) < v:
                    d[k] = v
            for (k, v) in t.rs:
                if d.get(k, 0) < v:
                    d[k] = v
        return d

    def _wait(self, eng, deps):
        seen = self.seen[eng]
        for k, v in deps.items():
            if k == eng and (eng == "pe" or not self.same):
                continue
            if seen.get(k, 0) >= v:
                continue
            self.e[eng].wait_ge(self.sem[k], v)
            self.nwait += 1
            seen[k] = v

    def _commit(self, key, val, reads, writes):
        for t in writes:
            t.w = (key, val)
            t.rs = []
        for t in reads:
            if t.excl:
                t.w = (key, val)
                t.rs = []
                continue
            t.rs.append((key, val))
            if len(t.rs) > 48:
                m = {}
                for (k, v) in t.rs:
                    if m.get(k, 0) < v:
                        m[k] = v
                t.rs = list(m.items())

    def op(self, eng, fn, reads=(), writes=()):
        if self.stopped:
            return
        self._wait(eng, self._deps(reads, writes))
        ins = fn(self.e[eng])
        self.cnt[eng] += 1
        ins.then_inc(self.sem[eng], 1)
        self.nins += 1
        self._commit(eng, self.cnt[eng], reads, writes)

    def pe_group(self, fns, reads=(), writes=()):
        if self.stopped:
            return
        self._wait("pe", self._deps(reads, writes))
        ins = None
        for fn in fns:
            ins = fn(self.e["pe"])
            self.nins += 1
        self.cnt["pe"] += 1
        ins.then_inc(self.sem["pe"], 1)
        self._commit("pe", self.cnt["pe"], reads, writes)

    def dma(self, q, out, in_, reads=(), writes=()):
        if self.stopped:
            return
        dq = self.dq[q]
        n = dq["n"]
        ns = len(dq["sems"])
        slot = n % ns
        rnd = n // ns
        key = ("d", q, slot)
        deps = self._deps(reads, writes)
        if rnd > 0 and deps.get(key, 0) < 16 * rnd:
            deps[key] = 16 * rnd
        self._wait(q, deps)
        ins = self.e[q].dma_start(out=out, in_=in_)
        ins.then_inc(self.sem[key], 16)
        dq["n"] = n + 1
        self.nins += 1
        self._commit(key, 16 * (rnd + 1), reads, writes)

    def barrier(self):
        if self.stopped:
            return
        deps = {}
        for k in self.ENG:
            if self.cnt[k] > 0:
                deps[k] = self.cnt[k]
        for q, dq in self.dq.items():
            ns = len(dq["sems"])
            for slot in range(ns):
                uses = (dq["n"] - slot + ns - 1) // ns if dq["n"] > slot else 0
                if uses > 0:
                    deps[("d", q, slot)] = 16 * uses
        for k in self.ENG:
            self._wait(k, dict(deps))

    def drain_all(self):
        if self.stopped:
            return
        deps = {}
        for k in self.ENG:
            if k != "sp" and self.cnt[k] > 0:
                deps[k] = self.cnt[k]
        for q, dq in self.dq.items():
            ns = len(dq["sems"])
            for slot in range(ns):
                uses = (dq["n"] - slot + ns - 1) // ns if dq["n"] > slot else 0
                if uses > 0:
                    deps[("d", q, slot)] = 16 * uses
        self._wait("sp", deps)


class Ring:
    def __init__(self, bufs):
        self.bufs = bufs
        self.i = 0

    def next(self):
        b = self.bufs[self.i % len(self.bufs)]
        self.i += 1
        return b


class _Stop(Exception):
    pass


class Cfg:
    def __init__(self, L=4, SL=4096, PAST=512, stop=None):
        self.stop = stop
        self.L = L
        self.SL = SL
        self.PAST = PAST
        self.TP = NPS * PSEQ
        self.T = self.TP + SL
        self.NT = self.T // TT
        self.NKC = (PAST + SL) // 128
        self.FPB = 256
        self.NPB = SL // self.FPB
        self.NQC = SL // 128


def lambda_init(l):
    return 0.8 - 0.6 * math.exp(-0.3 * l)


def build_program(cfg):
    L, SL, PAST, T, NT = cfg.L, cfg.SL, cfg.PAST, cfg.T, cfg.NT
    nc = bass.Bass("TRN2", target_bir_lowering=False)

    def din(name, shape, dt=F32):
        return nc.dram_tensor(name, list(shape), dt, kind="ExternalInput").ap()

    def dout(name, shape, dt=F32):
        return nc.dram_tensor(name, list(shape), dt, kind="ExternalOutput").ap()

    def dscr(name, shape, dt):
        kind = "ExternalOutput" if getattr(cfg, "debug", False) else "Internal"
        return nc.dram_tensor(name, list(shape), dt, kind=kind).ap()

    xT_in = din("xT_in", [NCH, 128, T])
    condT = din("condT", [128, NCH, 2])
    w_mod = din("w_mod", [L, 36, 128, 4, NCH, 128])
    b_modT = din("b_modT", [L, 128, 144])
    norm_gT = din("norm_gT", [L, 128, 3, NCH])
    w_up = [din("w_up%d" % i, [L, NFC, 128, 2, NCH, 128]) for i in range(2)]
    w_dn = [din("w_dn%d" % i, [L, NCH, 128, NFC, 128]) for i in range(2)]
    w_inf = din("w_inf", [L, 24, 128, NCH, 128])
    w_int = din("w_int", [L, 2, 128, NCH, 512])
    w_out = din("w_out", [L, NCH, 128, NCH, 128])
    b_pw = din("b_pw", [L, 4, 128, 4, 128])
    d_lin = din("d_lin", [L, 4, 128, 4, 128])
    a_ng = din("a_ng", [L, 128, 512])
    a_wsT = din("a_wsT", [L, 128, 4, 128])
    a_bs = din("a_bs", [L, 128, 512])
    b_cw = din("b_cw", [L, 128, 4, CONV_W])
    b_vec = din("b_vec", [L, 128, 3, 4])
    c_g = din("c_g", [L, 128, 2])
    c_lam = din("c_lam", [L, 128, 256])
    c_sub = din("c_sub", [L, 128, 1])
    cache_kT = din("cache_kT", [L, 4, 128, PAST])
    cache_v = din("cache_v", [L, PAST // 128, 128, 512])
    ropeT = din("ropeT", [128, 2, SL])
    permM = din("permM", [128, 128], BF16)
    onesM = din("onesM", [128, 128], BF16)
    blkM = din("blkM", [128, 128], BF16)
    csC = din("csC", [128, 256], BF16)
    dftP = din("dftP", [128, 2, 2, PSEQ], BF16)
    dftS = din("dftS", [cfg.NPB, 128, 2, cfg.NQC, cfg.FPB], BF16)
    yT = dout("yT", [NCH, 128, T])
    newkT = dout("newkT", [L, 4, 128, cfg.TP])
    newv = dout("newv", [L, cfg.TP // 128, 128, 512])
    XS = [dscr("xs%d" % i, [NCH, 128, T], F32) for i in range(3)]
    catA = dscr("catA", [4, 128, T], BF16)
    gluS2 = [dscr("gluS%d" % i, [4, 128, T], BF16) for i in range(2)]
    qS = dscr("qS", [4, 128, T], BF16)
    kS = dscr("kS", [4, 128, T], BF16)
    vS = dscr("vS", [T // 128, 128, 512], BF16)
    dxS = dscr("dxS", [4, 128, T], BF16)
    catC = dscr("catC", [4, 128, T], BF16)
    fS = dscr("fS", [4, 128, T], BF16)

    _uid = [0]

    def uname(name):
        _uid[0] += 1
        return "%s_%d" % (name, _uid[0])

    def chk(name):
        if cfg.stop == name:
            fw.drain_all()
            fw.stopped = True

    dbg_st2 = dout("dbg_st2", [T // 128, 128, 2]) if getattr(cfg, "debug", False) else None
    dbg_cat = dout("dbg_cat", [NT, 128, NCH, TT], BF16) if getattr(cfg, "debug", False) else None
    dbg_acc = dout("dbg_acc", [NT, 128, 4, TT]) if getattr(cfg, "debug", False) else None
    dbg_ln = dout("dbg_ln", [NT, 2, 128, TT]) if getattr(cfg, "debug", False) else None
    with ExitStack() as st:
      fw = FW(nc, st)
      if True:

        def sb(name, shape, dt):
            return st.enter_context(nc.sbuf_tensor(name, list(shape), dt))

        def tile_tks(prefix, n):
            return [Tk("%s%d" % (prefix, i)) for i in range(n)]

        tk_xin = tile_tks("xin", NT)
        tk_XS = [tile_tks("xs%d_" % i, NT) for i in range(3)]
        tk_yT = tile_tks("y", NT)
        tk_catA = tile_tks("catA", NT)
        tk_glu2 = [tile_tks("glu%d_" % i, NT) for i in range(2)]
        tk_q = tile_tks("q", NT)
        tk_k = tile_tks("k", NT)
        tk_v = tile_tks("v", NT)
        tk_dx = tile_tks("dx", NT)
        tk_catC = tile_tks("catC", NT)
        tk_f = tile_tks("f", NT)
        tk_out_misc = Tk("outmisc")

        ps_bufs = []
        for i in range(4):
            p = st.enter_context(nc.psum_tensor("ps%d" % i, [128, 1024], F32))
            ps_bufs.append((p, Tk("ps%d" % i, excl=True)))
        PS = Ring(ps_bufs)

        ones = sb("ones", [128, 128], BF16); tk_c = Tk("consts")
        blk = sb("blk", [128, 128], BF16)
        perm = sb("perm", [128, 128], BF16)
        csc = sb("csc", [128, 256], BF16)
        dftp = sb("dftp", [128, 2, 2, PSEQ], BF16)
        fw.dma("sp", ones[:], onesM, writes=[tk_c])
        fw.dma("sp", blk[:], blkM, writes=[tk_c])
        fw.dma("sp", perm[:], permM, writes=[tk_c])
        fw.dma("sp", csc[:], csC, writes=[tk_c])
        fw.dma("sp", dftp[:], dftP, writes=[tk_c])

        chk("c0")
        modT = sb("modT", [128, L, 144, 2], F32); tk_mod = Tk("mod")
        gm = sb("gm", [128, L, 3, NCH, 2], F32)
        gate = sb("gate", [128, L, 3, NCH, 2], F32)
        cwt = sb("cwt", [128, L, 4, CONV_W], F32)
        bvt = sb("bvt", [128, L, 3, 4], F32)
        cgt = sb("cgt", [128, L, 2], F32)
        cst = sb("cst", [128, L, 1], F32)
        nlam = sb("nlam", [128, L], F32)
        tk_par = Tk("params")
        for l in range(L):
            fw.dma("sp", cwt[:, l], b_cw[l], writes=[tk_par])
            fw.dma("sp", bvt[:, l], b_vec[l], writes=[tk_par])
            fw.dma("sp", cgt[:, l], c_g[l], writes=[tk_par])
            fw.dma("sp", cst[:, l], c_sub[l], writes=[tk_par])

        with ExitStack() as ps_:
            def sbp(name, shape, dt):
                return ps_.enter_context(nc.sbuf_tensor(uname(name), list(shape), dt))
            ngt = sbp("ngt", [128, L, 3, NCH], F32)
            bmt = sbp("bmt", [128, L, 144], F32)
            clt = sbp("clt", [128, L, 256], F32)
            for l in range(L):
                fw.dma("sp", ngt[:, l], norm_gT[l], writes=[tk_par])
                fw.dma("sp", bmt[:, l], b_modT[l], writes=[tk_par])
                fw.dma("sp", clt[:, l], c_lam[l], writes=[tk_par])
            cnd = sbp("cnd", [128, NCH, 2], F32); tk_cnd = Tk()
            scb = sbp("scb", [128, NCH, 2], BF16); tk_scb = Tk()
            fw.dma("sp", cnd[:], condT, writes=[tk_cnd])
            fw.op("act", lambda e: e.activation(out=scb[:], in_=cnd[:], func=AF.Silu), reads=[tk_cnd], writes=[tk_scb])
            wm_ring = Ring([(sbp("wm%d" % i, [128, 4, NCH, 128], BF16), Tk()) for i in range(3)])
            for l in range(L):
                mp, tk_mp = PS.next()
                mpv = mp[:, 0:288].rearrange("p (n c) -> p n c", c=2)
                for g in range(36):
                    wm, tk_wm = wm_ring.next()
                    fw.dma("pool", wm[:], w_mod[l, g], writes=[tk_wm])
                    for j in range(4):
                        n = g * 4 + j
                        fw.pe_group([
                            (lambda e, kc=kc, j=j, n=n: e.matmul(mpv[:, n, :], lhsT=wm[:, j, kc, :], rhs=scb[:, kc, :],
                                                                  start=(kc == 0), stop=(kc == NCH - 1)))
                            for kc in range(NCH)], reads=[tk_wm, tk_scb], writes=[tk_mp])
                for cidx in range(2):
                    fw.op("dve", lambda e, cidx=cidx: e.tensor_tensor(out=modT[:, l, :, cidx], in0=mpv[:, :, cidx], in1=bmt[:, l, :], op=ALU.add),
                          reads=[tk_mp, tk_par], writes=[tk_mod])
                for i in range(3):
                    for cidx in range(2):
                        fw.op("dve", lambda e, i=i, cidx=cidx: e.scalar_tensor_tensor(
                            out=gm[:, l, i, :, cidx], in0=modT[:, l, (3 * i + 1) * 16:(3 * i + 2) * 16, cidx], scalar=1.0,
                            in1=ngt[:, l, i, :], op0=ALU.add, op1=ALU.mult), reads=[tk_mod, tk_par], writes=[tk_mod])
                        gs = 1.0 if i == 1 else 0.5
                        fw.op("dve", lambda e, i=i, cidx=cidx, gs=gs: e.tensor_scalar(
                            out=gate[:, l, i, :, cidx], in0=modT[:, l, (3 * i + 2) * 16:(3 * i + 3) * 16, cidx], scalar1=gs, scalar2=None,
                            op0=ALU.mult), reads=[tk_mod], writes=[tk_mod])
                lt = sbp("lt%d" % l, [128, 2, 64], F32); tk_lt = Tk()
                ls = sbp("ls%d" % l, [128, 2], F32)
                for r in range(2):
                    fw.op("dve", lambda e, r=r: e.tensor_tensor(out=lt[:, r, :], in0=clt[:, l, (2 * r) * 64:(2 * r + 1) * 64],
                                                                 in1=clt[:, l, (2 * r + 1) * 64:(2 * r + 2) * 64], op=ALU.mult),
                          reads=[tk_par], writes=[tk_lt])
                    fw.op("dve", lambda e, r=r: e.reduce_sum(out=ls[:, r:r + 1], in_=lt[:, r, :], axis=AX.X), reads=[tk_lt], writes=[tk_lt])
                fw.op("act", lambda e: e.activation(out=ls[:], in_=ls[:], func=AF.Exp), reads=[tk_lt], writes=[tk_lt])
                fw.op("dve", lambda e, l=l: e.scalar_tensor_tensor(out=nlam[:, l:l + 1], in0=ls[:, 1:2], scalar=-lambda_init(l),
                                                                    in1=ls[:, 0:1], op0=ALU.add, op1=ALU.subtract),
                      reads=[tk_lt], writes=[tk_mod])

        chk("pro")
        fw.barrier()
        def cond_of(t):
            return 0 if t == 0 else 1

        def x_src(buf_idx):
            if buf_idx == "in":
                return xT_in, tk_xin
            if buf_idx == "out":
                return yT, tk_yT
            return XS[buf_idx], tk_XS[buf_idx]

        def norm_mod(env, src, t, l, i):
            sap, stk = x_src(src)
            cidx = cond_of(t)
            tok = slice(t * TT, (t + 1) * TT)
            hT, tk_h = env["hT"]
            ssq, tk_ssq = PS.next()
            for c in range(NCH):
                xc, tk_xc = env["xring"].next()
                fw.dma("sp", xc[:], sap[c, :, tok], reads=[stk[t]], writes=[tk_xc])
                sq, tk_sq = env["sqring"].next()
                fw.op("act", lambda e: e.activation(out=sq[:], in_=xc[:], func=AF.Square), reads=[tk_xc], writes=[tk_sq])
                fw.pe_group([(lambda e, h=h: e.matmul(ssq[:, h * 512:(h + 1) * 512], lhsT=ones[:], rhs=sq[:, h * 512:(h + 1) * 512],
                                                        start=(c == 0), stop=(c == NCH - 1))) for h in range(2)],
                            reads=[tk_sq, tk_c], writes=[tk_ssq])
            rstd, tk_rstd = env["rstd"]
            fw.op("act", lambda e: e.activation(out=rstd[:], in_=ssq[:], func=AF.Sqrt, bias=EPS, scale=1.0 / D), reads=[tk_ssq], writes=[tk_rstd])
            fw.op("dve", lambda e: e.reciprocal(out=rstd[:], in_=rstd[:]), reads=[tk_rstd], writes=[tk_rstd])
            for c in range(NCH):
                xc, tk_xc = env["xring"].next()
                fw.dma("sp", xc[:], sap[c, :, tok], reads=[stk[t]], writes=[tk_xc])
                tb, tk_tb = env["tring"].next()
                fw.op("dve", lambda e: e.scalar_tensor_tensor(out=tb[:], in0=xc[:], scalar=gm[:, l, i, c, cidx:cidx + 1], in1=rstd[:],
                                                              op0=ALU.mult, op1=ALU.mult), reads=[tk_xc, tk_rstd, tk_mod], writes=[tk_tb])
                sh = modT[:, l, (3 * i) * 16 + c, cidx:cidx + 1]
                fw.op("act", lambda e: e.activation(out=hT[:, c, :], in_=tb[:], func=AF.Identity, bias=sh, scale=1.0),
                      reads=[tk_tb, tk_mod], writes=[tk_h])

        def ffn(env, src, dst, t, l, which):
            cidx = cond_of(t)
            tok = slice(t * TT, (t + 1) * TT)
            hT, tk_h = env["hT"]
            mid, tk_mid = env["mid"]
            gi = 0 if which == 0 else 2
            for j in range(NFC):
                wu, tk_wu = env["wu"].next()
                fw.dma("pool", wu[:], w_up[which][l, j], writes=[tk_wu])
                pa, tk_pa = PS.next()
                pg, tk_pg = PS.next()
                for (ag, pp, tkp) in ((0, pa, tk_pa), (1, pg, tk_pg)):
                    for h in range(2):
                        fw.pe_group([(lambda e, kc=kc, ag=ag, pp=pp, h=h: e.matmul(
                            pp[:, h * 512:(h + 1) * 512], lhsT=wu[:, ag, kc, :], rhs=hT[:, kc, h * 512:(h + 1) * 512],
                            start=(kc == 0), stop=(kc == NCH - 1))) for kc in range(NCH)],
                            reads=[tk_wu, tk_h], writes=[tkp])
                sg, tk_sg = env["sgring"].next()
                fw.op("act", lambda e: e.activation(out=sg[:], in_=pg[:], func=AF.Silu), reads=[tk_pg], writes=[tk_sg])
                fw.op("dve", lambda e: e.tensor_tensor(out=mid[:, j, :], in0=pa[:], in1=sg[:], op=ALU.mult),
                      reads=[tk_pa, tk_sg], writes=[tk_mid])
            return residual_proj(env, src, dst, t, l, gi, w_dn[which], mid, tk_mid, NFC, env["wd"], xb=hT, tk_xb=tk_h)

        def residual_proj(env, src, dst, t, l, gi, wdram, act, tk_act, nk, wring, xb=None, tk_xb=None):
            cidx = cond_of(t)
            tok = slice(t * TT, (t + 1) * TT)
            sap, stk = x_src(src)
            dap, dtk = x_src(dst)
            carry = xb is not None
            ring3 = Ring(ps_bufs[0:3]) if carry else PS
            ssq, tk_ssq = ps_bufs[3]
            for c in range(NCH):
                wd, tk_wd = wring.next()
                fw.dma("pool", wd[:, 0:nk, :], wdram[l, c], writes=[tk_wd])
                pp, tk_pp = ring3.next()
                for h in range(2):
                    fw.pe_group([(lambda e, fc=fc, h=h: e.matmul(pp[:, h * 512:(h + 1) * 512], lhsT=wd[:, fc, :],
                                                                  rhs=act[:, fc, h * 512:(h + 1) * 512],
                                                                  start=(fc == 0), stop=(fc == nk - 1))) for fc in range(nk)],
                                reads=[tk_wd, tk_act], writes=[tk_pp])
                xc, tk_xc = env["xring"].next()
                fw.dma("sp", xc[:], sap[c, :, tok], reads=[stk[t]], writes=[tk_xc])
                xo, tk_xo = env["tring"].next()
                fw.op("dve", lambda e: e.scalar_tensor_tensor(out=xo[:], in0=pp[:], scalar=gate[:, l, gi, c, cidx:cidx + 1], in1=xc[:],
                                                              op0=ALU.mult, op1=ALU.add), reads=[tk_pp, tk_xc, tk_mod], writes=[tk_xo])
                fw.dma("sp", dap[c, :, tok], xo[:], reads=[tk_xo], writes=[dtk[t]])
                if carry:
                    sq, tk_sq = env["sqring"].next()
                    fw.op("act", lambda e: e.activation(out=sq[:], in_=xo[:], func=AF.Square), reads=[tk_xo], writes=[tk_sq])
                    fw.op("act", lambda e: e.activation(out=xb[:, c, :], in_=xo[:], func=AF.Copy), reads=[tk_xo], writes=[tk_xb])
                    fw.pe_group([(lambda e, h=h: e.matmul(ssq[:, h * 512:(h + 1) * 512], lhsT=ones[:], rhs=sq[:, h * 512:(h + 1) * 512],
                                                            start=(c == 0), stop=(c == NCH - 1))) for h in range(2)],
                                reads=[tk_sq, tk_c], writes=[tk_ssq])
            return (ssq, tk_ssq) if carry else None

        def norm_sb(env, xb, tk_xb, ssq_pair, t, l, i):
            ssq, tk_ssq = ssq_pair
            cidx = cond_of(t)
            hT, tk_h = env["hT"]
            rstd, tk_rstd = env["rstd"]
            fw.op("act", lambda e: e.activation(out=rstd[:], in_=ssq[:], func=AF.Sqrt, bias=EPS, scale=1.0 / D), reads=[tk_ssq], writes=[tk_rstd])
            fw.op("dve", lambda e: e.reciprocal(out=rstd[:], in_=rstd[:]), reads=[tk_rstd], writes=[tk_rstd])
            tk_hc = Tk("hchunks")
            inplace = tk_xb is tk_h
            for c in range(NCH):
                tb, tk_tb = env["tring"].next()
                fw.op("dve", lambda e: e.scalar_tensor_tensor(out=tb[:], in0=xb[:, c, :], scalar=gm[:, l, i, c, cidx:cidx + 1], in1=rstd[:],
                                                              op0=ALU.mult, op1=ALU.mult), reads=[tk_xb, tk_rstd, tk_mod], writes=[tk_tb])
                sh = modT[:, l, (3 * i) * 16 + c, cidx:cidx + 1]
                wr = [tk_hc, tk_h] if (c == 0 and not inplace) else [tk_hc]
                fw.op("act", lambda e: e.activation(out=hT[:, c, :], in_=tb[:], func=AF.Identity, bias=sh, scale=1.0),
                      reads=[tk_tb, tk_mod], writes=wr)
            if not fw.stopped:
                tk_h.w = ("act", fw.cnt["act"])
                tk_h.rs = []

        def in_proj(env, t, l):
            tok = slice(t * TT, (t + 1) * TT)
            is_prompt = (t == 0)
            hT, tk_h = env["hT"]
            uT, tk_u = env["uT"]
            stg, tk_stg = env["stage"]

            def proj_chunk(n):
                wt, tk_wt = env["wi"].next()
                fw.dma("pool", wt[:], w_inf[l, n], writes=[tk_wt])
                pp, tk_pp = PS.next()
                for h in range(2):
                    fw.pe_group([(lambda e, kc=kc, h=h: e.matmul(pp[:, h * 512:(h + 1) * 512], lhsT=wt[:, kc, :],
                                                                  rhs=hT[:, kc, h * 512:(h + 1) * 512],
                                                                  start=(kc == 0), stop=(kc == NCH - 1))) for kc in range(NCH)],
                                reads=[tk_wt, tk_h], writes=[tk_pp])
                return pp, tk_pp

            for j in range(4):
                pp, tk_pp = proj_chunk(j)
                fw.op("act", lambda e: e.activation(out=uT[:, j, :], in_=pp[:], func=AF.Gelu_apprx_tanh), reads=[tk_pp], writes=[tk_u])
            chk("ipA1")
            wav, tk_wav = env["wtok"].next()
            fw.dma("pool", wav[:], w_int[l, 0], writes=[tk_wav])
            ang, tk_ang = env["ang"]
            fw.dma("sp", ang[:], a_ng[l], writes=[tk_ang])
            wsT, tk_ws = env["wsT"]
            fw.dma("pool", wsT[:], a_wsT[l], writes=[tk_ws])
            bsr, tk_bs = env["bsr"]
            fw.dma("pool", bsr[:], a_bs[l], writes=[tk_bs])
            for tb in range(TT // 128):
                tsl = slice(tb * 128, (tb + 1) * 128)
                pp, tk_pp = PS.next()
                fw.pe_group([(lambda e, kc=kc: e.matmul(pp[:, 0:512], lhsT=hT[:, kc, tsl], rhs=wav[:, kc, :],
                                                         start=(kc == 0), stop=(kc == NCH - 1))) for kc in range(NCH)],
                            reads=[tk_wav, tk_h], writes=[tk_pp])
                gv, tk_gv = env["gv"].next()
                fw.op("act", lambda e: e.activation(out=gv[:], in_=pp[:, 0:512], func=AF.Gelu_apprx_tanh), reads=[tk_pp], writes=[tk_gv])
                jk, tk_jk = env["junk"]
                st2, tk_st2 = env["st2"].next()
                fw.op("act", lambda e: e.activation(out=jk[:, 0:512], in_=gv[:], func=AF.Square, accum_out=st2[:, 0:1]),
                      reads=[tk_gv], writes=[tk_jk, tk_st2])
                fw.op("act", lambda e: e.activation(out=st2[:, 1:2], in_=st2[:, 0:1], func=AF.Sqrt, bias=EPS, scale=1.0 / 512),
                      reads=[tk_st2], writes=[tk_st2])
                fw.op("dve", lambda e: e.reciprocal(out=st2[:, 1:2], in_=st2[:, 1:2]), reads=[tk_st2], writes=[tk_st2])
                if dbg_st2 is not None:
                    fw.dma("sp", dbg_st2[t * 8 + tb], st2[:], reads=[tk_st2], writes=[tk_out_misc])
                vn, tk_vn = env["vn"].next()
                fw.op("dve", lambda e: e.scalar_tensor_tensor(out=vn[:], in0=gv[:], scalar=st2[:, 1:2], in1=ang[:],
                                                              op0=ALU.mult, op1=ALU.mult), reads=[tk_gv, tk_st2, tk_ang], writes=[tk_vn])
                p2 = pp[:, 512:1024]
                for hh in range(4):
                    fw.pe_group([
                        (lambda e, hh=hh: e.matmul(p2[:, hh * 128:(hh + 1) * 128], lhsT=vn[:, hh * 128:(hh + 1) * 128], rhs=wsT[:, hh, :],
                                                   start=True, stop=False)),
                        (lambda e, hh=hh: e.matmul(p2[:, hh * 128:(hh + 1) * 128], lhsT=ones[:, :], rhs=bsr[:, hh * 128:(hh + 1) * 128],
                                                   start=False, stop=True)),
                    ], reads=[tk_vn, tk_ws, tk_bs, tk_c], writes=[tk_pp])
                fw.op("dve", lambda e: e.tensor_tensor(out=stg[:, :, tsl], in0=uT[:, :, tsl],
                                                       in1=p2.rearrange("p (h q) -> p h q", h=4), op=ALU.mult),
                      reads=[tk_pp, tk_u], writes=[tk_stg])
            for j in range(4):
                fw.dma("sp", catA[j, :, tok], stg[:, j, :], reads=[tk_stg], writes=[tk_catA[t]])
            chk("ipA2")
            for j in range(4):
                pa, tk_pa = proj_chunk(4 + j)
                pg, tk_pg = proj_chunk(8 + j)
                sg, tk_sg = env["sgring"].next()
                fw.op("act", lambda e: e.activation(out=sg[:], in_=pg[:], func=AF.Sigmoid), reads=[tk_pg], writes=[tk_sg])
                ob, tk_ob = env["obring"].next()
                fw.op("dve", lambda e: e.tensor_tensor(out=ob[:], in0=pa[:], in1=sg[:], op=ALU.mult), reads=[tk_pa, tk_sg], writes=[tk_ob])
                fw.dma("sp", gluS2[l % 2][j, :, tok], ob[:], reads=[tk_ob], writes=[tk_glu2[l % 2][t]])
            chk("ipB")
            if not is_prompt:
                rp, tk_rp = env["rope"]
                s0 = t * TT - cfg.TP
                fw.dma("sp", rp[:], ropeT[:, :, s0:s0 + TT], writes=[tk_rp])
            for qk in range(2):
                for hh in range(4):
                    pp, tk_pp = proj_chunk(12 + qk * 4 + hh)
                    sq, tk_sq = env["sqring"].next()
                    fw.op("act", lambda e: e.activation(out=sq[:], in_=pp[:], func=AF.Square), reads=[tk_pp], writes=[tk_sq])
                    p2, tk_p2 = PS.next()
                    fw.pe_group([(lambda e, h=h: e.matmul(p2[:, h * 512:(h + 1) * 512], lhsT=blk[:], rhs=sq[:, h * 512:(h + 1) * 512],
                                                            start=True, stop=True)) for h in range(2)], reads=[tk_sq, tk_c], writes=[tk_p2])
                    rs, tk_rs = env["tring"].next()
                    fw.op("act", lambda e: e.activation(out=rs[:], in_=p2[:], func=AF.Sqrt, bias=EPS, scale=1.0 / 64), reads=[tk_p2], writes=[tk_rs])
                    fw.op("dve", lambda e: e.reciprocal(out=rs[:], in_=rs[:]), reads=[tk_rs], writes=[tk_rs])
                    qn, tk_qn = env["tring"].next()
                    fw.op("dve", lambda e: e.scalar_tensor_tensor(out=qn[:], in0=pp[:], scalar=cgt[:, l, qk:qk + 1], in1=rs[:],
                                                                  op0=ALU.mult, op1=ALU.mult), reads=[tk_pp, tk_rs, tk_par], writes=[tk_qn])
                    ob, tk_ob = env["obring"].next()
                    dstS, dtk = (qS, tk_q) if qk == 0 else (kS, tk_k)
                    if is_prompt:
                        fw.op("act", lambda e: e.activation(out=ob[:], in_=qn[:], func=AF.Copy), reads=[tk_qn], writes=[tk_ob])
                        if qk == 1:
                            fw.dma("sp", newkT[l, hh], qn[:], reads=[tk_qn], writes=[tk_out_misc])
                    else:
                        qb_, tk_qb = env["obring"].next()
                        fw.op("act", lambda e: e.activation(out=qb_[:], in_=qn[:], func=AF.Copy), reads=[tk_qn], writes=[tk_qb])
                        p3, tk_p3 = PS.next()
                        fw.pe_group([(lambda e, h=h: e.matmul(p3[:, h * 512:(h + 1) * 512], lhsT=perm[:], rhs=qb_[:, h * 512:(h + 1) * 512],
                                                                start=True, stop=True)) for h in range(2)], reads=[tk_qb, tk_c], writes=[tk_p3])
                        t1, tk_t1 = env["tring"].next()
                        fw.op("dve", lambda e: e.tensor_tensor(out=t1[:], in0=qn[:], in1=rp[:, 0, :], op=ALU.mult), reads=[tk_qn, tk_rp], writes=[tk_t1])
                        t2, tk_t2 = env["tring"].next()
                        fw.op("dve", lambda e: e.tensor_tensor(out=t2[:], in0=p3[:], in1=rp[:, 1, :], op=ALU.mult), reads=[tk_p3, tk_rp], writes=[tk_t2])
                        fw.op("dve", lambda e: e.tensor_tensor(out=ob[:], in0=t1[:], in1=t2[:], op=ALU.add), reads=[tk_t1, tk_t2], writes=[tk_ob])
                    fw.dma("sp", dstS[hh, :, tok], ob[:], reads=[tk_ob], writes=[dtk[t]])
            chk("ipC")
            wcv, tk_wcv = env["wtok"].next()
            fw.dma("pool", wcv[:], w_int[l, 1], writes=[tk_wcv])
            for tb in range(TT // 128):
                tsl = slice(tb * 128, (tb + 1) * 128)
                pp, tk_pp = PS.next()
                fw.pe_group([(lambda e, kc=kc: e.matmul(pp[:, 0:512], lhsT=hT[:, kc, tsl], rhs=wcv[:, kc, :],
                                                         start=(kc == 0), stop=(kc == NCH - 1))) for kc in range(NCH)],
                            reads=[tk_wcv, tk_h], writes=[tk_pp])
                vn, tk_vn = env["vn"].next()
                fw.op("act", lambda e: e.activation(out=vn[:], in_=pp[:, 0:512], func=AF.Copy), reads=[tk_pp], writes=[tk_vn])
                fw.dma("sp", vS[t * (TT // 128) + tb], vn[:], reads=[tk_vn], writes=[tk_v[t]])
                if is_prompt:
                    gv, tk_gv = env["gv"].next()
                    fw.op("dve", lambda e: e.tensor_copy(out=gv[:], in_=pp[:, 0:512]), reads=[tk_pp], writes=[tk_gv])
                    fw.dma("sp", newv[l, tb], gv[:], reads=[tk_gv], writes=[tk_out_misc])
            chk("ipV")
            for j in range(4):
                pp, tk_pp = proj_chunk(20 + j)
                ob, tk_ob = env["obring"].next()
                fw.op("act", lambda e: e.activation(out=ob[:], in_=pp[:], func=AF.Copy), reads=[tk_pp], writes=[tk_ob])
                fw.dma("sp", dxS[j, :, tok], ob[:], reads=[tk_ob], writes=[tk_dx[t]])

        def out_mix(env, t, l):
            tok = slice(t * TT, (t + 1) * TT)
            is_prompt = (t == 0)
            catT, tk_cat = env["hT"]
            gluS = gluS2[l % 2]
            tk_glu = tk_glu2[l % 2]
            for j in range(4):
                fw.dma("sp", catT[:, j, :], catA[j, :, tok], reads=[tk_catA[t]], writes=[tk_cat])
                fw.dma("sp", catT[:, 8 + j, :], catC[j, :, tok], reads=[tk_catC[t]], writes=[tk_cat])
            fT, tk_fT = env["stage"]
            for j in range(4):
                fw.dma("sp", fT[:, j, :], fS[j, :, tok], reads=[tk_f[t]], writes=[tk_fT])
            for n in range(4):
                wt, tk_wt = env["wsm"].next()
                fw.dma("pool", wt[:], d_lin[l, n], writes=[tk_wt])
                pp, tk_pp = PS.next()
                for h in range(2):
                    fw.pe_group([(lambda e, kc=kc, h=h: e.matmul(pp[:, h * 512:(h + 1) * 512], lhsT=wt[:, kc, :],
                                                                  rhs=fT[:, kc, h * 512:(h + 1) * 512],
                                                                  start=(kc == 0), stop=(kc == 3))) for kc in range(4)],
                                reads=[tk_wt, tk_fT], writes=[tk_pp])
                fw.op("act", lambda e: e.activation(out=catT[:, 12 + n, :], in_=pp[:], func=AF.Copy), reads=[tk_pp], writes=[tk_cat])
            nseg, seglen = (NPS, PSEQ) if is_prompt else (1, TT)
            G, tk_G = env["G"]
            acc, tk_acc = env["acc"]
            Gv = G[:, 0:nseg * (seglen + 2 * HALO)].rearrange("p (s w) -> p s w", s=nseg)
            s1, tk_s1 = PS.next()
            s2, tk_s2 = PS.next()
            for j in range(4):
                fw.op("dve", lambda e: e.memset(G[:], 0.0), writes=[tk_G])
                if is_prompt:
                    for s in range(NPS):
                        fw.dma("sp", Gv[:, s, HALO:HALO + PSEQ], gluS[j, :, s * PSEQ:(s + 1) * PSEQ], reads=[tk_glu[t]], writes=[tk_G])
                else:
                    lo = t * TT - HALO
                    hi = (t + 1) * TT + HALO
                    lo_c = max(lo, cfg.TP)
                    hi_c = min(hi, T)
                    rd = [tk_glu[t]]
                    if t - 1 >= 1:
                        rd.append(tk_glu[t - 1])
                    if t + 1 < NT:
                        rd.append(tk_glu[t + 1])
                    fw.dma("sp", Gv[:, 0, lo_c - lo:hi_c - lo], gluS[j, :, lo_c:hi_c], reads=rd, writes=[tk_G])
                av = acc[:, j, :].rearrange("p (s w) -> p s w", s=nseg)
                fw.op("dve", lambda e: e.tensor_scalar(out=av, in0=Gv[:, :, 0:seglen], scalar1=cwt[:, l, j, 0:1], scalar2=bvt[:, l, 0, j:j + 1],
                                                       op0=ALU.mult, op1=ALU.add), reads=[tk_G, tk_par], writes=[tk_acc])
                for k in range(1, CONV_W):
                    fw.op("dve", lambda e, k=k: e.scalar_tensor_tensor(out=av, in0=Gv[:, :, k:k + seglen], scalar=cwt[:, l, j, k:k + 1], in1=av,
                                                                        op0=ALU.mult, op1=ALU.add), reads=[tk_G, tk_par], writes=[tk_acc])
                cb, tk_cb = env["obring"].next()
                fw.op("act", lambda e: e.activation(out=cb[:], in_=acc[:, j, :], func=AF.Copy), reads=[tk_acc], writes=[tk_cb])
                sq, tk_sq = env["sqring"].next()
                fw.op("act", lambda e: e.activation(out=sq[:], in_=acc[:, j, :], func=AF.Square), reads=[tk_acc], writes=[tk_sq])
                fw.pe_group([(lambda e, h=h: e.matmul(s1[:, h * 512:(h + 1) * 512], lhsT=ones[:], rhs=cb[:, h * 512:(h + 1) * 512],
                                                        start=(j == 0), stop=(j == 3))) for h in range(2)], reads=[tk_cb, tk_c], writes=[tk_s1])
                fw.pe_group([(lambda e, h=h: e.matmul(s2[:, h * 512:(h + 1) * 512], lhsT=ones[:], rhs=sq[:, h * 512:(h + 1) * 512],
                                                        start=(j == 0), stop=(j == 3))) for h in range(2)], reads=[tk_sq, tk_c], writes=[tk_s2])
            mean, tk_mean = env["lnmean"]
            fw.op("act", lambda e: e.activation(out=mean[:], in_=s1[:], func=AF.Copy, scale=1.0 / 512), reads=[tk_s1], writes=[tk_mean])
            msq, tk_msq = env["lnrstd"]
            fw.op("dve", lambda e: e.tensor_tensor(out=msq[:], in0=mean[:], in1=mean[:], op=ALU.mult), reads=[tk_mean], writes=[tk_msq])
            fw.op("dve", lambda e: e.scalar_tensor_tensor(out=msq[:], in0=s2[:], scalar=1.0 / 512, in1=msq[:], op0=ALU.mult, op1=ALU.subtract),
                  reads=[tk_s2, tk_msq], writes=[tk_msq])
            fw.op("act", lambda e: e.activation(out=msq[:], in_=msq[:], func=AF.Sqrt, bias=EPS, scale=1.0), reads=[tk_msq], writes=[tk_msq])
            fw.op("dve", lambda e: e.reciprocal(out=msq[:], in_=msq[:]), reads=[tk_msq], writes=[tk_msq])
            if dbg_acc is not None:
                fw.dma("sp", dbg_acc[t], acc[:], reads=[tk_acc], writes=[tk_out_misc])
                fw.dma("sp", dbg_ln[t, 0], mean[:], reads=[tk_mean], writes=[tk_out_misc])
                fw.dma("sp", dbg_ln[t, 1], msq[:], reads=[tk_msq], writes=[tk_out_misc])
            yb, tk_yb = env["stage"]
            for j in range(4):
                dd, tk_dd = env["tring"].next()
                fw.op("dve", lambda e: e.tensor_tensor(out=dd[:], in0=acc[:, j, :], in1=mean[:], op=ALU.subtract), reads=[tk_acc, tk_mean], writes=[tk_dd])
                fw.op("dve", lambda e: e.tensor_tensor(out=dd[:], in0=dd[:], in1=msq[:], op=ALU.mult), reads=[tk_dd, tk_msq], writes=[tk_dd])
                fw.op("act", lambda e: e.activation(out=yb[:, j, :], in_=dd[:], func=AF.Silu, bias=bvt[:, l, 2, j:j + 1], scale=bvt[:, l, 1, j:j + 1]),
                      reads=[tk_dd, tk_par], writes=[tk_yb])
            for n in range(4):
                wt, tk_wt = env["wsm"].next()
                fw.dma("pool", wt[:], b_pw[l, n], writes=[tk_wt])
                pp, tk_pp = PS.next()
                for h in range(2):
                    fw.pe_group([(lambda e, kc=kc, h=h: e.matmul(pp[:, h * 512:(h + 1) * 512], lhsT=wt[:, kc, :],
                                                                  rhs=yb[:, kc, h * 512:(h + 1) * 512],
                                                                  start=(kc == 0), stop=(kc == 3))) for kc in range(4)],
                                reads=[tk_wt, tk_yb], writes=[tk_pp])
                fw.op("act", lambda e: e.activation(out=catT[:, 4 + n, :], in_=pp[:], func=AF.Copy), reads=[tk_pp], writes=[tk_cat])

        def attention(l):
            with ExitStack() as es:
                def sbm(name, shape, dt):
                    return es.enter_context(nc.sbuf_tensor(uname(name), list(shape), dt))
                nkc_max = cfg.NKC
                Vt = sbm("Vt", [128, nkc_max, 512], BF16); tk_V = Tk()
                KT = sbm("KT", [128, nkc_max * 128], BF16); tk_K = Tk()
                QT = sbm("QT", [128, max(SL, PSEQ)], BF16); tk_Q = Tk()
                pring = Ring([(sbm("pT%d" % i, [128, 2, 512], BF16), Tk()) for i in range(3)])
                rr = sbm("rr", [128, 2, 512], F32); tk_rr = Tk()
                oo = sbm("oo", [128, 2, 512], F32); tk_oo = Tk()
                od = sbm("od", [128, 512], F32); tk_od = Tk()
                sqb = sbm("sqb", [128, 512], BF16); tk_sqb = Tk()
                rs = sbm("rs", [128, 512], F32); tk_rs = Tk()
                ocr = Ring([(sbm("oc%d" % i, [128, 512], BF16), Tk()) for i in range(2)])
                csc_l = sbm("cscl", [128, 1], F32); tk_cs = Tk()
                fw.op("dve", lambda e: e.tensor_scalar(out=csc_l[:], in0=cst[:, l, :], scalar1=(1.0 - lambda_init(l)), scalar2=None, op0=ALU.mult),
                      reads=[tk_par], writes=[tk_cs])
                seqs = [(s * PSEQ, PSEQ, False, [0]) for s in range(NPS)]
                seqs.append((cfg.TP, SL, True, list(range(1, NT))))
                for (tok0, Lq, has_ctx, tl) in seqs:
                    nctx = PAST if has_ctx else 0
                    nk = nctx + Lq
                    nkc = nk // 128
                    QB = min(512, Lq)
                    rd_v = [tk_v[t] for t in tl]
                    rd_q = [tk_q[t] for t in tl]
                    rd_k = [tk_k[t] for t in tl]
                    if has_ctx:
                        for kc in range(nctx // 128):
                            fw.dma("pool", Vt[:, kc, :], cache_v[l, kc], writes=[tk_V])
                    for kc in range(Lq // 128):
                        fw.dma("sp", Vt[:, nctx // 128 + kc, :], vS[tok0 // 128 + kc], reads=rd_v, writes=[tk_V])
                    for hh in range(4):
                        if has_ctx:
                            fw.dma("pool", KT[:, 0:nctx], cache_kT[l, hh], writes=[tk_K])
                        fw.dma("sp", KT[:, nctx:nk], kS[hh, :, tok0:tok0 + Lq], reads=rd_k, writes=[tk_K])
                        fw.dma("sp", QT[:, 0:Lq], qS[hh, :, tok0:tok0 + Lq], reads=rd_q, writes=[tk_Q])
                        for qb in range(Lq // QB):
                            qsl = slice(qb * QB, (qb + 1) * QB)
                            Op, tk_O = ps_bufs[0]
                            Sp, tk_S = ps_bufs[1]
                            def emit_S(kc):
                                ksl = slice(kc * 128, (kc + 1) * 128)
                                stp, tk_st = ps_bufs[2 + (kc % 2)]
                                fw.pe_group([(lambda e, m=m: e.matmul(stp[:, m * 512:m * 512 + QB], lhsT=KT[64 * m:64 * m + 64, ksl],
                                                                      rhs=QT[64 * m:64 * m + 64, qsl], start=True, stop=True)) for m in range(2)],
                                            reads=[tk_K, tk_Q], writes=[tk_st])

                            emit_S(0)
                            for kc in range(nkc):
                                stp, tk_st = ps_bufs[2 + (kc % 2)]
                                pT, tk_pT = pring.next()
                                stv = stp[:].rearrange("p (m q) -> p m q", m=2)[:, :, 0:QB]
                                fw.op("act", lambda e, pT=pT, stv=stv: e.activation(out=pT[:, :, 0:QB], in_=stv, func=AF.Exp, scale=0.125),
                                      reads=[tk_st], writes=[tk_pT])
                                if kc + 1 < nkc:
                                    emit_S(kc + 1)
                                fns = []
                                for m in range(2):
                                    fns.append(lambda e, m=m, pT=pT, kc=kc: e.matmul(Op[:, m * 512:m * 512 + QB], lhsT=Vt[:, kc, hh * 128:(hh + 1) * 128],
                                                                                     rhs=pT[:, m, 0:QB], start=(kc == 0), stop=(kc == nkc - 1)))
                                    fns.append(lambda e, m=m, pT=pT, kc=kc: e.matmul(Sp[:, m * 512:m * 512 + QB], lhsT=ones[:], rhs=pT[:, m, 0:QB],
                                                                                     start=(kc == 0), stop=(kc == nkc - 1)))
                                fw.pe_group(fns, reads=[tk_V, tk_pT, tk_c], writes=[tk_O, tk_S])
                            Ov = Op[:].rearrange("p (m q) -> p m q", m=2)[:, :, 0:QB]
                            Sv = Sp[:].rearrange("p (m q) -> p m q", m=2)[:, :, 0:QB]
                            fw.op("dve", lambda e: e.reciprocal(out=rr[:, :, 0:QB], in_=Sv), reads=[tk_S], writes=[tk_rr])
                            fw.op("dve", lambda e: e.tensor_tensor(out=oo[:, :, 0:QB], in0=Ov, in1=rr[:, :, 0:QB], op=ALU.mult), reads=[tk_O, tk_rr], writes=[tk_oo])
                            fw.op("dve", lambda e: e.scalar_tensor_tensor(out=od[:, 0:QB], in0=oo[:, 1, 0:QB], scalar=nlam[:, l:l + 1], in1=oo[:, 0, 0:QB],
                                                                          op0=ALU.mult, op1=ALU.add), reads=[tk_oo, tk_mod], writes=[tk_od])
                            fw.op("act", lambda e: e.activation(out=sqb[:, 0:QB], in_=od[:, 0:QB], func=AF.Square), reads=[tk_od], writes=[tk_sqb])
                            p2, tk_p2 = ps_bufs[2]
                            fw.pe_group([lambda e: e.matmul(p2[:, 0:QB], lhsT=ones[:], rhs=sqb[:, 0:QB], start=True, stop=True)],
                                        reads=[tk_sqb, tk_c], writes=[tk_p2])
                            fw.op("act", lambda e: e.activation(out=rs[:, 0:QB], in_=p2[:, 0:QB], func=AF.Sqrt, bias=EPS, scale=1.0 / 128), reads=[tk_p2], writes=[tk_rs])
                            fw.op("dve", lambda e: e.reciprocal(out=rs[:, 0:QB], in_=rs[:, 0:QB]), reads=[tk_rs], writes=[tk_rs])
                            oc, tk_oc = ocr.next()
                            fw.op("dve", lambda e: e.scalar_tensor_tensor(out=oc[:, 0:QB], in0=od[:, 0:QB], scalar=csc_l[:, 0:1], in1=rs[:, 0:QB],
                                                                          op0=ALU.mult, op1=ALU.mult), reads=[tk_od, tk_rs, tk_cs], writes=[tk_oc])
                            g0 = tok0 + qb * QB
                            tt = g0 // TT
                            fw.dma("sp", catC[hh, :, g0:g0 + QB], oc[:, 0:QB], reads=[tk_oc], writes=[tk_catC[tt]])

        def fnet(l):
            with ExitStack() as es:
                def sbm(name, shape, dt):
                    return es.enter_context(nc.sbuf_tensor(uname(name), list(shape), dt))
                nqc_max = max(cfg.NQC, PSEQ // 128)
                YZ = sbm("YZ", [128, 4, nqc_max, 256], BF16); tk_YZ = Tk()
                dxr = Ring([(sbm("dxr%d" % i, [128, 512], BF16), Tk()) for i in range(3)])
                slab = Ring([(sbm("slab%d" % i, [128, 2, cfg.NQC, cfg.FPB], BF16), Tk()) for i in range(2)])
                fo = Ring([(sbm("fo%d" % i, [128, 512], BF16), Tk()) for i in range(3)])
                seqs = [(s * PSEQ, PSEQ, False, [0]) for s in range(NPS)]
                seqs.append((cfg.TP, SL, True, list(range(1, NT))))
                for (tok0, Ls, is_s, tl) in seqs:
                    nqc = Ls // 128
                    rd = [tk_dx[t] for t in tl]
                    for g in range(4):
                        for q4 in range(0, nqc, 4):
                            nq = min(4, nqc - q4)
                            dx, tk_dxb = dxr.next()
                            fw.dma("sp", dx[:, 0:nq * 128], dxS[g, :, tok0 + q4 * 128: tok0 + (q4 + nq) * 128], reads=rd, writes=[tk_dxb])
                            pp, tk_pp = PS.next()
                            for i in range(nq):
                                fw.pe_group([lambda e, i=i: e.matmul(pp[:, i * 256:(i + 1) * 256], lhsT=dx[:, i * 128:(i + 1) * 128], rhs=csc[:],
                                                                     start=True, stop=True)], reads=[tk_dxb, tk_c], writes=[tk_pp])
                            fw.op("act", lambda e: e.activation(out=YZ[:, g, q4:q4 + nq, :], in_=pp[:, 0:nq * 256].rearrange("p (a b) -> p a b", b=256),
                                                                func=AF.Copy), reads=[tk_pp], writes=[tk_YZ])
                    PB = cfg.FPB if is_s else PSEQ
                    for pb in range(Ls // PB):
                        if is_s:
                            sl, tk_sl = slab.next()
                            fw.dma("sp", sl[:], dftS[pb], writes=[tk_sl])
                            Cm = lambda qc: sl[:, 0, qc, :]
                            Sm = lambda qc: sl[:, 1, qc, :]
                            rdm = [tk_sl]
                        else:
                            Cm = lambda qc: dftp[:, 0, qc, :]
                            Sm = lambda qc: dftp[:, 1, qc, :]
                            rdm = [tk_c]
                        for g2 in range(0, 4, 2):
                            pp, tk_pp = PS.next()
                            for gi in range(2):
                                g = g2 + gi
                                fns = []
                                for qc in range(nqc):
                                    fns.append(lambda e, qc=qc, g=g, gi=gi: e.matmul(pp[:, gi * 512:gi * 512 + PB], lhsT=YZ[:, g, qc, 0:128], rhs=Cm(qc),
                                                                                      start=(qc == 0), stop=False))
                                    fns.append(lambda e, qc=qc, g=g, gi=gi: e.matmul(pp[:, gi * 512:gi * 512 + PB], lhsT=YZ[:, g, qc, 128:256], rhs=Sm(qc),
                                                                                      start=False, stop=(qc == nqc - 1)))
                                fw.pe_group(fns, reads=[tk_YZ] + rdm, writes=[tk_pp])
                            ob, tk_ob = fo.next()
                            obv = ob[:].rearrange("p (m q) -> p m q", m=2)[:, :, 0:PB]
                            ppv = pp[:].rearrange("p (m q) -> p m q", m=2)[:, :, 0:PB]
                            fw.op("act", lambda e: e.activation(out=obv, in_=ppv, func=AF.Copy), reads=[tk_pp], writes=[tk_ob])
                            g0 = tok0 + pb * PB
                            tt = g0 // TT
                            for gi in range(2):
                                fw.dma("sp", fS[g2 + gi, :, g0:g0 + PB], ob[:, gi * 256:gi * 256 + PB], reads=[tk_ob], writes=[tk_f[tt]])

        class Region:
            def __init__(self):
                self.tks = []
                self.carry = {}

            def new_gen(self):
                m = self.carry
                for t in self.tks:
                    if t.w is not None and m.get(t.w[0], 0) < t.w[1]:
                        m[t.w[0]] = t.w[1]
                    for (k, v) in t.rs:
                        if m.get(k, 0) < v:
                            m[k] = v
                self.tks = []

            def tk(self, name=""):
                t = Tk(name)
                t.rs = list(self.carry.items())
                self.tks.append(t)
                return t

        def common_env(sbd):
            env = {}
            env["hT"] = (sbd("hT", [128, NCH, TT], BF16), Tk())
            env["xring"] = Ring([(sbd("xr%d" % i, [128, TT], F32), Tk()) for i in range(2)])
            env["tring"] = Ring([(sbd("tr%d" % i, [128, TT], F32), Tk()) for i in range(3)])
            env["sqring"] = Ring([(sbd("sq%d" % i, [128, TT], BF16), Tk()) for i in range(2)])
            env["rstd"] = (sbd("rstd", [128, TT], F32), Tk())
            env["wd"] = Ring([(sbd("wd%d" % i, [128, NFC, 128], BF16), Tk()) for i in range(2)])
            return env

        def ffn_env(env, sbe, reg):
            reg.new_gen()
            env["mid"] = (sbe("mid", [128, NFC, TT], BF16), reg.tk())
            env["wu"] = Ring([(sbe("wu%d" % i, [128, 2, NCH, 128], BF16), reg.tk()) for i in range(2)])
            env["sgring"] = Ring([(sbe("sg%d" % i, [128, TT], BF16), reg.tk()) for i in range(2)])

        def inproj_env(env, sbe, reg):
            reg.new_gen()
            env["uT"] = (sbe("uT", [128, 4, TT], BF16), reg.tk())
            env["stage"] = (sbe("stage", [128, 4, TT], BF16), reg.tk())
            env["wtok"] = Ring([(sbe("wtok", [128, NCH, 512], BF16), reg.tk())])
            env["ang"] = (sbe("ang", [128, 512], F32), reg.tk())
            env["wsT"] = (sbe("wsT", [128, 4, 128], BF16), reg.tk())
            env["bsr"] = (sbe("bsr", [128, 512], BF16), reg.tk())
            env["gv"] = Ring([(sbe("gv%d" % i, [128, 512], F32), reg.tk()) for i in range(2)])
            env["vn"] = Ring([(sbe("vn%d" % i, [128, 512], BF16), reg.tk()) for i in range(2)])
            env["st2"] = Ring([(sbe("st2%d" % i, [128, 2], F32), reg.tk()) for i in range(4)])
            env["junk"] = (sbe("junk", [128, 512], BF16), reg.tk())
            env["rope"] = (sbe("rope", [128, 2, TT], F32), reg.tk())
            env["obring"] = Ring([(sbe("ob%d" % i, [128, TT], BF16), reg.tk()) for i in range(2)])
            env["wi"] = Ring([(sbe("wi%d" % i, [128, NCH, 128], BF16), reg.tk()) for i in range(2)])
            env["sgring"] = Ring([(sbe("sg%d" % i, [128, TT], BF16), reg.tk()) for i in range(2)])

        def outmix_env(env, sbe, reg):
            reg.new_gen()
            env["stage"] = (sbe("stage", [128, 4, TT], BF16), reg.tk())
            env["G"] = (sbe("G", [128, TT + 8 * HALO], BF16), reg.tk())
            env["acc"] = (sbe("acc", [128, 4, TT], F32), reg.tk())
            env["obring"] = Ring([(sbe("ob%d" % i, [128, TT], BF16), reg.tk()) for i in range(2)])
            env["wsm"] = Ring([(sbe("wsm%d" % i, [128, 4, 128], BF16), reg.tk()) for i in range(2)])
            env["lnmean"] = (sbe("lnmean", [128, TT], F32), reg.tk())
            env["lnrstd"] = (sbe("lnrstd", [128, TT], F32), reg.tk())

        def sub(es):
            return lambda name, shape, dt: es.enter_context(nc.sbuf_tensor(uname(name), list(shape), dt))

        def xc_env(env, sbe, reg):
            reg.new_gen()
            env["xbC"] = (sbe("xbC", [128, NCH, TT], BF16), reg.tk())

        NSTEP = 3 * L

        def bufs_of(k):
            src = "in" if k == 0 else (k - 1) % 3
            dst = "out" if k == NSTEP - 1 else k % 3
            return src, dst

        def in_tile(env, reg, es, t, l, ssq_pair):
            src, dst = bufs_of(3 * l)
            if ssq_pair is None:
                norm_mod(env, src, t, l, 0)
            else:
                norm_sb(env, env["hT"][0], env["hT"][1], ssq_pair, t, l, 0)
            with ExitStack() as es2:
                ffn_env(env, sub(es2), reg)
                sp2 = ffn(env, src, dst, t, l, 0)
            norm_sb(env, env["hT"][0], env["hT"][1], sp2, t, l, 1)
            with ExitStack() as es2:
                inproj_env(env, sub(es2), reg)
                in_proj(env, t, l)

        with ExitStack() as es:
            env = common_env(sub(es))
            reg = Region()
            for t in range(NT):
                in_tile(env, reg, es, t, 0, None)
                chk("ip")
        fw.barrier()
        chk("in")
        for l in range(L):
            attention(l)
            fw.barrier()
            chk("att")
            fnet(l)
            fw.barrier()
            chk("fn")
            with ExitStack() as es:
                env = common_env(sub(es))
                reg = Region()
                for t in range(NT):
                    with ExitStack() as es2:
                        outmix_env(env, sub(es2), reg)
                        out_mix(env, t, l)
                    chk("om")
                    s1_, d1_ = bufs_of(3 * l + 1)
                    with ExitStack() as es2:
                        xc_env(env, sub(es2), reg)
                        xbC, tk_xbC = env["xbC"]
                        sp1 = residual_proj(env, s1_, d1_, t, l, 1, w_out, env["hT"][0], env["hT"][1], NCH, env["wd"], xb=xbC, tk_xb=tk_xbC)
                        chk("wo")
                        norm_sb(env, xbC, tk_xbC, sp1, t, l, 2)
                    s2_, d2_ = bufs_of(3 * l + 2)
                    with ExitStack() as es2:
                        ffn_env(env, sub(es2), reg)
                        sp2 = ffn(env, s2_, d2_, t, l, 1)
                    if l + 1 < L:
                        in_tile(env, reg, es, t, l + 1, sp2)
            fw.barrier()
      fw.drain_all()
    nc._fw_stats = (fw.nins, fw.nwait)
    return nc


def _bf16(a):
    return np.asarray(a, dtype=np.float32).astype(ml_dtypes.bfloat16)


def make_consts(cfg):
    SL = cfg.SL
    c = {}
    t = np.arange(SL)
    row = (t // GRID_W).astype(np.float32)
    col = (t % GRID_W).astype(np.float32)
    freqs = (np.float32(10000.0) ** (-np.arange(16, dtype=np.float32) / np.float32(16))).astype(np.float32)
    rope = np.zeros((128, 2, SL), np.float32)
    for p in range(128):
        d = p % 64
        pos = row if d < 32 else col
        ang = (pos * freqs[d % 16]).astype(np.float32)
        rope[p, 0] = np.cos(ang)
        sgn = -1.0 if (d % 32) < 16 else 1.0
        rope[p, 1] = sgn * np.sin(ang)
    c["ropeT"] = rope
    perm = np.zeros((128, 128), np.float32)
    for m in range(128):
        k = m + 16 if (m % 32) < 16 else m - 16
        perm[k, m] = 1.0
    c["permM"] = _bf16(perm)
    c["onesM"] = _bf16(np.ones((128, 128), np.float32))
    blk = np.zeros((128, 128), np.float32)
    blk[:64, :64] = 1.0
    blk[64:, 64:] = 1.0
    c["blkM"] = _bf16(blk)
    i = np.arange(128)
    a = 2.0 * np.pi * ((i[:, None] * i[None, :]) % 128) / 128.0
    c["csC"] = _bf16(np.concatenate([np.cos(a), np.sin(a)], axis=1) / np.sqrt(128.0))

    def pos_dft(n):
        q = np.arange(n, dtype=np.int64)
        a = 2.0 * np.pi * ((q[:, None] * q[None, :]) % n).astype(np.float64) / n
        return (np.cos(a) / np.sqrt(n)).astype(np.float32), (-np.sin(a) / np.sqrt(n)).astype(np.float32)

    C, S = pos_dft(PSEQ)
    cs = np.stack([C, S], 0).reshape(2, PSEQ // 128, 128, PSEQ)
    c["dftP"] = _bf16(cs.transpose(2, 0, 1, 3))
    C, S = pos_dft(SL)
    cs = np.stack([C, S], 0).reshape(2, cfg.NQC, 128, cfg.NPB, cfg.FPB)
    c["dftS"] = _bf16(np.ascontiguousarray(cs.transpose(3, 2, 0, 1, 4)))
    return c


def layout_weights(inp, cfg):
    L = cfg.L
    f = lambda k: np.asarray(inp[k], dtype=np.float32)
    w = {}
    w["w_mod"] = np.ascontiguousarray(f("w_mod").reshape(L, 16, 128, 36, 4, 128).transpose(0, 3, 2, 4, 1, 5))
    w["b_modT"] = np.ascontiguousarray(f("b_mod").reshape(L, 144, 128).transpose(0, 2, 1))
    w["norm_gT"] = np.ascontiguousarray(f("norm_g").reshape(L, 3, 16, 128).transpose(0, 3, 1, 2))
    for i, (a, b) in enumerate((("w_ff1_in", "w_ff1_down"), ("w_ff2_in", "w_ff2_down"))):
        w["w_up%d" % i] = np.ascontiguousarray(f(a).reshape(L, 16, 128, 2, NFC, 128).transpose(0, 4, 2, 3, 1, 5))
        w["w_dn%d" % i] = np.ascontiguousarray(f(b).reshape(L, NFC, 128, 16, 128).transpose(0, 3, 2, 1, 4))
    win = f("w_in").reshape(L, 16, 128, 32, 128)
    sel = [0, 1, 2, 3, 8, 9, 10, 11, 12, 13, 14, 15, 16, 17, 18, 19, 20, 21, 22, 23, 28, 29, 30, 31]
    w["w_inf"] = np.ascontiguousarray(win[:, :, :, sel, :].transpose(0, 3, 2, 1, 4))
    win2 = f("w_in").reshape(L, 16, 128, 8, 512)
    w["w_int"] = np.ascontiguousarray(win2[:, :, :, [1, 6], :].transpose(0, 3, 2, 1, 4))
    w["w_out"] = np.ascontiguousarray(f("w_out").reshape(L, 16, 128, 16, 128).transpose(0, 3, 2, 1, 4))
    w["b_pw"] = np.ascontiguousarray(f("b_pw").reshape(L, 4, 128, 4, 128).transpose(0, 3, 2, 1, 4))
    w["d_lin"] = np.ascontiguousarray(f("d_lin").reshape(L, 4, 128, 4, 128).transpose(0, 3, 2, 1, 4))
    w["a_ng"] = np.ascontiguousarray(np.broadcast_to(f("a_norm_g")[:, None, :], (L, 128, 512)))
    w["a_wsT"] = np.ascontiguousarray(f("a_ws").transpose(0, 3, 1, 2))
    abs_pad = np.zeros((L, 128, 512), np.float32)
    abs_pad[:, 0, :] = f("a_bs").reshape(L, 512)
    w["a_bs"] = abs_pad
    w["b_cw"] = np.ascontiguousarray(f("b_conv_w").reshape(L, CONV_W, 4, 128).transpose(0, 3, 2, 1))
    bv = np.stack([f("b_conv_b"), f("b_ln_g"), f("b_ln_b")], axis=1).reshape(L, 3, 4, 128)
    w["b_vec"] = np.ascontiguousarray(bv.transpose(0, 3, 1, 2))
    qg = np.concatenate([f("c_qnorm_g"), f("c_qnorm_g")], axis=1)
    kg = np.concatenate([f("c_knorm_g"), f("c_knorm_g")], axis=1)
    w["c_g"] = np.ascontiguousarray(np.stack([qg, kg], axis=2))
    w["c_lam"] = np.ascontiguousarray(np.broadcast_to(f("c_lambda").reshape(L, 1, 256), (L, 128, 256)))
    w["c_sub"] = np.ascontiguousarray(f("c_subln_g").reshape(L, 128, 1))
    return w


def per_core_inputs(inp, cfg, core, shared):
    L, SL = cfg.L, cfg.SL
    s = core % inp["x_sample"].shape[0]
    xp = np.asarray(inp["x_prompt"], np.float32)[NPS * core:NPS * (core + 1)].reshape(cfg.TP, D)
    xs = np.asarray(inp["x_sample"], np.float32)[s]
    x = np.concatenate([xp, xs], axis=0)
    m = dict(shared)
    m["xT_in"] = np.ascontiguousarray(x.T).reshape(NCH, 128, cfg.T)
    cond = np.stack([np.asarray(inp["c_ctx"], np.float32), np.asarray(inp["c"], np.float32)[s]], axis=1)
    m["condT"] = np.ascontiguousarray(cond.reshape(NCH, 128, 2).transpose(1, 0, 2))
    ck = np.asarray(inp["cache_k"], np.float32)[s]
    m["cache_kT"] = np.ascontiguousarray(ck.transpose(0, 2, 3, 1))
    cv = np.asarray(inp["cache_v"], np.float32)[s]
    m["cache_v"] = np.ascontiguousarray(cv.reshape(L, cfg.PAST // 128, 128, 512))
    return m


def run(inp, cfg, n_cores):
    nc = build_program(cfg)
    shared = layout_weights(inp, cfg)
    shared.update(make_consts(cfg))
    in_maps = [per_core_inputs(inp, cfg, c, shared) for c in range(n_cores)]
    res = run_bass_kernel_spmd(nc, in_maps, core_ids=list(range(n_cores)))
    return res.results


def assemble(results, cfg, n_cores, n_samples):
    L = cfg.L
    yp, ys, nk, nv = [], [None] * n_samples, [], []
    for c in range(n_cores):
        r = results[c]
        y = np.asarray(r["yT"]).reshape(D, cfg.T).T
        yp.append(y[:cfg.TP].reshape(NPS, PSEQ, D))
        if c < n_samples:
            ys[c] = y[cfg.TP:]
        k = np.asarray(r["newkT"]).transpose(3, 0, 1, 2).reshape(NPS, PSEQ, L, 4, 128).transpose(0, 2, 1, 3, 4)
        nk.append(k)
        v = np.asarray(r["newv"]).reshape(L, NPS, PSEQ, 4, 128).transpose(1, 0, 2, 3, 4)
        nv.append(v)
    return (np.ascontiguousarray(np.concatenate(yp, 0), dtype=np.float32),
            np.ascontiguousarray(np.stack(ys, 0), dtype=np.float32),
            np.ascontiguousarray(np.concatenate(nk, 0), dtype=np.float32),
            np.ascontiguousarray(np.concatenate(nv, 0), dtype=np.float32))


def kernel(**inputs):
    cfg = Cfg(L=4, SL=4096, PAST=512)
    results = run(inputs, cfg, 8)
    return assemble(results, cfg, 8, 4)
```

```python
import math
from contextlib import ExitStack
import numpy as np
import ml_dtypes
import concourse.bass as bass
import concourse.mybir as mybir
from concourse.bass_utils import run_bass_kernel_spmd

F32 = mybir.dt.float32
BF16 = mybir.dt.bfloat16
AF = mybir.ActivationFunctionType
ALU = mybir.AluOpType
AX = mybir.AxisListType

D = 2048
NCH = 16
DFF = 5632
NFC = 44
GRID_W = 64
EPS = 1e-6
TT = 1024
PSEQ = 256
NPS = 4
CONV_W = 31
HALO = 15


class Tk:
    __slots__ = ("name", "w", "rs", "excl")

    def __init__(self, name="", excl=False):
        self.name = name
        self.w = None
        self.rs = []
        self.excl = excl


class FW:
    ENG = ("pe", "act", "dve", "pool", "sp")

    def __init__(self, nc, stack, n_dma_sems=24, same_engine_sync=True):
        self.nc = nc
        self.e = {"pe": nc.tensor, "act": nc.scalar, "dve": nc.vector, "pool": nc.gpsimd, "sp": nc.sync}
        self.sem = {}
        self.cnt = {}
        for k in self.ENG:
            self.sem[k] = stack.enter_context(nc.semaphore("s_" + k))
            self.cnt[k] = 0
        self.dq = {}
        for q in ("sp", "pool"):
            sems = [stack.enter_context(nc.semaphore("d_%s%d" % (q, i))) for i in range(n_dma_sems)]
            self.dq[q] = {"sems": sems, "n": 0}
            for i, s in enumerate(sems):
                self.sem[("d", q, i)] = s
        self.seen = {k: {} for k in self.ENG}
        self.same = same_engine_sync
        self.nwait = 0
        self.nins = 0
        self.stopped = False

    def _deps(self, reads, writes):
        d = {}
        for t in reads:
            if t.w is not None:
                k, v = t.w
                if d.get(k, 0) < v:
                    d[k] = v
            if t.excl:
                for (k, v) in t.rs:
                    if d.get(k, 0) < v:
                        d[k] = v
        for t in writes:
            if t.w is not None:
                k, v = t.w
                if d.get(k, 0) < v:
                    d[k] = v
            for (k, v) in t.rs:
                if d.get(k, 0) < v:
                    d[k] = v
        return d

    def _wait(self, eng, deps, nosame=False):
        seen = self.seen[eng]
        for k, v in deps.items():
            if k == eng and (eng == "pe" or not self.same or nosame):
                continue
            if seen.get(k, 0) >= v:
                continue
            self.e[eng].wait_ge(self.sem[k], v)
            self.nwait += 1
            seen[k] = v

    def _commit(self, key, val, reads, writes):
        for t in writes:
            t.w = (key, val)
            t.rs = []
        for t in reads:
            if t.excl:
                t.w = (key, val)
                t.rs = []
                continue
            t.rs.append((key, val))
            if len(t.rs) > 48:
                m = {}
                for (k, v) in t.rs:
                    if m.get(k, 0) < v:
                        m[k] = v
                t.rs = list(m.items())

    def op(self, eng, fn, reads=(), writes=(), nosame=False):
        if self.stopped:
            return
        self._wait(eng, self._deps(reads, writes), nosame)
        ins = fn(self.e[eng])
        self.cnt[eng] += 1
        ins.then_inc(self.sem[eng], 1)
        self.nins += 1
        self._commit(eng, self.cnt[eng], reads, writes)

    def pe_group(self, fns, reads=(), writes=()):
        if self.stopped:
            return
        self._wait("pe", self._deps(reads, writes))
        ins = None
        for fn in fns:
            ins = fn(self.e["pe"])
            self.nins += 1
        self.cnt["pe"] += 1
        ins.then_inc(self.sem["pe"], 1)
        self._commit("pe", self.cnt["pe"], reads, writes)

    def dma(self, q, out, in_, reads=(), writes=()):
        if self.stopped:
            return
        dq = self.dq[q]
        n = dq["n"]
        ns = len(dq["sems"])
        slot = n % ns
        rnd = n // ns
        key = ("d", q, slot)
        deps = self._deps(reads, writes)
        if rnd > 0 and deps.get(key, 0) < 16 * rnd:
            deps[key] = 16 * rnd
        self._wait(q, deps)
        ins = self.e[q].dma_start(out=out, in_=in_)
        ins.then_inc(self.sem[key], 16)
        dq["n"] = n + 1
        self.nins += 1
        self._commit(key, 16 * (rnd + 1), reads, writes)

    def barrier(self):
        if self.stopped:
            return
        deps = {}
        for k in self.ENG:
            if self.cnt[k] > 0:
                deps[k] = self.cnt[k]
        for q, dq in self.dq.items():
            ns = len(dq["sems"])
            for slot in range(ns):
                uses = (dq["n"] - slot + ns - 1) // ns if dq["n"] > slot else 0
                if uses > 0:
                    deps[("d", q, slot)] = 16 * uses
        for k in self.ENG:
            self._wait(k, dict(deps))

    def drain_all(self):
        if self.stopped:
            return
        deps = {}
        for k in self.ENG:
            if k != "sp" and self.cnt[k] > 0:
                deps[k] = self.cnt[k]
        for q, dq in self.dq.items():
            ns = len(dq["sems"])
            for slot in range(ns):
                uses = (dq["n"] - slot + ns - 1) // ns if dq["n"] > slot else 0
                if uses > 0:
                    deps[("d", q, slot)] = 16 * uses
        self._wait("sp", deps)


class Ring:
    def __init__(self, bufs):
        self.bufs = bufs
        self.i = 0

    def next(self):
        b = self.bufs[self.i % len(self.bufs)]
        self.i += 1
        return b


class _Stop(Exception):
    pass


class Cfg:
    def __init__(self, L=4, SL=4096, PAST=512, stop=None):
        self.stop = stop
        self.L = L
        self.SL = SL
        self.PAST = PAST
        self.TP = NPS * PSEQ
        self.T = self.TP + SL
        self.NT = self.T // TT
        self.NKC = (PAST + SL) // 128
        self.FPB = 256
        self.NPB = SL // self.FPB
        self.NQC = SL // 128


def lambda_init(l):
    return 0.8 - 0.6 * math.exp(-0.3 * l)


def build_program(cfg):
    L, SL, PAST, T, NT = cfg.L, cfg.SL, cfg.PAST, cfg.T, cfg.NT
    nc = bass.Bass("TRN2", target_bir_lowering=False)

    def din(name, shape, dt=F32):
        return nc.dram_tensor(name, list(shape), dt, kind="ExternalInput").ap()

    def dout(name, shape, dt=F32):
        return nc.dram_tensor(name, list(shape), dt, kind="ExternalOutput").ap()

    def dscr(name, shape, dt):
        kind = "ExternalOutput" if getattr(cfg, "debug", False) else "Internal"
        return nc.dram_tensor(name, list(shape), dt, kind=kind).ap()

    xT_in = din("xT_in", [NCH, 128, T])
    condT = din("condT", [128, NCH, 2])
    w_mod = din("w_mod", [L, 36, 128, 4, NCH, 128])
    b_modT = din("b_modT", [L, 128, 144])
    norm_gT = din("norm_gT", [L, 128, 3, NCH])
    w_up = [din("w_up%d" % i, [L, NFC, 128, 2, NCH, 128]) for i in range(2)]
    w_dn = [din("w_dn%d" % i, [L, NCH, 128, NFC, 128]) for i in range(2)]
    w_inf = din("w_inf", [L, 24, 128, NCH, 128])
    w_int = din("w_int", [L, 2, 128, NCH, 512])
    w_out = din("w_out", [L, NCH, 128, NCH, 128])
    b_pw = din("b_pw", [L, 4, 128, 4, 128])
    d_lin = din("d_lin", [L, 4, 128, 4, 128])
    a_ng = din("a_ng", [L, 128, 512])
    a_wsT = din("a_wsT", [L, 128, 4, 128])
    a_bs = din("a_bs", [L, 128, 512])
    b_cw = din("b_cw", [L, 128, 4, CONV_W])
    b_vec = din("b_vec", [L, 128, 3, 4])
    c_g = din("c_g", [L, 128, 2])
    c_lam = din("c_lam", [L, 128, 256])
    c_sub = din("c_sub", [L, 128, 1])
    cache_kT = din("cache_kT", [L, 4, 128, PAST])
    cache_v = din("cache_v", [L, PAST // 128, 128, 512])
    ropeT = din("ropeT", [128, 2, SL])
    permM = din("permM", [128, 128], BF16)
    onesM = din("onesM", [128, 128], BF16)
    blkM = din("blkM", [128, 128], BF16)
    csC = din("csC", [128, 256], BF16)
    dftP = din("dftP", [128, 2, 2, PSEQ], BF16)
    dftS = din("dftS", [cfg.NPB, 128, 2, cfg.NQC, cfg.FPB], BF16)
    yT = dout("yT", [NCH, 128, T])
    newkT = dout("newkT", [L, 4, 128, cfg.TP])
    newv = dout("newv", [L, cfg.TP // 128, 128, 512])
    XS = [dscr("xs%d" % i, [NCH, 128, T], F32) for i in range(3)]
    catA = dscr("catA", [4, 128, T], BF16)
    gluS2 = [dscr("gluS%d" % i, [4, 128, T], BF16) for i in range(2)]
    qS = dscr("qS", [4, 128, T], BF16)
    kS = dscr("kS", [4, 128, T], BF16)
    vS = dscr("vS", [T // 128, 128, 512], BF16)
    dxS = dscr("dxS", [4, 128, T], BF16)
    catC = dscr("catC", [4, 128, T], BF16)
    fS = dscr("fS", [4, 128, T], BF16)

    _uid = [0]

    def uname(name):
        _uid[0] += 1
        return "%s_%d" % (name, _uid[0])

    def chk(name):
        if cfg.stop == name:
            fw.drain_all()
            fw.stopped = True

    dbg_st2 = dout("dbg_st2", [T // 128, 128, 2]) if getattr(cfg, "debug", False) else None
    dbg_cat = dout("dbg_cat", [NT, 128, NCH, TT], BF16) if getattr(cfg, "debug", False) else None
    dbg_acc = dout("dbg_acc", [NT, 128, 4, TT]) if getattr(cfg, "debug", False) else None
    dbg_ln = dout("dbg_ln", [NT, 2, 128, TT]) if getattr(cfg, "debug", False) else None
    with ExitStack() as st:
      fw = FW(nc, st)
      if True:

        def sb(name, shape, dt):
            return st.enter_context(nc.sbuf_tensor(name, list(shape), dt))

        def tile_tks(prefix, n):
            return [Tk("%s%d" % (prefix, i)) for i in range(n)]

        tk_xin = tile_tks("xin", NT)
        tk_XS = [tile_tks("xs%d_" % i, NT) for i in range(3)]
        tk_yT = tile_tks("y", NT)
        tk_catA = tile_tks("catA", NT)
        tk_glu2 = [tile_tks("glu%d_" % i, NT) for i in range(2)]
        tk_q = tile_tks("q", NT)
        tk_k = tile_tks("k", NT)
        tk_v = tile_tks("v", NT)
        tk_dx = tile_tks("dx", NT)
        tk_catC = tile_tks("catC", NT)
        tk_f = tile_tks("f", NT)
        tk_out_misc = Tk("outmisc")

        ps_bufs = []
        for i in range(4):
            p = st.enter_context(nc.psum_tensor("ps%d" % i, [128, 1024], F32))
            ps_bufs.append((p, Tk("ps%d" % i, excl=True)))
        PS = Ring(ps_bufs)

        ones = sb("ones", [128, 128], BF16); tk_c = Tk("consts")
        blk = sb("blk", [128, 128], BF16)
        perm = sb("perm", [128, 128], BF16)
        csc = sb("csc", [128, 256], BF16)
        dftp = sb("dftp", [128, 2, 2, PSEQ], BF16)
        fw.dma("sp", ones[:], onesM, writes=[tk_c])
        fw.dma("sp", blk[:], blkM, writes=[tk_c])
        fw.dma("sp", perm[:], permM, writes=[tk_c])
        fw.dma("sp", csc[:], csC, writes=[tk_c])
        fw.dma("sp", dftp[:], dftP, writes=[tk_c])

        chk("c0")
        modT = sb("modT", [128, L, 144, 2], F32); tk_mod = Tk("mod")
        gm = sb("gm", [128, L, 3, NCH, 2], F32)
        gate = sb("gate", [128, L, 3, NCH, 2], F32)
        cwt = sb("cwt", [128, L, 4, CONV_W], F32)
        bvt = sb("bvt", [128, L, 3, 4], F32)
        cgt = sb("cgt", [128, L, 2], F32)
        cst = sb("cst", [128, L, 1], F32)
        nlam = sb("nlam", [128, L], F32)
        tk_par = Tk("params")
        for l in range(L):
            fw.dma("sp", cwt[:, l], b_cw[l], writes=[tk_par])
            fw.dma("sp", bvt[:, l], b_vec[l], writes=[tk_par])
            fw.dma("sp", cgt[:, l], c_g[l], writes=[tk_par])
            fw.dma("sp", cst[:, l], c_sub[l], writes=[tk_par])

        with ExitStack() as ps_:
            def sbp(name, shape, dt):
                return ps_.enter_context(nc.sbuf_tensor(uname(name), list(shape), dt))
            ngt = sbp("ngt", [128, L, 3, NCH], F32)
            bmt = sbp("bmt", [128, L, 144], F32)
            clt = sbp("clt", [128, L, 256], F32)
            for l in range(L):
                fw.dma("sp", ngt[:, l], norm_gT[l], writes=[tk_par])
                fw.dma("sp", bmt[:, l], b_modT[l], writes=[tk_par])
                fw.dma("sp", clt[:, l], c_lam[l], writes=[tk_par])
            cnd = sbp("cnd", [128, NCH, 2], F32); tk_cnd = Tk()
            scb = sbp("scb", [128, NCH, 2], BF16); tk_scb = Tk()
            fw.dma("sp", cnd[:], condT, writes=[tk_cnd])
            fw.op("act", lambda e: e.activation(out=scb[:], in_=cnd[:], func=AF.Silu), reads=[tk_cnd], writes=[tk_scb])
            wm_ring = Ring([(sbp("wm%d" % i, [128, 4, NCH, 128], BF16), Tk()) for i in range(3)])
            for l in range(L):
                mp, tk_mp = PS.next()
                mpv = mp[:, 0:288].rearrange("p (n c) -> p n c", c=2)
                for g in range(36):
                    wm, tk_wm = wm_ring.next()
                    fw.dma("pool", wm[:], w_mod[l, g], writes=[tk_wm])
                    for j in range(4):
                        n = g * 4 + j
                        fw.pe_group([
                            (lambda e, kc=kc, j=j, n=n: e.matmul(mpv[:, n, :], lhsT=wm[:, j, kc, :], rhs=scb[:, kc, :],
                                                                  start=(kc == 0), stop=(kc == NCH - 1)))
                            for kc in range(NCH)], reads=[tk_wm, tk_scb], writes=[tk_mp])
                for cidx in range(2):
                    fw.op("dve", lambda e, cidx=cidx: e.tensor_tensor(out=modT[:, l, :, cidx], in0=mpv[:, :, cidx], in1=bmt[:, l, :], op=ALU.add),
                          reads=[tk_mp, tk_par], writes=[tk_mod])
                for i in range(3):
                    for cidx in range(2):
                        fw.op("dve", lambda e, i=i, cidx=cidx: e.scalar_tensor_tensor(
                            out=gm[:, l, i, :, cidx], in0=modT[:, l, (3 * i + 1) * 16:(3 * i + 2) * 16, cidx], scalar=1.0,
                            in1=ngt[:, l, i, :], op0=ALU.add, op1=ALU.mult), reads=[tk_mod, tk_par], writes=[tk_mod])
                        gs = 1.0 if i == 1 else 0.5
                        fw.op("dve", lambda e, i=i, cidx=cidx, gs=gs: e.tensor_scalar(
                            out=gate[:, l, i, :, cidx], in0=modT[:, l, (3 * i + 2) * 16:(3 * i + 3) * 16, cidx], scalar1=gs, scalar2=None,
                            op0=ALU.mult), reads=[tk_mod], writes=[tk_mod])
                lt = sbp("lt%d" % l, [128, 2, 64], F32); tk_lt = Tk()
                ls = sbp("ls%d" % l, [128, 2], F32)
                for r in range(2):
                    fw.op("dve", lambda e, r=r: e.tensor_tensor(out=lt[:, r, :], in0=clt[:, l, (2 * r) * 64:(2 * r + 1) * 64],
                                                                 in1=clt[:, l, (2 * r + 1) * 64:(2 * r + 2) * 64], op=ALU.mult),
                          reads=[tk_par], writes=[tk_lt])
                    fw.op("dve", lambda e, r=r: e.reduce_sum(out=ls[:, r:r + 1], in_=lt[:, r, :], axis=AX.X), reads=[tk_lt], writes=[tk_lt])
                fw.op("act", lambda e: e.activation(out=ls[:], in_=ls[:], func=AF.Exp), reads=[tk_lt], writes=[tk_lt])
                fw.op("dve", lambda e, l=l: e.scalar_tensor_tensor(out=nlam[:, l:l + 1], in0=ls[:, 1:2], scalar=-lambda_init(l),
                                                                    in1=ls[:, 0:1], op0=ALU.add, op1=ALU.subtract),
                      reads=[tk_lt], writes=[tk_mod])

        chk("pro")
        fw.barrier()
        def cond_of(t):
            return 0 if t == 0 else 1

        def x_src(buf_idx):
            if buf_idx == "in":
                return xT_in, tk_xin
            if buf_idx == "out":
                return yT, tk_yT
            return XS[buf_idx], tk_XS[buf_idx]

        def norm_mod(env, src, t, l, i):
            sap, stk = x_src(src)
            cidx = cond_of(t)
            tok = slice(t * TT, (t + 1) * TT)
            hT, tk_h = env["hT"]
            ssq, tk_ssq = PS.next()
            for c in range(NCH):
                xc, tk_xc = env["xring"].next()
                fw.dma("sp", xc[:], sap[c, :, tok], reads=[stk[t]], writes=[tk_xc])
                sq, tk_sq = env["sqring"].next()
                fw.op("act", lambda e: e.activation(out=sq[:], in_=xc[:], func=AF.Square), reads=[tk_xc], writes=[tk_sq])
                fw.pe_group([(lambda e, h=h: e.matmul(ssq[:, h * 512:(h + 1) * 512], lhsT=ones[:], rhs=sq[:, h * 512:(h + 1) * 512],
                                                        start=(c == 0), stop=(c == NCH - 1))) for h in range(2)],
                            reads=[tk_sq, tk_c], writes=[tk_ssq])
            rstd, tk_rstd = env["rstd"]
            fw.op("act", lambda e: e.activation(out=rstd[:], in_=ssq[:], func=AF.Sqrt, bias=EPS, scale=1.0 / D), reads=[tk_ssq], writes=[tk_rstd])
            fw.op("dve", lambda e: e.reciprocal(out=rstd[:], in_=rstd[:]), reads=[tk_rstd], writes=[tk_rstd])
            for c in range(NCH):
                xc, tk_xc = env["xring"].next()
                fw.dma("sp", xc[:], sap[c, :, tok], reads=[stk[t]], writes=[tk_xc])
                tb, tk_tb = env["tring"].next()
                fw.op("dve", lambda e: e.scalar_tensor_tensor(out=tb[:], in0=xc[:], scalar=gm[:, l, i, c, cidx:cidx + 1], in1=rstd[:],
                                                              op0=ALU.mult, op1=ALU.mult), reads=[tk_xc, tk_rstd, tk_mod], writes=[tk_tb])
                sh = modT[:, l, (3 * i) * 16 + c, cidx:cidx + 1]
                fw.op("act", lambda e: e.activation(out=hT[:, c, :], in_=tb[:], func=AF.Identity, bias=sh, scale=1.0),
                      reads=[tk_tb, tk_mod], writes=[tk_h])

        def ffn(env, src, dst, t, l, which):
            cidx = cond_of(t)
            tok = slice(t * TT, (t + 1) * TT)
            hT, tk_h = env["hT"]
            mid, tk_mid = env["mid"]
            gi = 0 if which == 0 else 2
            for j in range(NFC):
                wu, tk_wu = env["wu"].next()
                fw.dma("pool", wu[:], w_up[which][l, j], writes=[tk_wu])
                pa, tk_pa = PS.next()
                pg, tk_pg = PS.next()
                for (ag, pp, tkp) in ((0, pa, tk_pa), (1, pg, tk_pg)):
                    for h in range(2):
                        fw.pe_group([(lambda e, kc=kc, ag=ag, pp=pp, h=h: e.matmul(
                            pp[:, h * 512:(h + 1) * 512], lhsT=wu[:, ag, kc, :], rhs=hT[:, kc, h * 512:(h + 1) * 512],
                            start=(kc == 0), stop=(kc == NCH - 1))) for kc in range(NCH)],
                            reads=[tk_wu, tk_h], writes=[tkp])
                sg, tk_sg = env["sgring"].next()
                fw.op("act", lambda e: e.activation(out=sg[:], in_=pg[:], func=AF.Silu), reads=[tk_pg], writes=[tk_sg])
                fw.op("dve", lambda e: e.tensor_tensor(out=mid[:, j, :], in0=pa[:], in1=sg[:], op=ALU.mult),
                      reads=[tk_pa, tk_sg], writes=[tk_mid])
            return residual_proj(env, src, dst, t, l, gi, w_dn[which], mid, tk_mid, NFC, env["wd"], xb=hT, tk_xb=tk_h)

        def residual_proj(env, src, dst, t, l, gi, wdram, act, tk_act, nk, wring, xb=None, tk_xb=None):
            cidx = cond_of(t)
            tok = slice(t * TT, (t + 1) * TT)
            sap, stk = x_src(src)
            dap, dtk = x_src(dst)
            carry = xb is not None
            ring3 = Ring(ps_bufs[0:3]) if carry else PS
            ssq, tk_ssq = ps_bufs[3]
            for c in range(NCH):
                wd, tk_wd = wring.next()
                fw.dma("pool", wd[:, 0:nk, :], wdram[l, c], writes=[tk_wd])
                pp, tk_pp = ring3.next()
                for h in range(2):
                    fw.pe_group([(lambda e, fc=fc, h=h: e.matmul(pp[:, h * 512:(h + 1) * 512], lhsT=wd[:, fc, :],
                                                                  rhs=act[:, fc, h * 512:(h + 1) * 512],
                                                                  start=(fc == 0), stop=(fc == nk - 1))) for fc in range(nk)],
                                reads=[tk_wd, tk_act], writes=[tk_pp])
                xc, tk_xc = env["xring"].next()
                fw.dma("sp", xc[:], sap[c, :, tok], reads=[stk[t]], writes=[tk_xc])
                xo, tk_xo = env["tring"].next()
                fw.op("dve", lambda e: e.scalar_tensor_tensor(out=xo[:], in0=pp[:], scalar=gate[:, l, gi, c, cidx:cidx + 1], in1=xc[:],
                                                              op0=ALU.mult, op1=ALU.add), reads=[tk_pp, tk_xc, tk_mod], writes=[tk_xo])
                fw.dma("sp", dap[c, :, tok], xo[:], reads=[tk_xo], writes=[dtk[t]])
                if carry:
                    sq, tk_sq = env["sqring"].next()
                    fw.op("act", lambda e: e.activation(out=sq[:], in_=xo[:], func=AF.Square), reads=[tk_xo], writes=[tk_sq])
                    fw.op("act", lambda e: e.activation(out=xb[:, c, :], in_=xo[:], func=AF.Copy), reads=[tk_xo], writes=[tk_xb])
                    fw.pe_group([(lambda e, h=h: e.matmul(ssq[:, h * 512:(h + 1) * 512], lhsT=ones[:], rhs=sq[:, h * 512:(h + 1) * 512],
                                                            start=(c == 0), stop=(c == NCH - 1))) for h in range(2)],
                                reads=[tk_sq, tk_c], writes=[tk_ssq])
            return (ssq, tk_ssq) if carry else None

        def norm_sb(env, xb, tk_xb, ssq_pair, t, l, i):
            ssq, tk_ssq = ssq_pair
            cidx = cond_of(t)
            hT, tk_h = env["hT"]
            rstd, tk_rstd = env["rstd"]
            fw.op("act", lambda e: e.activation(out=rstd[:], in_=ssq[:], func=AF.Sqrt, bias=EPS, scale=1.0 / D), reads=[tk_ssq], writes=[tk_rstd])
            fw.op("dve", lambda e: e.reciprocal(out=rstd[:], in_=rstd[:]), reads=[tk_rstd], writes=[tk_rstd])
            tk_hc = Tk("hchunks")
            inplace = tk_xb is tk_h
            for c in range(NCH):
                tb, tk_tb = env["tring"].next()
                fw.op("dve", lambda e: e.scalar_tensor_tensor(out=tb[:], in0=xb[:, c, :], scalar=gm[:, l, i, c, cidx:cidx + 1], in1=rstd[:],
                                                              op0=ALU.mult, op1=ALU.mult), reads=[tk_xb, tk_rstd, tk_mod], writes=[tk_tb])
                sh = modT[:, l, (3 * i) * 16 + c, cidx:cidx + 1]
                wr = [tk_hc, tk_h] if (c == 0 and not inplace) else [tk_hc]
                fw.op("act", lambda e: e.activation(out=hT[:, c, :], in_=tb[:], func=AF.Identity, bias=sh, scale=1.0),
                      reads=[tk_tb, tk_mod], writes=wr)
            if not fw.stopped:
                tk_h.w = ("act", fw.cnt["act"])
                tk_h.rs = []

        def in_proj(env, t, l):
            tok = slice(t * TT, (t + 1) * TT)
            is_prompt = (t == 0)
            hT, tk_h = env["hT"]
            uT, tk_u = env["uT"]
            stg, tk_stg = env["stage"]

            def proj_chunk(n):
                wt, tk_wt = env["wi"].next()
                fw.dma("pool", wt[:], w_inf[l, n], writes=[tk_wt])
                pp, tk_pp = PS.next()
                for h in range(2):
                    fw.pe_group([(lambda e, kc=kc, h=h: e.matmul(pp[:, h * 512:(h + 1) * 512], lhsT=wt[:, kc, :],
                                                                  rhs=hT[:, kc, h * 512:(h + 1) * 512],
                                                                  start=(kc == 0), stop=(kc == NCH - 1))) for kc in range(NCH)],
                                reads=[tk_wt, tk_h], writes=[tk_pp])
                return pp, tk_pp

            for j in range(4):
                pp, tk_pp = proj_chunk(j)
                fw.op("act", lambda e: e.activation(out=uT[:, j, :], in_=pp[:], func=AF.Gelu_apprx_tanh), reads=[tk_pp], writes=[tk_u])
            chk("ipA1")
            wav, tk_wav = env["wtok"].next()
            fw.dma("pool", wav[:], w_int[l, 0], writes=[tk_wav])
            ang, tk_ang = env["ang"]
            fw.dma("sp", ang[:], a_ng[l], writes=[tk_ang])
            wsT, tk_ws = env["wsT"]
            fw.dma("pool", wsT[:], a_wsT[l], writes=[tk_ws])
            bsr, tk_bs = env["bsr"]
            fw.dma("pool", bsr[:], a_bs[l], writes=[tk_bs])
            for tb in range(TT // 128):
                tsl = slice(tb * 128, (tb + 1) * 128)
                pp, tk_pp = PS.next()
                fw.pe_group([(lambda e, kc=kc: e.matmul(pp[:, 0:512], lhsT=hT[:, kc, tsl], rhs=wav[:, kc, :],
                                                         start=(kc == 0), stop=(kc == NCH - 1))) for kc in range(NCH)],
                            reads=[tk_wav, tk_h], writes=[tk_pp])
                gv, tk_gv = env["gv"].next()
                fw.op("act", lambda e: e.activation(out=gv[:], in_=pp[:, 0:512], func=AF.Gelu_apprx_tanh), reads=[tk_pp], writes=[tk_gv])
                jk, tk_jk = env["junk"]
                st2, tk_st2 = env["st2"].next()
                fw.op("act", lambda e: e.activation(out=jk[:, 0:512], in_=gv[:], func=AF.Square, accum_out=st2[:, 0:1]),
                      reads=[tk_gv], writes=[tk_jk, tk_st2])
                fw.op("act", lambda e: e.activation(out=st2[:, 1:2], in_=st2[:, 0:1], func=AF.Sqrt, bias=EPS, scale=1.0 / 512),
                      reads=[tk_st2], writes=[tk_st2])
                fw.op("dve", lambda e: e.reciprocal(out=st2[:, 1:2], in_=st2[:, 1:2]), reads=[tk_st2], writes=[tk_st2])
                if dbg_st2 is not None:
                    fw.dma("sp", dbg_st2[t * 8 + tb], st2[:], reads=[tk_st2], writes=[tk_out_misc])
                vn, tk_vn = env["vn"].next()
                fw.op("dve", lambda e: e.scalar_tensor_tensor(out=vn[:], in0=gv[:], scalar=st2[:, 1:2], in1=ang[:],
                                                              op0=ALU.mult, op1=ALU.mult), reads=[tk_gv, tk_st2, tk_ang], writes=[tk_vn])
                p2 = pp[:, 512:1024]
                for hh in range(4):
                    fw.pe_group([
                        (lambda e, hh=hh: e.matmul(p2[:, hh * 128:(hh + 1) * 128], lhsT=vn[:, hh * 128:(hh + 1) * 128], rhs=wsT[:, hh, :],
                                                   start=True, stop=False)),
                        (lambda e, hh=hh: e.matmul(p2[:, hh * 128:(hh + 1) * 128], lhsT=ones[:, :], rhs=bsr[:, hh * 128:(hh + 1) * 128],
                                                   start=False, stop=True)),
                    ], reads=[tk_vn, tk_ws, tk_bs, tk_c], writes=[tk_pp])
                fw.op("dve", lambda e: e.tensor_tensor(out=stg[:, :, tsl], in0=uT[:, :, tsl],
                                                       in1=p2.rearrange("p (h q) -> p h q", h=4), op=ALU.mult),
                      reads=[tk_pp, tk_u], writes=[tk_stg])
            for j in range(4):
                fw.dma("sp", catA[j, :, tok], stg[:, j, :], reads=[tk_stg], writes=[tk_catA[t]])
            chk("ipA2")
            for j in range(4):
                pa, tk_pa = proj_chunk(4 + j)
                pg, tk_pg = proj_chunk(8 + j)
                sg, tk_sg = env["sgring"].next()
                fw.op("act", lambda e: e.activation(out=sg[:], in_=pg[:], func=AF.Sigmoid), reads=[tk_pg], writes=[tk_sg])
                ob, tk_ob = env["obring"].next()
                fw.op("dve", lambda e: e.tensor_tensor(out=ob[:], in0=pa[:], in1=sg[:], op=ALU.mult), reads=[tk_pa, tk_sg], writes=[tk_ob])
                fw.dma("sp", gluS2[l % 2][j, :, tok], ob[:], reads=[tk_ob], writes=[tk_glu2[l % 2][t]])
            chk("ipB")
            if not is_prompt:
                rp, tk_rp = env["rope"]
                s0 = t * TT - cfg.TP
                fw.dma("sp", rp[:], ropeT[:, :, s0:s0 + TT], writes=[tk_rp])
            for qk in range(2):
                for hh in range(4):
                    pp, tk_pp = proj_chunk(12 + qk * 4 + hh)
                    sq, tk_sq = env["sqring"].next()
                    fw.op("act", lambda e: e.activation(out=sq[:], in_=pp[:], func=AF.Square), reads=[tk_pp], writes=[tk_sq])
                    p2, tk_p2 = PS.next()
                    fw.pe_group([(lambda e, h=h: e.matmul(p2[:, h * 512:(h + 1) * 512], lhsT=blk[:], rhs=sq[:, h * 512:(h + 1) * 512],
                                                            start=True, stop=True)) for h in range(2)], reads=[tk_sq, tk_c], writes=[tk_p2])
                    rs, tk_rs = env["tring"].next()
                    fw.op("act", lambda e: e.activation(out=rs[:], in_=p2[:], func=AF.Sqrt, bias=EPS, scale=1.0 / 64), reads=[tk_p2], writes=[tk_rs])
                    fw.op("dve", lambda e: e.reciprocal(out=rs[:], in_=rs[:]), reads=[tk_rs], writes=[tk_rs])
                    qn, tk_qn = env["tring"].next()
                    fw.op("dve", lambda e: e.scalar_tensor_tensor(out=qn[:], in0=pp[:], scalar=cgt[:, l, qk:qk + 1], in1=rs[:],
                                                                  op0=ALU.mult, op1=ALU.mult), reads=[tk_pp, tk_rs, tk_par], writes=[tk_qn])
                    ob, tk_ob = env["obring"].next()
                    dstS, dtk = (qS, tk_q) if qk == 0 else (kS, tk_k)
                    if is_prompt:
                        fw.op("act", lambda e: e.activation(out=ob[:], in_=qn[:], func=AF.Copy), reads=[tk_qn], writes=[tk_ob])
                        if qk == 1:
                            fw.dma("sp", newkT[l, hh], qn[:], reads=[tk_qn], writes=[tk_out_misc])
                    else:
                        qb_, tk_qb = env["obring"].next()
                        fw.op("act", lambda e: e.activation(out=qb_[:], in_=qn[:], func=AF.Copy), reads=[tk_qn], writes=[tk_qb])
                        p3, tk_p3 = PS.next()
                        fw.pe_group([(lambda e, h=h: e.matmul(p3[:, h * 512:(h + 1) * 512], lhsT=perm[:], rhs=qb_[:, h * 512:(h + 1) * 512],
                                                                start=True, stop=True)) for h in range(2)], reads=[tk_qb, tk_c], writes=[tk_p3])
                        t1, tk_t1 = env["tring"].next()
                        fw.op("dve", lambda e: e.tensor_tensor(out=t1[:], in0=qn[:], in1=rp[:, 0, :], op=ALU.mult), reads=[tk_qn, tk_rp], writes=[tk_t1])
                        t2, tk_t2 = env["tring"].next()
                        fw.op("dve", lambda e: e.tensor_tensor(out=t2[:], in0=p3[:], in1=rp[:, 1, :], op=ALU.mult), reads=[tk_p3, tk_rp], writes=[tk_t2])
                        fw.op("dve", lambda e: e.tensor_tensor(out=ob[:], in0=t1[:], in1=t2[:], op=ALU.add), reads=[tk_t1, tk_t2], writes=[tk_ob])
                    fw.dma("sp", dstS[hh, :, tok], ob[:], reads=[tk_ob], writes=[dtk[t]])
            chk("ipC")
            wcv, tk_wcv = env["wtok"].next()
            fw.dma("pool", wcv[:], w_int[l, 1], writes=[tk_wcv])
            for tb in range(TT // 128):
                tsl = slice(tb * 128, (tb + 1) * 128)
                pp, tk_pp = PS.next()
                fw.pe_group([(lambda e, kc=kc: e.matmul(pp[:, 0:512], lhsT=hT[:, kc, tsl], rhs=wcv[:, kc, :],
                                                         start=(kc == 0), stop=(kc == NCH - 1))) for kc in range(NCH)],
                            reads=[tk_wcv, tk_h], writes=[tk_pp])
                vn, tk_vn = env["vn"].next()
                fw.op("act", lambda e: e.activation(out=vn[:], in_=pp[:, 0:512], func=AF.Copy), reads=[tk_pp], writes=[tk_vn])
                fw.dma("sp", vS[t * (TT // 128) + tb], vn[:], reads=[tk_vn], writes=[tk_v[t]])
                if is_prompt:
                    gv, tk_gv = env["gv"].next()
                    fw.op("dve", lambda e: e.tensor_copy(out=gv[:], in_=pp[:, 0:512]), reads=[tk_pp], writes=[tk_gv])
                    fw.dma("sp", newv[l, tb], gv[:], reads=[tk_gv], writes=[tk_out_misc])
            chk("ipV")
            for j in range(4):
                pp, tk_pp = proj_chunk(20 + j)
                ob, tk_ob = env["obring"].next()
                fw.op("act", lambda e: e.activation(out=ob[:], in_=pp[:], func=AF.Copy), reads=[tk_pp], writes=[tk_ob])
                fw.dma("sp", dxS[j, :, tok], ob[:], reads=[tk_ob], writes=[tk_dx[t]])

        def out_mix(env, t, l):
            tok = slice(t * TT, (t + 1) * TT)
            is_prompt = (t == 0)
            catT, tk_cat = env["hT"]
            gluS = gluS2[l % 2]
            tk_glu = tk_glu2[l % 2]
            for j in range(4):
                fw.dma("sp", catT[:, j, :], catA[j, :, tok], reads=[tk_catA[t]], writes=[tk_cat])
                fw.dma("sp", catT[:, 8 + j, :], catC[j, :, tok], reads=[tk_catC[t]], writes=[tk_cat])
            fT, tk_fT = env["stage"]
            for j in range(4):
                fw.dma("sp", fT[:, j, :], fS[j, :, tok], reads=[tk_f[t]], writes=[tk_fT])
            for n in range(4):
                wt, tk_wt = env["wsm"].next()
                fw.dma("pool", wt[:], d_lin[l, n], writes=[tk_wt])
                pp, tk_pp = PS.next()
                for h in range(2):
                    fw.pe_group([(lambda e, kc=kc, h=h: e.matmul(pp[:, h * 512:(h + 1) * 512], lhsT=wt[:, kc, :],
                                                                  rhs=fT[:, kc, h * 512:(h + 1) * 512],
                                                                  start=(kc == 0), stop=(kc == 3))) for kc in range(4)],
                                reads=[tk_wt, tk_fT], writes=[tk_pp])
                fw.op("act", lambda e: e.activation(out=catT[:, 12 + n, :], in_=pp[:], func=AF.Copy), reads=[tk_pp], writes=[tk_cat])
            nseg, seglen = (NPS, PSEQ) if is_prompt else (1, TT)
            G, tk_G = env["G"]
            acc, tk_acc = env["acc"]
            Gv = G[:, 0:nseg * (seglen + 2 * HALO)].rearrange("p (s w) -> p s w", s=nseg)
            s1, tk_s1 = PS.next()
            s2, tk_s2 = PS.next()
            for j in range(4):
                fw.op("dve", lambda e: e.memset(G[:], 0.0), writes=[tk_G])
                if is_prompt:
                    for s in range(NPS):
                        fw.dma("sp", Gv[:, s, HALO:HALO + PSEQ], gluS[j, :, s * PSEQ:(s + 1) * PSEQ], reads=[tk_glu[t]], writes=[tk_G])
                else:
                    lo = t * TT - HALO
                    hi = (t + 1) * TT + HALO
                    lo_c = max(lo, cfg.TP)
                    hi_c = min(hi, T)
                    rd = [tk_glu[t]]
                    if t - 1 >= 1:
                        rd.append(tk_glu[t - 1])
                    if t + 1 < NT:
                        rd.append(tk_glu[t + 1])
                    fw.dma("sp", Gv[:, 0, lo_c - lo:hi_c - lo], gluS[j, :, lo_c:hi_c], reads=rd, writes=[tk_G])
                av = acc[:, j, :].rearrange("p (s w) -> p s w", s=nseg)
                fw.op("dve", lambda e: e.tensor_scalar(out=av, in0=Gv[:, :, 0:seglen], scalar1=cwt[:, l, j, 0:1], scalar2=bvt[:, l, 0, j:j + 1],
                                                       op0=ALU.mult, op1=ALU.add), reads=[tk_G, tk_par], writes=[tk_acc])
                for k in range(1, CONV_W):
                    fw.op("dve", lambda e, k=k: e.scalar_tensor_tensor(out=av, in0=Gv[:, :, k:k + seglen], scalar=cwt[:, l, j, k:k + 1], in1=av,
                                                                        op0=ALU.mult, op1=ALU.add), reads=[tk_G, tk_par], writes=[tk_acc], nosame=True)
                cb, tk_cb = env["obring"].next()
                fw.op("act", lambda e: e.activation(out=cb[:], in_=acc[:, j, :], func=AF.Copy), reads=[tk_acc], writes=[tk_cb])
                sq, tk_sq = env["sqring"].next()
                fw.op("act", lambda e: e.activation(out=sq[:], in_=acc[:, j, :], func=AF.Square), reads=[tk_acc], writes=[tk_sq])
                fw.pe_group([(lambda e, h=h: e.matmul(s1[:, h * 512:(h + 1) * 512], lhsT=ones[:], rhs=cb[:, h * 512:(h + 1) * 512],
                                                        start=(j == 0), stop=(j == 3))) for h in range(2)], reads=[tk_cb, tk_c], writes=[tk_s1])
                fw.pe_group([(lambda e, h=h: e.matmul(s2[:, h * 512:(h + 1) * 512], lhsT=ones[:], rhs=sq[:, h * 512:(h + 1) * 512],
                                                        start=(j == 0), stop=(j == 3))) for h in range(2)], reads=[tk_sq, tk_c], writes=[tk_s2])
            mean, tk_mean = env["lnmean"]
            fw.op("act", lambda e: e.activation(out=mean[:], in_=s1[:], func=AF.Copy, scale=1.0 / 512), reads=[tk_s1], writes=[tk_mean])
            msq, tk_msq = env["lnrstd"]
            fw.op("dve", lambda e: e.tensor_tensor(out=msq[:], in0=mean[:], in1=mean[:], op=ALU.mult), reads=[tk_mean], writes=[tk_msq])
            fw.op("dve", lambda e: e.scalar_tensor_tensor(out=msq[:], in0=s2[:], scalar=1.0 / 512, in1=msq[:], op0=ALU.mult, op1=ALU.subtract),
                  reads=[tk_s2, tk_msq], writes=[tk_msq])
            fw.op("act", lambda e: e.activation(out=msq[:], in_=msq[:], func=AF.Sqrt, bias=EPS, scale=1.0), reads=[tk_msq], writes=[tk_msq])
            fw.op("dve", lambda e: e.reciprocal(out=msq[:], in_=msq[:]), reads=[tk_msq], writes=[tk_msq])
            if dbg_acc is not None:
                fw.dma("sp", dbg_acc[t], acc[:], reads=[tk_acc], writes=[tk_out_misc])
                fw.dma("sp", dbg_ln[t, 0], mean[:], reads=[tk_mean], writes=[tk_out_misc])
                fw.dma("sp", dbg_ln[t, 1], msq[:], reads=[tk_msq], writes=[tk_out_misc])
            yb, tk_yb = env["stage"]
            for j in range(4):
                dd, tk_dd = env["tring"].next()
                fw.op("dve", lambda e: e.tensor_tensor(out=dd[:], in0=acc[:, j, :], in1=mean[:], op=ALU.subtract), reads=[tk_acc, tk_mean], writes=[tk_dd])
                fw.op("dve", lambda e: e.tensor_tensor(out=dd[:], in0=dd[:], in1=msq[:], op=ALU.mult), reads=[tk_dd, tk_msq], writes=[tk_dd])
                fw.op("act", lambda e: e.activation(out=yb[:, j, :], in_=dd[:], func=AF.Silu, bias=bvt[:, l, 2, j:j + 1], scale=bvt[:, l, 1, j:j + 1]),
                      reads=[tk_dd, tk_par], writes=[tk_yb])
            for n in range(4):
                wt, tk_wt = env["wsm"].next()
                fw.dma("pool", wt[:], b_pw[l, n], writes=[tk_wt])
                pp, tk_pp = PS.next()
                for h in range(2):
                    fw.pe_group([(lambda e, kc=kc, h=h: e.matmul(pp[:, h * 512:(h + 1) * 512], lhsT=wt[:, kc, :],
                                                                  rhs=yb[:, kc, h * 512:(h + 1) * 512],
                                                                  start=(kc == 0), stop=(kc == 3))) for kc in range(4)],
                                reads=[tk_wt, tk_yb], writes=[tk_pp])
                fw.op("act", lambda e: e.activation(out=catT[:, 4 + n, :], in_=pp[:], func=AF.Copy), reads=[tk_pp], writes=[tk_cat])

        def attention(l):
            with ExitStack() as es:
                def sbm(name, shape, dt):
                    return es.enter_context(nc.sbuf_tensor(uname(name), list(shape), dt))
                nkc_max = cfg.NKC
                Vt = sbm("Vt", [128, nkc_max, 512], BF16); tk_V = Tk()
                KT = sbm("KT", [128, nkc_max * 128], BF16); tk_K = Tk()
                QT = sbm("QT", [128, max(SL, PSEQ)], BF16); tk_Q = Tk()
                pring = Ring([(sbm("pT%d" % i, [128, 2, 512], BF16), Tk()) for i in range(3)])
                rr = sbm("rr", [128, 2, 512], F32); tk_rr = Tk()
                oo = sbm("oo", [128, 2, 512], F32); tk_oo = Tk()
                od = sbm("od", [128, 512], F32); tk_od = Tk()
                sqb = sbm("sqb", [128, 512], BF16); tk_sqb = Tk()
                rs = sbm("rs", [128, 512], F32); tk_rs = Tk()
                ocr = Ring([(sbm("oc%d" % i, [128, 512], BF16), Tk()) for i in range(2)])
                csc_l = sbm("cscl", [128, 1], F32); tk_cs = Tk()
                fw.op("dve", lambda e: e.tensor_scalar(out=csc_l[:], in0=cst[:, l, :], scalar1=(1.0 - lambda_init(l)), scalar2=None, op0=ALU.mult),
                      reads=[tk_par], writes=[tk_cs])
                seqs = [(s * PSEQ, PSEQ, False, [0]) for s in range(NPS)]
                seqs.append((cfg.TP, SL, True, list(range(1, NT))))
                for (tok0, Lq, has_ctx, tl) in seqs:
                    nctx = PAST if has_ctx else 0
                    nk = nctx + Lq
                    nkc = nk // 128
                    QB = min(512, Lq)
                    rd_v = [tk_v[t] for t in tl]
                    rd_q = [tk_q[t] for t in tl]
                    rd_k = [tk_k[t] for t in tl]
                    if has_ctx:
                        for kc in range(nctx // 128):
                            fw.dma("pool", Vt[:, kc, :], cache_v[l, kc], writes=[tk_V])
                    for kc in range(Lq // 128):
                        fw.dma("sp", Vt[:, nctx // 128 + kc, :], vS[tok0 // 128 + kc], reads=rd_v, writes=[tk_V])
                    for hh in range(4):
                        if has_ctx:
                            fw.dma("pool", KT[:, 0:nctx], cache_kT[l, hh], writes=[tk_K])
                        fw.dma("sp", KT[:, nctx:nk], kS[hh, :, tok0:tok0 + Lq], reads=rd_k, writes=[tk_K])
                        fw.dma("sp", QT[:, 0:Lq], qS[hh, :, tok0:tok0 + Lq], reads=rd_q, writes=[tk_Q])
                        for qb in range(Lq // QB):
                            qsl = slice(qb * QB, (qb + 1) * QB)
                            Op, tk_O = ps_bufs[0]
                            Sp, tk_S = ps_bufs[1]
                            def emit_S(kc):
                                ksl = slice(kc * 128, (kc + 1) * 128)
                                stp, tk_st = ps_bufs[2 + (kc % 2)]
                                fw.pe_group([(lambda e, m=m: e.matmul(stp[:, m * 512:m * 512 + QB], lhsT=KT[64 * m:64 * m + 64, ksl],
                                                                      rhs=QT[64 * m:64 * m + 64, qsl], start=True, stop=True)) for m in range(2)],
                                            reads=[tk_K, tk_Q], writes=[tk_st])

                            emit_S(0)
                            for kc in range(nkc):
                                stp, tk_st = ps_bufs[2 + (kc % 2)]
                                pT, tk_pT = pring.next()
                                stv = stp[:].rearrange("p (m q) -> p m q", m=2)[:, :, 0:QB]
                                fw.op("act", lambda e, pT=pT, stv=stv: e.activation(out=pT[:, :, 0:QB], in_=stv, func=AF.Exp, scale=0.125),
                                      reads=[tk_st], writes=[tk_pT])
                                if kc + 1 < nkc:
                                    emit_S(kc + 1)
                                fns = []
                                for m in range(2):
                                    fns.append(lambda e, m=m, pT=pT, kc=kc: e.matmul(Op[:, m * 512:m * 512 + QB], lhsT=Vt[:, kc, hh * 128:(hh + 1) * 128],
                                                                                     rhs=pT[:, m, 0:QB], start=(kc == 0), stop=(kc == nkc - 1)))
                                    fns.append(lambda e, m=m, pT=pT, kc=kc: e.matmul(Sp[:, m * 512:m * 512 + QB], lhsT=ones[:], rhs=pT[:, m, 0:QB],
                                                                                     start=(kc == 0), stop=(kc == nkc - 1)))
                                fw.pe_group(fns, reads=[tk_V, tk_pT, tk_c], writes=[tk_O, tk_S])
                            Ov = Op[:].rearrange("p (m q) -> p m q", m=2)[:, :, 0:QB]
                            Sv = Sp[:].rearrange("p (m q) -> p m q", m=2)[:, :, 0:QB]
                            fw.op("dve", lambda e: e.reciprocal(out=rr[:, :, 0:QB], in_=Sv), reads=[tk_S], writes=[tk_rr])
                            fw.op("dve", lambda e: e.tensor_tensor(out=oo[:, :, 0:QB], in0=Ov, in1=rr[:, :, 0:QB], op=ALU.mult), reads=[tk_O, tk_rr], writes=[tk_oo])
                            fw.op("dve", lambda e: e.scalar_tensor_tensor(out=od[:, 0:QB], in0=oo[:, 1, 0:QB], scalar=nlam[:, l:l + 1], in1=oo[:, 0, 0:QB],
                                                                          op0=ALU.mult, op1=ALU.add), reads=[tk_oo, tk_mod], writes=[tk_od])
                            fw.op("act", lambda e: e.activation(out=sqb[:, 0:QB], in_=od[:, 0:QB], func=AF.Square), reads=[tk_od], writes=[tk_sqb])
                            p2, tk_p2 = ps_bufs[2]
                            fw.pe_group([lambda e: e.matmul(p2[:, 0:QB], lhsT=ones[:], rhs=sqb[:, 0:QB], start=True, stop=True)],
                                        reads=[tk_sqb, tk_c], writes=[tk_p2])
                            fw.op("act", lambda e: e.activation(out=rs[:, 0:QB], in_=p2[:, 0:QB], func=AF.Sqrt, bias=EPS, scale=1.0 / 128), reads=[tk_p2], writes=[tk_rs])
                            fw.op("dve", lambda e: e.reciprocal(out=rs[:, 0:QB], in_=rs[:, 0:QB]), reads=[tk_rs], writes=[tk_rs])
                            oc, tk_oc = ocr.next()
                            fw.op("dve", lambda e: e.scalar_tensor_tensor(out=oc[:, 0:QB], in0=od[:, 0:QB], scalar=csc_l[:, 0:1], in1=rs[:, 0:QB],
                                                                          op0=ALU.mult, op1=ALU.mult), reads=[tk_od, tk_rs, tk_cs], writes=[tk_oc])
                            g0 = tok0 + qb * QB
                            tt = g0 // TT
                            fw.dma("sp", catC[hh, :, g0:g0 + QB], oc[:, 0:QB], reads=[tk_oc], writes=[tk_catC[tt]])

        def fnet(l):
            with ExitStack() as es:
                def sbm(name, shape, dt):
                    return es.enter_context(nc.sbuf_tensor(uname(name), list(shape), dt))
                nqc_max = max(cfg.NQC, PSEQ // 128)
                YZ = sbm("YZ", [128, 4, nqc_max, 256], BF16); tk_YZ = Tk()
                dxr = Ring([(sbm("dxr%d" % i, [128, 512], BF16), Tk()) for i in range(3)])
                slab = Ring([(sbm("slab%d" % i, [128, 2, cfg.NQC, cfg.FPB], BF16), Tk()) for i in range(2)])
                fo = Ring([(sbm("fo%d" % i, [128, 512], BF16), Tk()) for i in range(3)])
                seqs = [(s * PSEQ, PSEQ, False, [0]) for s in range(NPS)]
                seqs.append((cfg.TP, SL, True, list(range(1, NT))))
                for (tok0, Ls, is_s, tl) in seqs:
                    nqc = Ls // 128
                    rd = [tk_dx[t] for t in tl]
                    for g in range(4):
                        for q4 in range(0, nqc, 4):
                            nq = min(4, nqc - q4)
                            dx, tk_dxb = dxr.next()
                            fw.dma("sp", dx[:, 0:nq * 128], dxS[g, :, tok0 + q4 * 128: tok0 + (q4 + nq) * 128], reads=rd, writes=[tk_dxb])
                            pp, tk_pp = PS.next()
                            for i in range(nq):
                                fw.pe_group([lambda e, i=i: e.matmul(pp[:, i * 256:(i + 1) * 256], lhsT=dx[:, i * 128:(i + 1) * 128], rhs=csc[:],
                                                                     start=True, stop=True)], reads=[tk_dxb, tk_c], writes=[tk_pp])
                            fw.op("act", lambda e: e.activation(out=YZ[:, g, q4:q4 + nq, :], in_=pp[:, 0:nq * 256].rearrange("p (a b) -> p a b", b=256),
                                                                func=AF.Copy), reads=[tk_pp], writes=[tk_YZ])
                    PB = cfg.FPB if is_s else PSEQ
                    for pb in range(Ls // PB):
                        if is_s:
                            sl, tk_sl = slab.next()
                            fw.dma("sp", sl[:], dftS[pb], writes=[tk_sl])
                            Cm = lambda qc: sl[:, 0, qc, :]
                            Sm = lambda qc: sl[:, 1, qc, :]
                            rdm = [tk_sl]
                        else:
                            Cm = lambda qc: dftp[:, 0, qc, :]
                            Sm = lambda qc: dftp[:, 1, qc, :]
                            rdm = [tk_c]
                        for g2 in range(0, 4, 2):
                            pp, tk_pp = PS.next()
                            for gi in range(2):
                                g = g2 + gi
                                fns = []
                                for qc in range(nqc):
                                    fns.append(lambda e, qc=qc, g=g, gi=gi: e.matmul(pp[:, gi * 512:gi * 512 + PB], lhsT=YZ[:, g, qc, 0:128], rhs=Cm(qc),
                                                                                      start=(qc == 0), stop=False))
                                    fns.append(lambda e, qc=qc, g=g, gi=gi: e.matmul(pp[:, gi * 512:gi * 512 + PB], lhsT=YZ[:, g, qc, 128:256], rhs=Sm(qc),
                                                                                      start=False, stop=(qc == nqc - 1)))
                                fw.pe_group(fns, reads=[tk_YZ] + rdm, writes=[tk_pp])
                            ob, tk_ob = fo.next()
                            obv = ob[:].rearrange("p (m q) -> p m q", m=2)[:, :, 0:PB]
                            ppv = pp[:].rearrange("p (m q) -> p m q", m=2)[:, :, 0:PB]
                            fw.op("act", lambda e: e.activation(out=obv, in_=ppv, func=AF.Copy), reads=[tk_pp], writes=[tk_ob])
                            g0 = tok0 + pb * PB
                            tt = g0 // TT
                            for gi in range(2):
                                fw.dma("sp", fS[g2 + gi, :, g0:g0 + PB], ob[:, gi * 256:gi * 256 + PB], reads=[tk_ob], writes=[tk_f[tt]])

        class Region:
            def __init__(self):
                self.tks = []
                self.carry = {}

            def new_gen(self):
                m = self.carry
                for t in self.tks:
                    if t.w is not None and m.get(t.w[0], 0) < t.w[1]:
                        m[t.w[0]] = t.w[1]
                    for (k, v) in t.rs:
                        if m.get(k, 0) < v:
                            m[k] = v
                self.tks = []

            def tk(self, name=""):
                t = Tk(name)
                t.rs = list(self.carry.items())
                self.tks.append(t)
                return t

        def common_env(sbd):
            env = {}
            env["hT"] = (sbd("hT", [128, NCH, TT], BF16), Tk())
            env["xring"] = Ring([(sbd("xr%d" % i, [128, TT], F32), Tk()) for i in range(2)])
            env["tring"] = Ring([(sbd("tr%d" % i, [128, TT], F32), Tk()) for i in range(3)])
            env["sqring"] = Ring([(sbd("sq%d" % i, [128, TT], BF16), Tk()) for i in range(2)])
            env["rstd"] = (sbd("rstd", [128, TT], F32), Tk())
            env["wd"] = Ring([(sbd("wd%d" % i, [128, NFC, 128], BF16), Tk()) for i in range(2)])
            return env

        def ffn_env(env, sbe, reg):
            reg.new_gen()
            env["mid"] = (sbe("mid", [128, NFC, TT], BF16), reg.tk())
            env["wu"] = Ring([(sbe("wu%d" % i, [128, 2, NCH, 128], BF16), reg.tk()) for i in range(2)])
            env["sgring"] = Ring([(sbe("sg%d" % i, [128, TT], BF16), reg.tk()) for i in range(2)])

        def inproj_env(env, sbe, reg):
            reg.new_gen()
            env["uT"] = (sbe("uT", [128, 4, TT], BF16), reg.tk())
            env["stage"] = (sbe("stage", [128, 4, TT], BF16), reg.tk())
            env["wtok"] = Ring([(sbe("wtok", [128, NCH, 512], BF16), reg.tk())])
            env["ang"] = (sbe("ang", [128, 512], F32), reg.tk())
            env["wsT"] = (sbe("wsT", [128, 4, 128], BF16), reg.tk())
            env["bsr"] = (sbe("bsr", [128, 512], BF16), reg.tk())
            env["gv"] = Ring([(sbe("gv%d" % i, [128, 512], F32), reg.tk()) for i in range(2)])
            env["vn"] = Ring([(sbe("vn%d" % i, [128, 512], BF16), reg.tk()) for i in range(2)])
            env["st2"] = Ring([(sbe("st2%d" % i, [128, 2], F32), reg.tk()) for i in range(4)])
            env["junk"] = (sbe("junk", [128, 512], BF16), reg.tk())
            env["rope"] = (sbe("rope", [128, 2, TT], F32), reg.tk())
            env["obring"] = Ring([(sbe("ob%d" % i, [128, TT], BF16), reg.tk()) for i in range(2)])
            env["wi"] = Ring([(sbe("wi%d" % i, [128, NCH, 128], BF16), reg.tk()) for i in range(2)])
            env["sgring"] = Ring([(sbe("sg%d" % i, [128, TT], BF16), reg.tk()) for i in range(2)])

        def outmix_env(env, sbe, reg):
            reg.new_gen()
            env["stage"] = (sbe("stage", [128, 4, TT], BF16), reg.tk())
            env["G"] = (sbe("G", [128, TT + 8 * HALO], BF16), reg.tk())
            env["acc"] = (sbe("acc", [128, 4, TT], F32), reg.tk())
            env["obring"] = Ring([(sbe("ob%d" % i, [128, TT], BF16), reg.tk()) for i in range(2)])
            env["wsm"] = Ring([(sbe("wsm%d" % i, [128, 4, 128], BF16), reg.tk()) for i in range(2)])
            env["lnmean"] = (sbe("lnmean", [128, TT], F32), reg.tk())
            env["lnrstd"] = (sbe("lnrstd", [128, TT], F32), reg.tk())

        def sub(es):
            return lambda name, shape, dt: es.enter_context(nc.sbuf_tensor(uname(name), list(shape), dt))

        def xc_env(env, sbe, reg):
            reg.new_gen()
            env["xbC"] = (sbe("xbC", [128, NCH, TT], BF16), reg.tk())

        NSTEP = 3 * L

        def bufs_of(k):
            src = "in" if k == 0 else (k - 1) % 3
            dst = "out" if k == NSTEP - 1 else k % 3
            return src, dst

        def in_tile(env, reg, es, t, l, ssq_pair):
            src, dst = bufs_of(3 * l)
            if ssq_pair is None:
                norm_mod(env, src, t, l, 0)
            else:
                norm_sb(env, env["hT"][0], env["hT"][1], ssq_pair, t, l, 0)
            with ExitStack() as es2:
                ffn_env(env, sub(es2), reg)
                sp2 = ffn(env, src, dst, t, l, 0)
            norm_sb(env, env["hT"][0], env["hT"][1], sp2, t, l, 1)
            with ExitStack() as es2:
                inproj_env(env, sub(es2), reg)
                in_proj(env, t, l)

        with ExitStack() as es:
            env = common_env(sub(es))
            reg = Region()
            for t in range(NT):
                in_tile(env, reg, es, t, 0, None)
                chk("ip")
        fw.barrier()
        chk("in")
        for l in range(L):
            attention(l)
            fw.barrier()
            chk("att")
            fnet(l)
            fw.barrier()
            chk("fn")
            with ExitStack() as es:
                env = common_env(sub(es))
                reg = Region()
                for t in range(NT):
                    with ExitStack() as es2:
                        outmix_env(env, sub(es2), reg)
                        out_mix(env, t, l)
                    chk("om")
                    s1_, d1_ = bufs_of(3 * l + 1)
                    with ExitStack() as es2:
                        xc_env(env, sub(es2), reg)
                        xbC, tk_xbC = env["xbC"]
                        sp1 = residual_proj(env, s1_, d1_, t, l, 1, w_out, env["hT"][0], env["hT"][1], NCH, env["wd"], xb=xbC, tk_xb=tk_xbC)
                        chk("wo")
                        norm_sb(env, xbC, tk_xbC, sp1, t, l, 2)
                    s2_, d2_ = bufs_of(3 * l + 2)
                    with ExitStack() as es2:
                        ffn_env(env, sub(es2), reg)
                        sp2 = ffn(env, s2_, d2_, t, l, 1)
                    if l + 1 < L:
                        in_tile(env, reg, es, t, l + 1, sp2)
            fw.barrier()
      fw.drain_all()
    nc._fw_stats = (fw.nins, fw.nwait)
    return nc


def _bf16(a):
    return np.asarray(a, dtype=np.float32).astype(ml_dtypes.bfloat16)


def make_consts(cfg):
    SL = cfg.SL
    c = {}
    t = np.arange(SL)
    row = (t // GRID_W).astype(np.float32)
    col = (t % GRID_W).astype(np.float32)
    freqs = (np.float32(10000.0) ** (-np.arange(16, dtype=np.float32) / np.float32(16))).astype(np.float32)
    rope = np.zeros((128, 2, SL), np.float32)
    for p in range(128):
        d = p % 64
        pos = row if d < 32 else col
        ang = (pos * freqs[d % 16]).astype(np.float32)
        rope[p, 0] = np.cos(ang)
        sgn = -1.0 if (d % 32) < 16 else 1.0
        rope[p, 1] = sgn * np.sin(ang)
    c["ropeT"] = rope
    perm = np.zeros((128, 128), np.float32)
    for m in range(128):
        k = m + 16 if (m % 32) < 16 else m - 16
        perm[k, m] = 1.0
    c["permM"] = _bf16(perm)
    c["onesM"] = _bf16(np.ones((128, 128), np.float32))
    blk = np.zeros((128, 128), np.float32)
    blk[:64, :64] = 1.0
    blk[64:, 64:] = 1.0
    c["blkM"] = _bf16(blk)
    i = np.arange(128)
    a = 2.0 * np.pi * ((i[:, None] * i[None, :]) % 128) / 128.0
    c["csC"] = _bf16(np.concatenate([np.cos(a), np.sin(a)], axis=1) / np.sqrt(128.0))

    def pos_dft(n):
        q = np.arange(n, dtype=np.int64)
        a = 2.0 * np.pi * ((q[:, None] * q[None, :]) % n).astype(np.float64) / n
        return (np.cos(a) / np.sqrt(n)).astype(np.float32), (-np.sin(a) / np.sqrt(n)).astype(np.float32)

    C, S = pos_dft(PSEQ)
    cs = np.stack([C, S], 0).reshape(2, PSEQ // 128, 128, PSEQ)
    c["dftP"] = _bf16(cs.transpose(2, 0, 1, 3))
    C, S = pos_dft(SL)
    cs = np.stack([C, S], 0).reshape(2, cfg.NQC, 128, cfg.NPB, cfg.FPB)
    c["dftS"] = _bf16(np.ascontiguousarray(cs.transpose(3, 2, 0, 1, 4)))
    return c


def layout_weights(inp, cfg):
    L = cfg.L
    f = lambda k: np.asarray(inp[k], dtype=np.float32)
    w = {}
    w["w_mod"] = np.ascontiguousarray(f("w_mod").reshape(L, 16, 128, 36, 4, 128).transpose(0, 3, 2, 4, 1, 5))
    w["b_modT"] = np.ascontiguousarray(f("b_mod").reshape(L, 144, 128).transpose(0, 2, 1))
    w["norm_gT"] = np.ascontiguousarray(f("norm_g").reshape(L, 3, 16, 128).transpose(0, 3, 1, 2))
    for i, (a, b) in enumerate((("w_ff1_in", "w_ff1_down"), ("w_ff2_in", "w_ff2_down"))):
        w["w_up%d" % i] = np.ascontiguousarray(f(a).reshape(L, 16, 128, 2, NFC, 128).transpose(0, 4, 2, 3, 1, 5))
        w["w_dn%d" % i] = np.ascontiguousarray(f(b).reshape(L, NFC, 128, 16, 128).transpose(0, 3, 2, 1, 4))
    win = f("w_in").reshape(L, 16, 128, 32, 128)
    sel = [0, 1, 2, 3, 8, 9, 10, 11, 12, 13, 14, 15, 16, 17, 18, 19, 20, 21, 22, 23, 28, 29, 30, 31]
    w["w_inf"] = np.ascontiguousarray(win[:, :, :, sel, :].transpose(0, 3, 2, 1, 4))
    win2 = f("w_in").reshape(L, 16, 128, 8, 512)
    w["w_int"] = np.ascontiguousarray(win2[:, :, :, [1, 6], :].transpose(0, 3, 2, 1, 4))
    w["w_out"] = np.ascontiguousarray(f("w_out").reshape(L, 16, 128, 16, 128).transpose(0, 3, 2, 1, 4))
    w["b_pw"] = np.ascontiguousarray(f("b_pw").reshape(L, 4, 128, 4, 128).transpose(0, 3, 2, 1, 4))
    w["d_lin"] = np.ascontiguousarray(f("d_lin").reshape(L, 4, 128, 4, 128).transpose(0, 3, 2, 1, 4))
    w["a_ng"] = np.ascontiguousarray(np.broadcast_to(f("a_norm_g")[:, None, :], (L, 128, 512)))
    w["a_wsT"] = np.ascontiguousarray(f("a_ws").transpose(0, 3, 1, 2))
    abs_pad = np.zeros((L, 128, 512), np.float32)
    abs_pad[:, 0, :] = f("a_bs").reshape(L, 512)
    w["a_bs"] = abs_pad
    w["b_cw"] = np.ascontiguousarray(f("b_conv_w").reshape(L, CONV_W, 4, 128).transpose(0, 3, 2, 1))
    bv = np.stack([f("b_conv_b"), f("b_ln_g"), f("b_ln_b")], axis=1).reshape(L, 3, 4, 128)
    w["b_vec"] = np.ascontiguousarray(bv.transpose(0, 3, 1, 2))
    qg = np.concatenate([f("c_qnorm_g"), f("c_qnorm_g")], axis=1)
    kg = np.concatenate([f("c_knorm_g"), f("c_knorm_g")], axis=1)
    w["c_g"] = np.ascontiguousarray(np.stack([qg, kg], axis=2))
    w["c_lam"] = np.ascontiguousarray(np.broadcast_to(f("c_lambda").reshape(L, 1, 256), (L, 128, 256)))
    w["c_sub"] = np.ascontiguousarray(f("c_subln_g").reshape(L, 128, 1))
    return w


def per_core_inputs(inp, cfg, core, shared):
    L, SL = cfg.L, cfg.SL
    s = core % inp["x_sample"].shape[0]
    xp = np.asarray(inp["x_prompt"], np.float32)[NPS * core:NPS * (core + 1)].reshape(cfg.TP, D)
    xs = np.asarray(inp["x_sample"], np.float32)[s]
    x = np.concatenate([xp, xs], axis=0)
    m = dict(shared)
    m["xT_in"] = np.ascontiguousarray(x.T).reshape(NCH, 128, cfg.T)
    cond = np.stack([np.asarray(inp["c_ctx"], np.float32), np.asarray(inp["c"], np.float32)[s]], axis=1)
    m["condT"] = np.ascontiguousarray(cond.reshape(NCH, 128, 2).transpose(1, 0, 2))
    ck = np.asarray(inp["cache_k"], np.float32)[s]
    m["cache_kT"] = np.ascontiguousarray(ck.transpose(0, 2, 3, 1))
    cv = np.asarray(inp["cache_v"], np.float32)[s]
    m["cache_v"] = np.ascontiguousarray(cv.reshape(L, cfg.PAST // 128, 128, 512))
    return m


def run(inp, cfg, n_cores):
    nc = build_program(cfg)
    shared = layout_weights(inp, cfg)
    shared.update(make_consts(cfg))
    in_maps = [per_core_inputs(inp, cfg, c, shared) for c in range(n_cores)]
    res = run_bass_kernel_spmd(nc, in_maps, core_ids=list(range(n_cores)))
    return res.results


def assemble(results, cfg, n_cores, n_samples):
    L = cfg.L
    yp, ys, nk, nv = [], [None] * n_samples, [], []
    for c in range(n_cores):
        r = results[c]
        y = np.asarray(r["yT"]).reshape(D, cfg.T).T
        yp.append(y[:cfg.TP].reshape(NPS, PSEQ, D))
        if c < n_samples:
            ys[c] = y[cfg.TP:]
        k = np.asarray(r["newkT"]).transpose(3, 0, 1, 2).reshape(NPS, PSEQ, L, 4, 128).transpose(0, 2, 1, 3, 4)
        nk.append(k)
        v = np.asarray(r["newv"]).reshape(L, NPS, PSEQ, 4, 128).transpose(1, 0, 2, 3, 4)
        nv.append(v)
    return (np.ascontiguousarray(np.concatenate(yp, 0), dtype=np.float32),
            np.ascontiguousarray(np.stack(ys, 0), dtype=np.float32),
            np.ascontiguousarray(np.concatenate(nk, 0), dtype=np.float32),
            np.ascontiguousarray(np.concatenate(nv, 0), dtype=np.float32))


def kernel(**inputs):
    cfg = Cfg(L=4, SL=4096, PAST=512)
    results = run(inputs, cfg, 8)
    return assemble(results, cfg, 8, 4)
```
